# Optimizing a Trainium2 kernel written in Bass

```python
import math
import jax
import jax.numpy as jnp
from jax import lax
import numpy as np

D_MODEL = 1024
BATCH = 8
SEQ = 2048
DEPTH = 1
DEC_BATCH = 128
DEC_SEQ = 4
PAST_LEN = 2048
PAGE_SIZE = 128

D_MIX = D_MODEL
NSA_HEAD_DIM = 64
NSA_WIDTH = D_MIX // 2
NSA_Q_HEADS = NSA_WIDTH // NSA_HEAD_DIM
NSA_KV_HEADS = 2
NSA_GROUP = NSA_Q_HEADS // NSA_KV_HEADS
NSA_KV_WIDTH = NSA_KV_HEADS * NSA_HEAD_DIM
CMP_LEN = 32
CMP_STRIDE = 16
CMP_HIDDEN = 128
SEL_BLOCK = 64
TOP_N = 8
WINDOW = 512
SEL_QBLOCK = 32
WIN_QBLOCK = 128
FORCE_BONUS = 1.0e4
ROT_DIM = NSA_HEAD_DIM // 4
ROPE_THETA = 500000.0
GDN_DK = 128
GDN_DV = 128
GDN_WIDTH = D_MIX - NSA_WIDTH
GDN_HEADS = GDN_WIDTH // GDN_DV
GDN_CONV = 4
GDN_CONV_CH = 2 * GDN_HEADS * GDN_DK + GDN_HEADS * GDN_DV
GDN_CHUNK = 64
IN_SIZES = (NSA_WIDTH, NSA_KV_WIDTH, NSA_KV_WIDTH, NSA_KV_WIDTH, NSA_KV_WIDTH, NSA_KV_WIDTH, NSA_KV_WIDTH, 3 * NSA_Q_HEADS, NSA_WIDTH, GDN_CONV_CH, GDN_HEADS, GDN_HEADS, GDN_WIDTH)
N_IN = 2 * NSA_WIDTH + 6 * NSA_KV_WIDTH + 3 * NSA_Q_HEADS + GDN_CONV_CH + 2 * GDN_HEADS + GDN_WIDTH
NORM_EPS = 1e-6
MASK_VALUE = -1e30

kernel_name = 'hymba_nsa_gated_deltanet_step'


def rmsnorm(x, w):
    xf = x.astype(jnp.float32)
    y = xf * lax.rsqrt(jnp.mean(xf * xf, axis=-1, keepdims=True) + NORM_EPS)
    return (y * w.astype(jnp.float32)).astype(x.dtype)


def l2norm(x):
    xf = x.astype(jnp.float32)
    return xf * lax.rsqrt(jnp.sum(xf * xf, axis=-1, keepdims=True) + NORM_EPS)


def rope_partial(x, pos):
    half = ROT_DIM // 2
    inv = ROPE_THETA ** (-(jnp.arange(half, dtype=jnp.float32) * 2.0 / ROT_DIM))
    ang = pos.astype(jnp.float32)[:, None] * inv[None, :]
    bshape = (pos.shape[0],) + (1,) * (x.ndim - 3) + (half,)
    cos = jnp.cos(ang).reshape(bshape)
    sin = jnp.sin(ang).reshape(bshape)
    xr = x[..., :ROT_DIM].astype(jnp.float32)
    x1, x2 = xr[..., :half], xr[..., half:]
    rot = jnp.concatenate([x1 * cos - x2 * sin, x2 * cos + x1 * sin], axis=-1).astype(x.dtype)
    return jnp.concatenate([rot, x[..., ROT_DIM:]], axis=-1)


def pad_rows(r):
    L = r.shape[1]
    lp = -(-L // SEL_BLOCK) * SEL_BLOCK
    return jnp.pad(r, ((0, 0), (0, lp - L), (0, 0), (0, 0)))


def compress_rows(rows, pe, w1, w2):
    B, Lk, H, Dh = rows.shape
    r = CMP_LEN // CMP_STRIDE
    nsub = Lk // CMP_STRIDE
    nc = nsub - r + 1
    sub = rows.reshape(B, nsub, CMP_STRIDE, H, Dh)
    blk = jnp.concatenate([sub[:, j:j + nc] for j in range(r)], axis=2)
    blk = blk + pe[:, None, :].astype(rows.dtype)
    u = blk.transpose(0, 1, 3, 2, 4).reshape(B, nc, H, CMP_LEN * Dh)
    return jax.nn.silu(u @ w1) @ w2


def nsa_cmp_slc(q_rot, q_raw, ck_rows, cv_rows, sk_rows, sv_rows, q_pos, pe_k, wk1, wk2, pe_v, wv1, wv2):
    B, Lq = q_rot.shape[:2]
    Lk = ck_rows.shape[1]
    scale = NSA_HEAD_DIM ** -0.5
    ck = compress_rows(ck_rows, pe_k, wk1, wk2)
    cv = compress_rows(cv_rows, pe_v, wv1, wv2)
    nc = ck.shape[1]
    c_start = jnp.arange(nc, dtype=jnp.int32) * CMP_STRIDE
    cmask = (c_start + CMP_LEN - 1)[None, :] <= q_pos[:, None]
    s = jnp.einsum('bqhgd,bchd->bhgqc', q_raw, ck).astype(jnp.float32) * scale
    p = jax.nn.softmax(jnp.where(cmask, s, MASK_VALUE), axis=-1) * cmask
    o_cmp = jnp.einsum('bhgqc,bchd->bqhgd', p.astype(cv.dtype), cv)
    ns = Lk // SEL_BLOCK
    b_start = jnp.arange(ns, dtype=jnp.int32) * SEL_BLOCK
    ov = jnp.minimum(c_start[:, None] + CMP_LEN, b_start[None, :] + SEL_BLOCK) - jnp.maximum(c_start[:, None], b_start[None, :])
    overlap = jnp.maximum(ov, 0).astype(jnp.float32) / CMP_LEN
    imp = jnp.einsum('bhgqc,cn->bhqn', p, overlap)
    q_blk = q_pos // SEL_BLOCK
    blk = jnp.arange(ns, dtype=jnp.int32)[None, :]
    forced = (blk == 0) | (blk == q_blk[:, None]) | (blk == q_blk[:, None] - 1)
    imp = jnp.where(blk <= q_blk[:, None], imp + FORCE_BONUS * forced, MASK_VALUE)
    n_sel = min(TOP_N, ns)
    _, idx = lax.top_k(imp, n_sel)
    kb = sk_rows.reshape(B, ns, SEL_BLOCK, NSA_KV_HEADS, NSA_HEAD_DIM).transpose(0, 3, 1, 2, 4)
    vb = sv_rows.reshape(B, ns, SEL_BLOCK, NSA_KV_HEADS, NSA_HEAD_DIM).transpose(0, 3, 1, 2, 4)
    bi = jnp.arange(B)[:, None, None, None]
    hi = jnp.arange(NSA_KV_HEADS)[None, :, None, None]
    offs = jnp.arange(SEL_BLOCK, dtype=jnp.int32)

    def one_block(args):
        q_b, idx_b, pos_b = args
        kg = kb[bi, hi, idx_b]
        vg = vb[bi, hi, idx_b]
        sc = jnp.einsum('bqhgd,bhqnsd->bhgqns', q_b, kg).astype(jnp.float32) * scale
        kpos = idx_b[..., None] * SEL_BLOCK + offs
        m = kpos[:, :, None] <= pos_b[:, None, None]
        sc = jnp.where(m, sc, MASK_VALUE)
        shp = sc.shape
        pr = jax.nn.softmax(sc.reshape(shp[:4] + (-1,)), axis=-1).reshape(shp)
        return jnp.einsum('bhgqns,bhqnsd->bqhgd', pr.astype(vg.dtype), vg)

    qb = SEL_QBLOCK if Lq % SEL_QBLOCK == 0 else Lq
    nq = Lq // qb
    q_blocks = q_rot.reshape(B, nq, qb, NSA_KV_HEADS, NSA_GROUP, NSA_HEAD_DIM).swapaxes(0, 1)
    idx_blocks = idx.reshape(B, NSA_KV_HEADS, nq, qb, n_sel).transpose(2, 0, 1, 3, 4)
    pos_blocks = q_pos.reshape(nq, qb)
    o = lax.map(one_block, (q_blocks, idx_blocks, pos_blocks))
    o_slc = o.swapaxes(0, 1).reshape(B, Lq, NSA_KV_HEADS, NSA_GROUP, NSA_HEAD_DIM)
    return o_cmp, o_slc


def window_banded(q, k, v):
    B, T = q.shape[:2]
    qb = WIN_QBLOCK if T % WIN_QBLOCK == 0 else T
    nb = T // qb
    nprev = -(-WINDOW // qb)
    pad = nprev * qb
    span = (nprev + 1) * qb
    sel = jnp.arange(nb)[:, None] + jnp.arange(nprev + 1)[None, :]

    def band(t):
        tp = jnp.pad(t, ((0, 0), (pad, 0), (0, 0), (0, 0))).reshape(B, nb + nprev, qb, NSA_KV_HEADS, NSA_HEAD_DIM)
        return tp[:, sel].reshape(B, nb, span, NSA_KV_HEADS, NSA_HEAD_DIM)

    kw, vw = band(k), band(v)
    q_b = q.reshape(B, nb, qb, NSA_KV_HEADS, NSA_GROUP, NSA_HEAD_DIM)
    s = jnp.einsum('bnqhgd,bnkhd->bhgnqk', q_b, kw).astype(jnp.float32) * (NSA_HEAD_DIM ** -0.5)
    q_pos = jnp.arange(T, dtype=jnp.int32).reshape(nb, qb)
    k_pos = jnp.arange(nb, dtype=jnp.int32)[:, None] * qb - pad + jnp.arange(span, dtype=jnp.int32)[None, :]
    diff = q_pos[:, :, None] - k_pos[:, None, :]
    m = (diff >= 0) & (diff <= WINDOW) & (k_pos[:, None, :] >= 0)
    p = jax.nn.softmax(jnp.where(m, s, MASK_VALUE), axis=-1)
    o = jnp.einsum('bhgnqk,bnkhd->bnqhgd', p.astype(vw.dtype), vw)
    return o.reshape(B, T, NSA_KV_HEADS, NSA_GROUP, NSA_HEAD_DIM)


def window_dense(q, k, v, q_pos, k_pos):
    s = jnp.einsum('bqhgd,bkhd->bhgqk', q, k).astype(jnp.float32) * (NSA_HEAD_DIM ** -0.5)
    diff = q_pos[:, None] - k_pos[None, :]
    m = (diff >= 0) & (diff <= WINDOW)
    p = jax.nn.softmax(jnp.where(m, s, MASK_VALUE), axis=-1)
    return jnp.einsum('bhgqk,bkhd->bqhgd', p.astype(v.dtype), v)


def nsa_merge(gate, o_cmp, o_slc, o_win, z):
    B, L = gate.shape[:2]
    g = jax.nn.sigmoid(gate.astype(jnp.float32)).reshape(B, L, 3, NSA_KV_HEADS, NSA_GROUP, 1).astype(o_cmp.dtype)
    o = g[:, :, 0] * o_cmp + g[:, :, 1] * o_slc + g[:, :, 2] * o_win
    return o.reshape(B, L, NSA_WIDTH) * jax.nn.silu(z)


def short_conv(x, buf, w):
    L = x.shape[1]
    xp = jnp.concatenate([buf.astype(x.dtype), x], axis=1)
    y = xp[:, 0:L] * w[0]
    for j in range(1, GDN_CONV):
        y = y + xp[:, j:j + L] * w[j]
    return jax.nn.silu(y), xp[:, -(GDN_CONV - 1):]


def gated_delta_chunked(q, k, v, beta, g, s0):
    B, L, H, Dk = q.shape
    Dv = v.shape[-1]
    C = min(GDN_CHUNK, L)
    n = -(-L // C)
    pad = n * C - L
    f32 = jnp.float32

    def to_chunks(t):
        t = jnp.pad(t.astype(f32), ((0, 0), (0, pad)) + ((0, 0),) * (t.ndim - 2))
        t = t.reshape((B, n, C) + t.shape[2:])
        return jnp.moveaxis(t, 3, 1)

    q, k, v, beta, g = to_chunks(q), to_chunks(k), to_chunks(v), to_chunks(beta), to_chunks(g)
    decay = jnp.cumsum(g, axis=-1)
    tri = jnp.tril(jnp.ones((C, C), dtype=bool))
    tri_s = jnp.tril(jnp.ones((C, C), dtype=bool), -1)
    diff = decay[..., :, None] - decay[..., None, :]
    dmask = jnp.where(tri, jnp.exp(jnp.where(tri, diff, 0.0)), 0.0)
    kb = k * beta[..., None]
    a_mat = jnp.where(tri_s, jnp.einsum('bhncd,bhnsd->bhncs', kb, k) * dmask, 0.0)
    t_mat = a_mat + jnp.eye(C, dtype=f32)
    rhs = jnp.concatenate([v * beta[..., None], kb * jnp.exp(decay)[..., None]], axis=-1)
    sol = lax.linalg.triangular_solve(t_mat, rhs, left_side=True, lower=True, unit_diagonal=True)
    u, w = sol[..., :Dv], sol[..., Dv:]
    qk = jnp.einsum('bhncd,bhnsd->bhncs', q, k) * dmask
    qd = q * jnp.exp(decay)[..., None]
    kd = k * jnp.exp(decay[..., -1:] - decay)[..., None]
    gl = jnp.exp(decay[..., -1])
    xs = tuple(jnp.moveaxis(t, 2, 0) for t in (u, w, qk, qd, kd, gl))

    def step(S, xc):
        u_c, w_c, qk_c, qd_c, kd_c, gl_c = xc
        v_new = u_c - jnp.einsum('bhcd,bhde->bhce', w_c, S)
        o = jnp.einsum('bhcd,bhde->bhce', qd_c, S) + jnp.einsum('bhcs,bhse->bhce', qk_c, v_new)
        S = S * gl_c[..., None, None] + jnp.einsum('bhcd,bhce->bhde', kd_c, v_new)
        return S, o

    S, o = lax.scan(step, s0.astype(f32), xs)
    o = o.transpose(1, 0, 3, 2, 4).reshape(B, n * C, H, Dv)[:, :L]
    return o, S.astype(s0.dtype)


def gdn_mixer(qkv, b_l, a_l, z, conv_buf, s0, w_conv, a_log, dt_bias, w_gnorm):
    B, L, _ = qkv.shape
    act, conv_new = short_conv(qkv, conv_buf, w_conv)
    nq = GDN_HEADS * GDN_DK
    q = l2norm(act[..., :nq].reshape(B, L, GDN_HEADS, GDN_DK)) * (GDN_DK ** -0.5)
    k = l2norm(act[..., nq:2 * nq].reshape(B, L, GDN_HEADS, GDN_DK))
    v = act[..., 2 * nq:].reshape(B, L, GDN_HEADS, GDN_DV)
    beta = jax.nn.sigmoid(b_l.astype(jnp.float32))
    g = -jnp.exp(a_log.astype(jnp.float32)) * jax.nn.softplus(a_l.astype(jnp.float32) + dt_bias.astype(jnp.float32))
    o, s_new = gated_delta_chunked(q, k, v, beta, g, s0)
    o = rmsnorm(o.astype(qkv.dtype), w_gnorm) * jax.nn.silu(z.reshape(B, L, GDN_HEADS, GDN_DV))
    return o.reshape(B, L, GDN_WIDTH), conv_new, s_new


def project(x, pos, w_norm, w_in):
    B, L, _ = x.shape
    h = rmsnorm(x, w_norm) @ w_in
    cuts = np.cumsum(np.array(IN_SIZES))[:-1].tolist()
    q, ck, cv, sk, sv, wk, wv, gate, z_nsa, qkv, b_l, a_l, z_gdn = jnp.split(h, cuts, axis=-1)
    kvshape = (B, L, NSA_KV_HEADS, NSA_HEAD_DIM)
    q = q.reshape(B, L, NSA_KV_HEADS, NSA_GROUP, NSA_HEAD_DIM)
    ck, cv, sk, sv, wk, wv = [t.reshape(kvshape) for t in (ck, cv, sk, sv, wk, wv)]
    return (q, rope_partial(q, pos), ck, cv, rope_partial(sk, pos), sv, rope_partial(wk, pos), wv, gate, z_nsa, qkv, b_l, a_l, z_gdn)


def layer_prompt(x, lw):
    w_norm, w_in, pe_k, wk1, wk2, pe_v, wv1, wv2, w_conv, a_log, dt_bias, w_gnorm, w_out = lw
    B, T, _ = x.shape
    pos = jnp.arange(T, dtype=jnp.int32)
    q, q_r, ck, cv, sk_r, sv, wk_r, wv, gate, z_nsa, qkv, b_l, a_l, z_gdn = project(x, pos, w_norm, w_in)
    o_cmp, o_slc = nsa_cmp_slc(q_r, q, pad_rows(ck), pad_rows(cv), pad_rows(sk_r), pad_rows(sv), pos, pe_k, wk1, wk2, pe_v, wv1, wv2)
    o_win = window_banded(q_r, wk_r, wv)
    nsa_out = nsa_merge(gate, o_cmp, o_slc, o_win, z_nsa)
    conv0 = jnp.zeros((B, GDN_CONV - 1, GDN_CONV_CH), x.dtype)
    s0 = jnp.zeros((B, GDN_HEADS, GDN_DK, GDN_DV), x.dtype)
    gdn_out, conv_new, s_new = gdn_mixer(qkv, b_l, a_l, z_gdn, conv0, s0, w_conv, a_log, dt_bias, w_gnorm)
    y = x + jnp.concatenate([nsa_out, gdn_out], axis=-1) @ w_out
    wb = min(WINDOW, PAST_LEN)
    lead = ((0, 0), (max(wb - T, 0), 0), (0, 0), (0, 0))
    win_k = jnp.pad(wk_r, lead)[:, -wb:]
    win_v = jnp.pad(wv, lead)[:, -wb:]
    return y, (ck, cv, sk_r, sv, win_k, win_v, conv_new, s_new)


def layer_sample(x, c_cmp_k, c_cmp_v, c_slc_k, c_slc_v, c_win_k, c_win_v, s_conv, s_gdn, page_table, lw):
    w_norm, w_in, pe_k, wk1, wk2, pe_v, wv1, wv2, w_conv, a_log, dt_bias, w_gnorm, w_out = lw
    B, L, _ = x.shape
    pos = PAST_LEN + jnp.arange(L, dtype=jnp.int32)
    q, q_r, ck, cv, sk_r, sv, wk_r, wv, gate, z_nsa, qkv, b_l, a_l, z_gdn = project(x, pos, w_norm, w_in)

    def with_past(cache, new):
        past = cache[page_table].reshape(B, -1, NSA_KV_HEADS, NSA_HEAD_DIM)
        return pad_rows(jnp.concatenate([past.astype(new.dtype), new], axis=1))

    o_cmp, o_slc = nsa_cmp_slc(q_r, q, with_past(c_cmp_k, ck), with_past(c_cmp_v, cv), with_past(c_slc_k, sk_r), with_past(c_slc_v, sv), pos, pe_k, wk1, wk2, pe_v, wv1, wv2)
    wb = c_win_k.shape[1]
    keys = jnp.concatenate([c_win_k.astype(wk_r.dtype), wk_r], axis=1)
    vals = jnp.concatenate([c_win_v.astype(wv.dtype), wv], axis=1)
    k_pos = PAST_LEN - wb + jnp.arange(wb + L, dtype=jnp.int32)
    o_win = window_dense(q_r, keys, vals, pos, k_pos)
    nsa_out = nsa_merge(gate, o_cmp, o_slc, o_win, z_nsa)
    gdn_out, conv_new, s_new = gdn_mixer(qkv, b_l, a_l, z_gdn, s_conv, s_gdn, w_conv, a_log, dt_bias, w_gnorm)
    y = x + jnp.concatenate([nsa_out, gdn_out], axis=-1) @ w_out
    return y, (ck, cv, sk_r, sv, keys[:, -wb:], vals[:, -wb:], conv_new, s_new)


def setup_inputs(seed: int = 0) -> dict:
    key = jax.random.key(seed)
    ks = jax.random.split(key, 24)
    f32 = jnp.float32
    n_pages = PAST_LEN // PAGE_SIZE
    n_pool = (DEC_BATCH * n_pages * 5) // 4
    win_buf = min(WINDOW, PAST_LEN)

    def nrm(k, shape, scale):
        return jax.random.normal(k, shape, f32) * scale

    kv_pool = (DEPTH, n_pool, PAGE_SIZE, NSA_KV_HEADS, NSA_HEAD_DIM)
    win_shape = (DEPTH, DEC_BATCH, win_buf, NSA_KV_HEADS, NSA_HEAD_DIM)
    perm = jax.random.permutation(ks[10], n_pool)
    page_table = perm[:DEC_BATCH * n_pages].reshape(DEC_BATCH, n_pages).astype(jnp.int32)
    dt = jnp.exp(jax.random.uniform(ks[20], (DEPTH, GDN_HEADS), f32, math.log(1e-3), math.log(1e-1)))
    return {
        'x_prompt': nrm(ks[0], (BATCH, SEQ, D_MODEL), 1.0),
        'x_sample': nrm(ks[1], (DEC_BATCH, DEC_SEQ, D_MODEL), 1.0),
        'cache_cmp_k': nrm(ks[2], kv_pool, 1.0),
        'cache_cmp_v': nrm(ks[3], kv_pool, 1.0),
        'cache_slc_k': nrm(ks[4], kv_pool, 1.0),
        'cache_slc_v': nrm(ks[5], kv_pool, 1.0),
        'cache_win_k': nrm(ks[6], win_shape, 1.0),
        'cache_win_v': nrm(ks[7], win_shape, 1.0),
        'state_conv': nrm(ks[8], (DEPTH, DEC_BATCH, GDN_CONV - 1, GDN_CONV_CH), 1.0),
        'state_gdn': nrm(ks[9], (DEPTH, DEC_BATCH, GDN_HEADS, GDN_DK, GDN_DV), 0.5),
        'page_table': page_table,
        'w_norm': 1.0 + nrm(ks[11], (DEPTH, D_MODEL), 0.02),
        'w_in': nrm(ks[12], (DEPTH, D_MODEL, N_IN), D_MODEL ** -0.5),
        'pe_cmp_k': nrm(ks[13], (DEPTH, CMP_LEN, NSA_HEAD_DIM), 0.1),
        'w_cmp_k1': nrm(ks[14], (DEPTH, CMP_LEN * NSA_HEAD_DIM, CMP_HIDDEN), (CMP_LEN * NSA_HEAD_DIM) ** -0.5),
        'w_cmp_k2': nrm(ks[15], (DEPTH, CMP_HIDDEN, NSA_HEAD_DIM), CMP_HIDDEN ** -0.5),
        'pe_cmp_v': nrm(ks[16], (DEPTH, CMP_LEN, NSA_HEAD_DIM), 0.1),
        'w_cmp_v1': nrm(ks[17], (DEPTH, CMP_LEN * NSA_HEAD_DIM, CMP_HIDDEN), (CMP_LEN * NSA_HEAD_DIM) ** -0.5),
        'w_cmp_v2': nrm(ks[18], (DEPTH, CMP_HIDDEN, NSA_HEAD_DIM), CMP_HIDDEN ** -0.5),
        'w_conv': nrm(ks[19], (DEPTH, GDN_CONV, GDN_CONV_CH), GDN_CONV ** -0.5),
        'a_log': jnp.log(jax.random.uniform(ks[21], (DEPTH, GDN_HEADS), f32, 1.0, 16.0)),
        'dt_bias': jnp.log(jnp.expm1(dt)),
        'w_gdn_norm': 1.0 + nrm(ks[22], (DEPTH, GDN_DV), 0.02),
        'w_out': nrm(ks[23], (DEPTH, D_MIX, D_MODEL), D_MIX ** -0.5),
        'w_final_norm': 1.0 + nrm(jax.random.fold_in(key, 99), (D_MODEL,), 0.02),
    }


def reference(x_prompt, x_sample, cache_cmp_k, cache_cmp_v, cache_slc_k, cache_slc_v, cache_win_k, cache_win_v, state_conv, state_gdn, page_table, w_norm, w_in, pe_cmp_k, w_cmp_k1, w_cmp_k2, pe_cmp_v, w_cmp_v1, w_cmp_v2, w_conv, a_log, dt_bias, w_gdn_norm, w_out, w_final_norm):
    h_p, h_s = x_prompt, x_sample
    st_p, st_s = [], []
    for layer in range(DEPTH):
        lw = (w_norm[layer], w_in[layer], pe_cmp_k[layer], w_cmp_k1[layer], w_cmp_k2[layer], pe_cmp_v[layer], w_cmp_v1[layer], w_cmp_v2[layer], w_conv[layer], a_log[layer], dt_bias[layer], w_gdn_norm[layer], w_out[layer])
        h_p, sp = layer_prompt(h_p, lw)
        h_s, ss = layer_sample(h_s, cache_cmp_k[layer], cache_cmp_v[layer], cache_slc_k[layer], cache_slc_v[layer], cache_win_k[layer], cache_win_v[layer], state_conv[layer], state_gdn[layer], page_table, lw)
        st_p.append(sp)
        st_s.append(ss)

    def stk(states, i):
        return jnp.stack([s[i] for s in states])

    y_prompt = rmsnorm(h_p, w_final_norm)
    y_sample = rmsnorm(h_s, w_final_norm)
    return (y_prompt, y_sample, stk(st_p, 0), stk(st_s, 0), stk(st_p, 1), stk(st_s, 1), stk(st_p, 2), stk(st_s, 2), stk(st_p, 3), stk(st_s, 3), stk(st_p, 4), stk(st_s, 4), stk(st_p, 5), stk(st_s, 5), stk(st_p, 6), stk(st_s, 6), stk(st_p, 7), stk(st_s, 7))
```

```python
import os
import numpy as np
import ml_dtypes
import concourse.bass as bass
import concourse.mybir as mybir
from concourse.bass_utils import run_bass_kernel_spmd
from contextlib import ExitStack

F32 = mybir.dt.float32
BF16 = mybir.dt.bfloat16
I32 = mybir.dt.int32
ALU = mybir.AluOpType
AF = mybir.ActivationFunctionType
AX = mybir.AxisListType
NEG = -1.0e30
NCORES = 8
NT = 16
TS = 64
NB = 16
NIN = 3872
C_GATE, C_ZN, C_QKV, C_B, C_A, C_ZG = 1280, 1304, 1816, 3352, 3356, 3360


class Sch:
    ENGS = ("pe", "act", "dve", "pool", "sp")
    NRING = 8

    def __init__(self, nc, es):
        self.nc = nc
        self.prog = {e: [] for e in self.ENGS}
        self.sems = {}
        for e in self.ENGS:
            self.sems[e] = es.enter_context(nc.semaphore("c_" + e))
        self.cnt = {e: 0 for e in self.ENGS}
        self.waited = {e: {} for e in self.ENGS}
        self.ring, self.ring_val, self.ring_i = {}, {}, {}
        for q in ("sp", "act", "pool"):
            self.ring[q] = [es.enter_context(nc.semaphore(f"d_{q}{i}")) for i in range(self.NRING)]
            for i in range(self.NRING):
                self.sems[("d", q, i)] = self.ring[q][i]
            self.ring_val[q] = [0] * self.NRING
            self.ring_i[q] = 0
        self.lastw, self.readers = {}, {}
        self.nops = 0
        self.eo = {"pe": nc.tensor, "act": nc.scalar, "dve": nc.vector, "pool": nc.gpsimd, "sp": nc.sync}

    def _deps(self, reads, writes):
        deps = {}

        def add(k, v):
            if deps.get(k, 0) < v:
                deps[k] = v
        for t in reads:
            if t in self.lastw:
                add(*self.lastw[t])
        for t in writes:
            if t in self.lastw:
                add(*self.lastw[t])
            for k, v in self.readers.get(t, {}).items():
                add(k, v)
        return deps

    def _emit_waits(self, eng, deps, same_engine=True):
        for k, v in deps.items():
            if k == eng and not same_engine:
                continue
            if self.waited[eng].get(k, 0) >= v:
                continue
            self.waited[eng][k] = v
            sem = self.sems[k]
            self.eo[eng].wait_ge(sem, v)

    def _record(self, ev, reads, writes):
        for t in writes:
            self.lastw[t] = ev
            self.readers[t] = {}
        for t in reads:
            if t in writes:
                continue
            r = self.readers.setdefault(t, {})
            if r.get(ev[0], 0) < ev[1]:
                r[ev[0]] = ev[1]

    def op(self, eng, fns, reads=(), writes=()):
        if callable(fns):
            fns = [fns]
        writes = list(writes) + [t for t in reads if isinstance(t, str) and t.startswith("ps_")]
        reads = [t for t in reads if not (isinstance(t, str) and t.startswith("ps_"))]
        deps = self._deps(reads, writes)
        self._emit_waits(eng, deps, same_engine=(eng != "pe") and not os.environ.get("NOSAME"))
        self.cnt[eng] += 1
        sem = self.sems[eng]
        n = len(fns)
        for i, f in enumerate(fns):
            if i == n - 1:
                f(self.eo[eng]).then_inc(sem, 1)
            else:
                f(self.eo[eng])
        self.nops += n
        ev = (eng, self.cnt[eng])
        self._record(ev, reads, writes)
        return ev

    def dma(self, q, fns, reads=(), writes=()):
        if callable(fns):
            fns = [fns]
        i = self.ring_i[q]
        self.ring_i[q] = (i + 1) % self.NRING
        key = ("d", q, i)
        deps = self._deps(reads, writes)
        if self.ring_val[q][i] > 0:
            deps[key] = max(deps.get(key, 0), self.ring_val[q][i])
        self._emit_waits(q, deps)
        sem = self.ring[q][i]
        for f in fns:
            f(self.eo[q]).then_inc(sem, 16)
        self.ring_val[q][i] += 16 * len(fns)
        self.nops += len(fns)
        ev = (key, self.ring_val[q][i])
        self._record(ev, reads, writes)
        return ev

    def finish(self):
        deps = {}
        for q in self.ring:
            for i in range(self.NRING):
                if self.ring_val[q][i] > 0:
                    deps[("d", q, i)] = self.ring_val[q][i]
        for e in self.ENGS:
            if e != "sp" and self.cnt[e] > 0:
                deps[e] = self.cnt[e]
        self._emit_waits("sp", deps)

    def emit(self):
        pass


def _bf(a):
    return np.asarray(a, np.float32).astype(ml_dtypes.bfloat16)


def make_consts():
    c = {}
    i128 = np.arange(128)
    c["identb"] = _bf(np.eye(128))
    c["identf"] = np.eye(128, dtype=np.float32)
    c["onesb"] = _bf(np.ones((128, 128)))
    c["onesf"] = np.ones((128, 128), np.float32)
    inv = (500000.0 ** (-(np.arange(8, dtype=np.float32) * 2.0 / 16))).astype(np.float32)
    pos = np.arange(2048, dtype=np.float32)
    ang = (pos[:, None] * inv[None, :]).astype(np.float32)
    cs = np.concatenate([np.cos(ang), np.sin(ang)], -1).astype(np.float32)
    c["cs_p"] = np.ascontiguousarray(cs.reshape(16, 128, 16).transpose(1, 0, 2))
    poss = (2048 + np.arange(4, dtype=np.float32))
    angs = (poss[:, None] * inv[None, :]).astype(np.float32)
    css = np.concatenate([np.cos(angs), np.sin(angs)], -1).astype(np.float32)
    c["cs_s"] = np.ascontiguousarray(np.tile(css, (16, 1)).reshape(64, 1, 16))
    cc = np.arange(128)
    t_abs = np.arange(2048)
    cm = np.where((16 * cc[:, None] + 31) <= t_abs[None, :], 0.0, NEG)
    cm[127, :] = NEG
    c["cmask"] = _bf(cm.reshape(128, 16, 128))
    n = np.arange(33)
    ov = np.minimum(16 * cc[:, None] + 32, 64 * n[None, :] + 64) - np.maximum(16 * cc[:, None], 64 * n[None, :])
    ov = np.maximum(ov, 0).astype(np.float32) / 32.0
    ov[127, :] = 0.0
    c["ovl"] = _bf(np.concatenate([np.ones((128, 1), np.float32), ov], 1))
    qb = t_abs // 64
    blk = np.arange(32)
    valid = (blk[None, :] <= qb[:, None])
    forced = (blk[None, :] == 0) | (blk[None, :] == qb[:, None]) | (blk[None, :] == qb[:, None] - 1)
    bonus = np.where(valid, 1.0e4 * forced, NEG).astype(np.float32)
    c["valid_p"] = _bf(np.ascontiguousarray(valid.astype(np.float32).reshape(16, 128, 32).transpose(1, 0, 2)))
    c["bonus_p"] = np.ascontiguousarray(bonus.reshape(16, 128, 32).transpose(1, 0, 2))
    blk33 = np.arange(33)
    forced_s = ((blk33 == 0) | (blk33 == 32) | (blk33 == 31)).astype(np.float32)
    c["valid_s"] = _bf(np.ones((64, 1, 33), np.float32))
    c["bonus_s"] = np.ascontiguousarray(np.tile(1.0e4 * forced_s[None, None, :], (64, 1, 1))).astype(np.float32)
    c["causneg"] = _bf(np.where(i128[:, None] <= i128[None, :], 0.0, NEG))
    c["edgeneg"] = _bf(np.where(i128[:, None] >= i128[None, :], 0.0, NEG))
    irs = np.zeros((64, 16, 4, 4), np.float32)
    for b in range(16):
        for t in range(4):
            irs[4 * b + t, b, :, t] = 1.0
    c["irep_s"] = _bf(irs)
    nm = np.zeros((128, 16, 4, 4), np.float32)
    nm[:64] = NEG
    for b in range(16):
        for tk in range(4):
            for t in range(4):
                if tk <= t:
                    nm[4 * b + tk, b, :, t] = 0.0
    c["newmask_s"] = _bf(nm)
    wm = np.zeros((128, 4, 4), np.float32)
    for t in range(4):
        wm[:t, :, t] = NEG
    c["winmask_s"] = _bf(wm)
    for nm_, cs_ in (("p", 64), ("s", 4)):
        ch = i128 // cs_
        same = ch[:, None] == ch[None, :]
        c["tri_" + nm_] = (same & (i128[:, None] <= i128[None, :])).astype(np.float32)
        c["chk_" + nm_] = same.astype(np.float32)
        c["negU_" + nm_] = np.where(same & (i128[:, None] <= i128[None, :]), 0.0, NEG).astype(np.float32)
        c["negL_" + nm_] = np.where(same & (i128[:, None] >= i128[None, :]), 0.0, NEG).astype(np.float32)
        c["strictL_" + nm_] = (same & (i128[:, None] > i128[None, :])).astype(np.float32)
    c["chsel_p"] = (i128[:, None] // 64 == np.arange(2)[None, :]).astype(np.float32)
    c["chsel_s"] = (i128[:64, None] // 4 == np.arange(16)[None, :]).astype(np.float32)
    c["rgcol"] = (i128 % 8).astype(np.float32).reshape(128, 1)
    n32 = np.arange(32)
    ex = np.zeros((128, 16, 128), np.float32)
    for j in range(16):
        ex[:32, j, :] = (n32[:, None] == (2 * j + i128[None, :] // 64))
    c["expand_p"] = _bf(ex)
    exs = np.zeros((128, 128), np.float32)
    exs[:32] = (n32[:, None] == (i128[None, :] // 4))
    c["expand_s"] = _bf(exs)
    exn = np.zeros((128, 64), np.float32)
    exn[32, :] = 1.0
    c["expand_n"] = _bf(exn)
    return c


CONST_DT = {"valid_p": BF16, "valid_s": BF16, "identb": BF16, "onesb": BF16, "cmask": BF16, "ovl": BF16, "causneg": BF16, "edgeneg": BF16,
            "irep_s": BF16, "newmask_s": BF16, "winmask_s": BF16, "expand_p": BF16, "expand_s": BF16, "expand_n": BF16}


def build_program(consts, do_prompt=True, do_sample=True, nt_prompt=NT, stage=99):
    nc = bass.Bass("TRN2", target_bir_lowering=False)
    es = ExitStack()
    D = {}

    def din(name, shape, dt=F32):
        D[name] = nc.dram_tensor(name, list(shape), dt, kind="ExternalInput").ap()
        return D[name]

    def dout(name, shape):
        D[name] = nc.dram_tensor(name, list(shape), F32, kind="ExternalOutput").ap()
        return D[name]

    din("xp", [2048, 1024]); din("xs", [64, 1024])
    for nm in ("cck", "ccv", "csk", "csv"):
        din(nm, [2560 * 8, 2048])
    din("cwk", [16, 512, 128]); din("cwv", [16, 512, 128])
    din("sconv", [48, 1536]); din("sgdn", [16, 4, 128, 128])
    din("ptl", [128, 16], I32)
    din("w_norm", [1, 1024]); din("w_in", [1024, NIN])
    din("pe_k", [32, 64]); din("w1k", [2048, 128]); din("w2k", [128, 64])
    din("pe_v", [32, 64]); din("w1v", [2048, 128]); din("w2v", [128, 64])
    din("w_conv", [4, 1536]); din("a_log", [1, 4]); din("dt_bias", [1, 4]); din("wg", [1, 128])
    din("w_out", [1024, 1024]); din("w_fn", [1, 1024])
    for k, v in consts.items():
        din("k_" + k, v.shape, CONST_DT.get(k, F32))
    dout("y_p", [2048, 1024]); dout("y_s", [64, 1024])
    for nm in ("ck", "cv", "sk", "sv"):
        dout(nm + "_p", [2048, 128]); dout(nm + "_s", [64, 128])
    dout("wk_p", [512, 128]); dout("wv_p", [512, 128])
    dout("wk_s", [16, 512, 128]); dout("wv_s", [16, 512, 128])
    dout("conv_p", [3, 1536]); dout("conv_s", [16, 3, 1536])
    dout("gdn_p", [4, 128, 128]); dout("gdn_s", [16, 4, 128, 128])

    with es:
        s = Sch(nc, es)

        def sb(name, shape, dt=F32):
            return es.enter_context(nc.sbuf_tensor(name, list(shape), dt))

        def psb(name, shape, dt=F32):
            return es.enter_context(nc.psum_tensor(name, list(shape), dt))

        K = {}
        for k, v in consts.items():
            K[k] = sb("K" + k, v.shape, CONST_DT.get(k, F32))
            s.dma("sp", lambda e, k=k: e.dma_start(out=K[k][:], in_=D["k_" + k]), writes=["K" + k])
        identb, identf = K["identb"], K["identf"]

        def bcast_load(name, src, n):
            t = sb(name, [128, n])
            s.dma("sp", lambda e: e.dma_start(out=t[:], in_=src.partition_broadcast(128)), writes=[name])
            return t
        wnorm_bc = bcast_load("wnorm_bc", D["w_norm"][0:1, :], 1024)
        wfn_bc = bcast_load("wfn_bc", D["w_fn"][0:1, :], 1024)
        wg_bc = bcast_load("wg_bc", D["wg"][0:1, :], 128)
        alog_bc = bcast_load("alog_bc", D["a_log"][0:1, :], 4)
        dtb_bc = bcast_load("dtb_bc", D["dt_bias"][0:1, :], 4)
        nea_bc = sb("nea_bc", [128, 4])
        s.op("act", lambda e: e.activation(out=nea_bc[:], in_=alog_bc[:], func=AF.Exp), reads=["alog_bc"], writes=["nea_bc"])
        s.op("dve", lambda e: e.tensor_scalar(out=nea_bc[:], in0=nea_bc[:], scalar1=-1.0, scalar2=None, op0=ALU.mult), reads=["nea_bc"], writes=["nea_bc"])
        wconv = sb("wconv", [128, 12, 4])
        s.dma("sp", [lambda e, j=j: e.dma_start(out=wconv[:, :, j], in_=D["w_conv"][j].rearrange("(c p) -> p c", p=128), allow_slow_non_contiguous=True) for j in range(4)], writes=["wconv"])

        h0_ = sb("h0", [128, NIN])
        hbuf = [h0_, h0_]
        wscr = nc.dram_tensor("wscr", [8, 128, NIN], BF16, kind="Internal").ap()
        wstage = [sb("wstage0", [128, 8, 512], BF16), sb("wstage1", [128, 8, 512], BF16)]
        WTMP = []
        wscr2 = nc.dram_tensor("wscr2", [8, 128, 1024], BF16, kind="Internal").ap()
        w1b = sb("w1b", [128, 2, 32, 128], BF16)
        for kind, nm in enumerate(("w1k", "w1v")):
            st = hbuf[0]
            tok = "h0"
            for half in range(2):
                src = D[nm][1024 * half:1024 * (half + 1), :].rearrange("(p d) h -> d p h", d=64)
                s.dma("sp", [lambda e, st=st, src=src: e.dma_start(out=st[0:64, 0:2048].rearrange("d (p h) -> d p h", h=128), in_=src),
                             lambda e, st=st, src=src: e.dma_start(out=st[64:128, 0:2048].rearrange("d (p h) -> d p h", h=128), in_=src)], writes=[tok])
                s.op("dve", lambda e, st=st, kind=kind, half=half: e.tensor_copy(out=w1b[:, kind, 16 * half:16 * (half + 1), :], in_=st[:, 0:2048].rearrange("d (p h) -> d p h", h=128)), reads=[tok], writes=["w1b"])
        w2f = sb("w2f", [128, 2, 64])
        w2b = sb("w2b", [128, 2, 64], BF16)
        s.dma("sp", [lambda e: e.dma_start(out=w2f[:, 0, :], in_=D["w2k"]), lambda e: e.dma_start(out=w2f[:, 1, :], in_=D["w2v"])], writes=["w2f"])
        s.op("dve", lambda e: e.tensor_copy(out=w2b[:], in_=w2f[:]), reads=["w2f"], writes=["w2b"])
        pef = sb("pef", [128, 2, 32])
        peb = sb("peb", [128, 2, 32], BF16)
        s.dma("sp", [lambda e, hh=hh, kind=kind, nm=nm: e.dma_start(out=pef[64 * hh:64 * hh + 64, kind, :], in_=D[nm].rearrange("p d -> d p"), allow_slow_non_contiguous=True)
                     for hh in range(2) for kind, nm in enumerate(("pe_k", "pe_v"))], writes=["pef"])
        s.op("dve", lambda e: e.tensor_copy(out=peb[:], in_=pef[:]), reads=["pef"], writes=["peb"])

        PS = {}

        def mkps(name, n, dt=F32, cols=512):
            PS[name] = [[(psb(f"ps_{name}{i}", [128, cols], dt), f"ps_{name}{i}") for i in range(n)], 0]

        mkps("mm", 2)
        mkps("tp", 2)
        mkps("acc", 2)
        mkps("g", 2)

        def ps(name):
            lst, i = PS[name]
            PS[name][1] = (i + 1) % len(lst)
            return lst[i]

        rr = [0]

        def evac_eng():
            rr[0] ^= 1
            return "act" if rr[0] else "dve"

        def copy_op(eng, out, in_, reads, writes):
            if eng == "act":
                s.op("act", lambda e: e.copy(out=out, in_=in_), reads=reads, writes=writes)
            else:
                s.op(eng, lambda e: e.tensor_copy(out=out, in_=in_), reads=reads, writes=writes)

        cbias = sb("cbias", [128, 2])
        pt_, ptok = ps("g")
        for kind in range(2):
            s.op("pe", [lambda e, kind=kind, p=p: e.matmul(pt_[:, kind:kind + 1], lhsT=w1b[0:64, kind, p, :], rhs=peb[0:64, kind, p:p + 1], start=(p == 0), stop=(p == 31)) for p in range(32)],
                 reads=["w1b", "peb"], writes=[ptok])
        s.op("dve", lambda e: e.tensor_copy(out=cbias[:], in_=pt_[:, 0:2]), reads=[ptok], writes=["cbias"])

        xt0_ = sb("xt0", [128, 1024])
        xt = [xt0_, xt0_]
        small = sb("small", [128, 64])
        xnb = sb("xnb", [128, 1024], BF16)
        xnT = sb("xnT", [128, 8, 128], BF16)
        ropet = sb("ropet", [128, 4, 8, 8])
        qb16 = sb("qb16", [128, 2, 512], BF16)
        kvb16 = sb("kvb16", [128, 768], BF16)
        qT = sb("qT", [128, 2, 4, 128], BF16)
        gates = sb("gates", [128, 24])
        zsil = sb("zsil", [128, 2, 512], BF16)
        onsa = sb("onsa", [128, 8, 64])
        mixb = sb("mixb", [128, 1024], BF16)
        mixT = sb("mixT", [128, 8, 128], BF16)
        hTc = sb("hTc", [128, 2, 2, 128], BF16)
        cckT = sb("cckT", [128, 128], BF16)
        ccv1 = sb("ccv1", [128, 2, 98], BF16)
        ET = sb("ET", [128, 2, 4, 128], BF16)
        imp = sb("imp", [128, 2, 33])
        m8 = sb("m8", [128, 8])
        selneg = sb("selneg", [128, 2, 33], BF16)
        selT = sb("selT", [128, 2, 128], BF16)
        s.op("pool", lambda e: e.memset(selT[:], 0.0), writes=["selT"])
        rz = sb("rz", [128, 8])
        wgt = sb("wgt", [128, 8])
        kT_all = sb("kT_all", [128, 4, 2048], BF16)
        v1_all = sb("v1_all", [128, 2, 16, 2, 65], BF16)
        convb = sb("convb", [128, 12, 131])
        cacc = sb("cacc", [128, 12, 128])
        ctmp = sb("ctmp", [128, 12, 128])
        gact = sb("gact", [128, 12, 128])
        gsq = sb("gsq", [128, 8, 128], BF16)
        gqk = sb("gqk", [128, 8, 128])
        gqkb = sb("gqkb", [128, 8, 128], BF16)
        gtok = sb("gtok", [128, 8, 128])
        gsc = sb("gsc", [128, 32])
        NSET = int(os.environ.get("NSET", "4"))
        HB = []
        HBBIG = []
        for i in range(NSET):
            big = sb("hbbig%d" % i, [128, 2048 if i == 0 else 1936])
            HBBIG.append(big)
            o_ = [0]

            def carve(n, dt=F32, big=big, o_=o_):
                w_ = n if dt == F32 else n // 2
                ap = big[:, o_[0]:o_[0] + w_]
                o_[0] += w_
                return ap if dt == F32 else ap.bitcast(BF16)
            d_ = {k_: carve(128) for k_ in "Gb Eup Elo etmp EDrow uf wTf qdT qkT kd vnew".split()}
            d_["GLb"] = carve(16)
            d_["Nm"] = [carve(128, BF16), carve(128, BF16)]
            d_["Mm"] = [carve(128, BF16), carve(128, BF16)]
            d_["Xb"] = [carve(256, BF16), carve(256, BF16)]
            d_["i"] = i
            HB.append(d_)
        Sst = [sb("S%d" % i, [128, 128]) for i in range(4)]
        ogdn = sb("ogdn", [128, 4, 128])
        otmp = sb("otmp", [128, 4, 128])
        v1flat = v1_all[:].rearrange("p a b c d -> p (a b c d)")
        kTf32 = kT_all[:].rearrange("p a n -> p (a n)").bitcast(F32)
        ws0flat = wstage[0][:].rearrange("p k n -> p (k n)")
        pairs = [(hbuf[0], "h0", v1flat, "v1_all"), (kTf32, "kT_all", ws0flat, "wstage0")]
        for k in range(8):
            st_, sttok_, tb_, tbtok_ = pairs[k % 2]
            s.dma("sp", lambda e, k=k, st_=st_: e.dma_start(out=st_[:, 0:NIN], in_=D["w_in"][128 * k:128 * (k + 1), :]), writes=[sttok_])
            s.op("dve" if k % 2 == 0 else "pool", lambda e, st_=st_, tb_=tb_: e.tensor_copy(out=tb_[:, 0:NIN], in_=st_[:, 0:NIN]), reads=[sttok_], writes=[tbtok_])
            s.dma("sp", lambda e, k=k, tb_=tb_: e.dma_start(out=wscr[k], in_=tb_[:, 0:NIN]), reads=[tbtok_], writes=["wscr"])
        for k in range(8):
            st_, sttok_, tb_, tbtok_ = pairs[k % 2]
            s.dma("sp", lambda e, k=k, st_=st_: e.dma_start(out=st_[:, 0:1024], in_=D["w_out"][128 * k:128 * (k + 1), :]), writes=[sttok_])
            s.op("dve" if k % 2 == 0 else "pool", lambda e, st_=st_, tb_=tb_: e.tensor_copy(out=tb_[:, 0:1024], in_=st_[:, 0:1024]), reads=[sttok_], writes=[tbtok_])
            s.dma("sp", lambda e, k=k, tb_=tb_: e.dma_start(out=wscr2[k], in_=tb_[:, 0:1024]), reads=[tbtok_], writes=["wscr2"])
        v1init = [False]
        s.op("pool", lambda e: e.memset(hTc[:], 0.0), writes=["hTc"])
        for hh_ in range(2):
            s.op("pool", lambda e, hh_=hh_: e.tensor_copy(out=ccv1[:, hh_, 64:98], in_=K["ovl"][:, :]), reads=["Kovl"], writes=["ccv1"])

        def run_tile(ctx):
            T = ctx["T"]
            it = ctx["it"]
            slot = ctx["slot"]
            x_src = ctx["x_src"]
            xtile, xtok = xt[0], "xt0"
            h, htok = hbuf[0], "h0"
            sample = ctx["sample"]
            sfx = "s" if sample else "p"
            s.dma("sp", lambda e: e.dma_start(out=xtile[0:T, :], in_=x_src), writes=[xtok])
            s.op("dve", lambda e: e.memset(small[0:T, 0:8], 0.0), writes=["small"])
            s.op("act", lambda e: e.activation(out=mixb[0:T, :], in_=xtile[0:T, :], func=AF.Square, accum_out=small[0:T, 0:1]), reads=[xtok], writes=["mixb", "small"])
            s.op("dve", lambda e: e.tensor_scalar(out=small[0:T, 1:2], in0=small[0:T, 0:1], scalar1=1.0 / 1024, scalar2=1e-6, op0=ALU.mult, op1=ALU.add), reads=["small"], writes=["small"])
            s.op("act", lambda e: e.activation(out=small[0:T, 2:3], in_=small[0:T, 1:2], func=AF.Sqrt), reads=["small"], writes=["small"])
            s.op("dve", lambda e: e.reciprocal(out=small[0:T, 2:3], in_=small[0:T, 2:3]), reads=["small"], writes=["small"])
            s.op("dve", lambda e: e.scalar_tensor_tensor(out=xnb[0:T, :], in0=xtile[0:T, :], scalar=small[0:T, 2:3], in1=wnorm_bc[0:T, :], op0=ALU.mult, op1=ALU.mult),
                 reads=[xtok, "small", "wnorm_bc"], writes=["xnb"])
            tp, tptok = ps("tp")
            tpb = tp[:].bitcast(BF16)
            s.op("pe", [lambda e, k=k: e.transpose(out=tpb[:, 128 * k:128 * k + T], in_=xnb[0:T, 128 * k:128 * (k + 1)], identity=identb[0:T, 0:T]) for k in range(8)],
                 reads=["xnb", "Kidentb"], writes=[tptok])
            copy_op(evac_eng(), xnT[:, :, 0:T], tpb.rearrange("p (k t) -> p k t", t=128)[:, :, 0:T], [tptok], ["xnT"])
            for gi, c0 in enumerate(range(0, NIN, 512)):
                cw = min(512, NIN - c0)
                pm, pmtok = ps("mm")
                wsl = gi % 2
                wsg, wstok = wstage[wsl], "wstage%d" % wsl
                s.dma("sp", lambda e, c0=c0, cw=cw, wsg=wsg: e.dma_start(out=wsg[:, :, 0:cw], in_=wscr[:, :, c0:c0 + cw].rearrange("k p n -> p k n")), reads=["wscr"], writes=[wstok])
                s.op("pe", [lambda e, k=k, c0=c0, cw=cw, pm=pm, wsg=wsg: e.matmul(pm[0:T, 0:cw], lhsT=xnT[:, k, 0:T], rhs=wsg[:, k, 0:cw], start=(k == 0), stop=(k == 7)) for k in range(8)],
                     reads=["xnT", wstok], writes=[pmtok])
                copy_op(evac_eng(), h[0:T, c0:c0 + cw], pm[0:T, 0:cw], [pmtok], [htok])
            if not sample:
                for gi, c0 in enumerate((0, 512)):
                    wsg, wstok = wstage[gi], "wstage%d" % gi
                    s.dma("sp", lambda e, c0=c0, wsg=wsg: e.dma_start(out=wsg[:, :, :], in_=wscr2[:, :, c0:c0 + 512].rearrange("k p n -> p k n")), reads=["wscr2"], writes=[wstok])
            if stage < 2:
                return
            cs = K["cs_" + sfx]
            cosv = cs[0:T, it, 0:8]
            sinv = cs[0:T, it, 8:16]

            def rope(view, nh_shape, outs):
                A, B = nh_shape
                x1 = view[:, :, :, 0:8]
                x2 = view[:, :, :, 8:16]
                cb = cosv.unsqueeze(1).unsqueeze(1).to_broadcast([T, A, B, 8])
                sbc = sinv.unsqueeze(1).unsqueeze(1).to_broadcast([T, A, B, 8])
                t = [ropet[0:T, j, 0:A * B, :].rearrange("p (a b) d -> p a b d", b=B) for j in range(4)]
                rd = [htok, "K" + "cs_" + sfx]
                s.op("dve", lambda e: e.tensor_tensor(out=t[0], in0=x1, in1=cb, op=ALU.mult), reads=rd, writes=["ropet"])
                s.op("dve", lambda e: e.tensor_tensor(out=t[1], in0=x2, in1=sbc, op=ALU.mult), reads=rd, writes=["ropet"])
                s.op("dve", lambda e: e.tensor_tensor(out=t[2], in0=x2, in1=cb, op=ALU.mult), reads=rd, writes=["ropet"])
                s.op("dve", lambda e: e.tensor_tensor(out=t[3], in0=x1, in1=sbc, op=ALU.mult), reads=rd, writes=["ropet"])
                return t
            hq = h[0:T, 0:512].rearrange("p (a g d) -> p a g d", a=2, g=4)
            qraw = qb16[0:T, 0, :].rearrange("p (g a d) -> p a g d", g=4, a=2)
            qo = qb16[0:T, 1, :].rearrange("p (g a d) -> p a g d", g=4, a=2)
            s.op("act", lambda e: e.copy(out=qraw, in_=hq), reads=[htok], writes=["qb16"])
            s.op("pool", lambda e: e.tensor_copy(out=qo, in_=hq), reads=[htok], writes=["qb16"])
            t = rope(hq[:, :, :, 0:16], (2, 4), None)
            s.op("dve", lambda e: e.tensor_tensor(out=qo[:, :, :, 0:8], in0=t[0], in1=t[1], op=ALU.subtract), reads=["ropet"], writes=["qb16"])
            s.op("dve", lambda e: e.tensor_tensor(out=qo[:, :, :, 8:16], in0=t[2], in1=t[3], op=ALU.add), reads=["ropet"], writes=["qb16"])
            kvw = h[0:T, 768:1280].rearrange("p (a r) -> p a r", a=2)[:, :, 0:128].rearrange("p a (b d) -> p a b d", b=2)
            t = rope(kvw[:, :, :, 0:16], (2, 2), None)
            s.op("dve", lambda e: e.tensor_tensor(out=kvw[:, :, :, 0:8], in0=t[0], in1=t[1], op=ALU.subtract), reads=["ropet"], writes=[htok])
            s.op("dve", lambda e: e.tensor_tensor(out=kvw[:, :, :, 8:16], in0=t[2], in1=t[3], op=ALU.add), reads=["ropet"], writes=[htok])
            r0 = ctx["row0"]
            outs = []
            for j, nm in enumerate(("ck", "cv", "sk", "sv")):
                dst = D[nm + "_" + sfx][r0:r0 + T, :]
                outs.append(lambda e, j=j, dst=dst: e.dma_start(out=dst, in_=h[0:T, 512 + 128 * j:640 + 128 * j]))
            s.dma("sp", outs, reads=[htok])
            if not sample and it >= NT - 4:
                w0 = (it - (NT - 4)) * 128
                s.dma("sp", [lambda e: e.dma_start(out=D["wk_p"][w0:w0 + 128, :], in_=h[0:T, 1024:1152]),
                             lambda e: e.dma_start(out=D["wv_p"][w0:w0 + 128, :], in_=h[0:T, 1152:1280])], reads=[htok])
            if not sample and it == NT - 1:
                s.dma("sp", lambda e: e.dma_start(out=D["conv_p"], in_=h[125:128, C_QKV:C_QKV + 1536]), reads=[htok])
            if sample:
                s.dma("sp", [lambda e, b=b: e.dma_start(out=D["conv_s"][b], in_=h[4 * b + 1:4 * b + 4, C_QKV:C_QKV + 1536]) for b in range(NB)], reads=[htok])
                s.dma("sp", [lambda e, b=b: e.dma_start(out=D["wk_s"][b, 508:512, :], in_=h[4 * b:4 * b + 4, 1024:1152]) for b in range(NB)]
                      + [lambda e, b=b: e.dma_start(out=D["wv_s"][b, 508:512, :], in_=h[4 * b:4 * b + 4, 1152:1280]) for b in range(NB)], reads=[htok])
            if stage < 3:
                return
            s.op("pool", lambda e: e.tensor_copy(out=kvb16[0:T, :], in_=h[0:T, 512:1280]), reads=[htok], writes=["kvb16"])
            tp, tptok = ps("tp")
            tpb = tp[:].bitcast(BF16)
            srcs = [0, 1, 2, 4]
            s.op("pe", [lambda e, j=j, c=c: e.transpose(out=tpb[:, 128 * j:128 * j + T], in_=kvb16[0:T, 128 * c:128 * (c + 1)], identity=identb[0:T, 0:T]) for j, c in enumerate(srcs)],
                 reads=["kvb16", "Kidentb"], writes=[tptok])
            if stage < 3.2:
                return
            kdst = ctx["kT_dst"]
            copy_op(evac_eng(), kdst, tpb[:, 0:512].rearrange("p (k t) -> p k t", t=128)[:, :, 0:T], [tptok], [ctx["kT_tok"]])
            if stage < 3.4:
                return
            if not v1init[0]:
                s.op("pool", lambda e: e.memset(v1_all[:], 1.0), writes=["v1_all"])
                v1init[0] = True
            vd = ctx["v1_dst"]
            s.op("pool", lambda e: e.tensor_copy(out=vd[:, 0, :, 0:64], in_=h[0:T, 896:1024].rearrange("p (h d) -> p h d", h=2)), reads=[htok], writes=[ctx["v1_tok"]])
            s.op("pool", lambda e: e.tensor_copy(out=vd[:, 1, :, 0:64], in_=h[0:T, 1152:1280].rearrange("p (h d) -> p h d", h=2)), reads=[htok], writes=[ctx["v1_tok"]])
            if stage < 3.6:
                return
            tp, tptok = ps("tp")
            tpb = tp[:].bitcast(BF16)
            fl = []
            for w in range(2):
                for g in range(4):
                    src = qb16[0:T, w, 128 * g:128 * (g + 1)]
                    fl.append(lambda e, w=w, g=g, src=src: e.transpose(out=tpb[:, 128 * (4 * w + g):128 * (4 * w + g) + T], in_=src, identity=identb[0:T, 0:T]))
            s.op("pe", fl, reads=["qb16", "Kidentb"], writes=[tptok])
            copy_op(evac_eng(), qT[:, :, :, 0:T], tpb.rearrange("p (w g t) -> p w g t", w=2, g=4)[:, :, :, 0:T], [tptok], ["qT"])
            if stage < 3.8:
                return
            s.op("act", lambda e: e.activation(out=gates[0:T, :], in_=h[0:T, C_GATE:C_GATE + 24], func=AF.Sigmoid), reads=[htok], writes=["gates"])
            if stage < 3.9:
                return
            s.op("act", lambda e: e.activation(out=zsil[0:T, 0, :], in_=h[0:T, C_ZN:C_ZN + 512], func=AF.Silu), reads=[htok], writes=["zsil"])
            s.op("act", lambda e: e.activation(out=zsil[0:T, 1, :], in_=h[0:T, C_ZG:C_ZG + 512], func=AF.Silu), reads=[htok], writes=["zsil"])

            if stage < 4:
                return
            gdn_tile(ctx, T, it, h, htok)
            if not sample:
                gp = ctx["gdn_prep"]

                def tick(n=2):
                    for _ in range(n):
                        try:
                            next(gp)
                        except StopIteration:
                            return
                TICK[0] = tick
                nsa_prompt(ctx, T, it)
                TICK[0] = None
            else:
                nsa_sample(ctx, T)
            s.op("dve", lambda e: e.tensor_tensor(out=mixb[0:T, 0:512], in0=onsa[0:T, :, :].rearrange("p a d -> p (a d)"), in1=zsil[0:T, 0, :], op=ALU.mult), reads=["onsa", "zsil"], writes=["mixb"])
            if stage < 5:
                return
            gdn_tile2(ctx, T, it, h, htok)
            if stage < 6:
                return
            tp, tptok = ps("tp")
            tpb = tp[:].bitcast(BF16)
            s.op("pe", [lambda e, k=k: e.transpose(out=tpb[:, 128 * k:128 * k + T], in_=mixb[0:T, 128 * k:128 * (k + 1)], identity=identb[0:T, 0:T]) for k in range(8)],
                 reads=["mixb", "Kidentb"], writes=[tptok])
            copy_op(evac_eng(), mixT[:, :, 0:T], tpb.rearrange("p (k t) -> p k t", t=128)[:, :, 0:T], [tptok], ["mixT"])
            for gi, c0 in enumerate((0, 512)):
                pm, pmtok = ps("mm")
                wsg, wstok = wstage[gi], "wstage%d" % gi
                if sample:
                    s.dma("sp", lambda e, c0=c0, wsg=wsg: e.dma_start(out=wsg[:, :, :], in_=wscr2[:, :, c0:c0 + 512].rearrange("k p n -> p k n")), reads=["wscr2"], writes=[wstok] + ctx.get("ws_extra", []))
                s.op("pe", [lambda e, k=k, c0=c0, pm=pm, wsg=wsg: e.matmul(pm[0:T, :], lhsT=mixT[:, k, 0:T], rhs=wsg[:, k, :], start=(k == 0), stop=(k == 7)) for k in range(8)],
                     reads=["mixT", wstok], writes=[pmtok])
                s.op("dve", lambda e, c0=c0, pm=pm: e.tensor_tensor(out=xtile[0:T, c0:c0 + 512], in0=pm[0:T, :], in1=xtile[0:T, c0:c0 + 512], op=ALU.add), reads=[pmtok, xtok], writes=[xtok])
            s.op("act", lambda e: e.activation(out=xnb[0:T, :], in_=xtile[0:T, :], func=AF.Square, accum_out=small[0:T, 4:5]), reads=[xtok], writes=["xnb", "small"])
            s.op("dve", lambda e: e.tensor_scalar(out=small[0:T, 5:6], in0=small[0:T, 4:5], scalar1=1.0 / 1024, scalar2=1e-6, op0=ALU.mult, op1=ALU.add), reads=["small"], writes=["small"])
            s.op("act", lambda e: e.activation(out=small[0:T, 6:7], in_=small[0:T, 5:6], func=AF.Sqrt), reads=["small"], writes=["small"])
            s.op("dve", lambda e: e.reciprocal(out=small[0:T, 6:7], in_=small[0:T, 6:7]), reads=["small"], writes=["small"])
            s.op("dve", lambda e: e.scalar_tensor_tensor(out=xtile[0:T, :], in0=xtile[0:T, :], scalar=small[0:T, 6:7], in1=wfn_bc[0:T, :], op0=ALU.mult, op1=ALU.mult),
                 reads=[xtok, "small", "wfn_bc"], writes=[xtok])
            s.dma("sp", lambda e: e.dma_start(out=ctx["y_dst"], in_=xtile[0:T, :]), reads=[xtok])

        def compress(rowsT_k, rowsT_v, rtok, c0, nblk):
            pms = [ps("mm"), ps("mm")]
            for kind, rows in enumerate((rowsT_k, rowsT_v)):
                for hh in range(2):
                    pm, pmtok = pms[hh]
                    fl = []
                    for p in range(32):
                        rhs = rows[64 * hh:64 * hh + 64, 16 * c0 + p:16 * c0 + p + 16 * (nblk - 1) + 1:16]
                        fl.append(lambda e, kind=kind, hh=hh, p=p, rhs=rhs, pm=pm: e.matmul(pm[:, kind * 127:kind * 127 + nblk],
                                                                              lhsT=w1b[64 * hh:64 * hh + 64, kind, p, :], rhs=rhs, start=(p == 0), stop=(p == 31)))
                    s.op("pe", fl, reads=["w1b", rtok], writes=[pmtok])
            if stage < 4.11:
                return
            for kind in range(2):
                for hh in range(2):
                    pm, pmtok = pms[hh]
                    s.op("act", lambda e, kind=kind, hh=hh, pm=pm: e.activation(out=hTc[:, kind, hh, c0:c0 + nblk], in_=pm[:, kind * 127:kind * 127 + nblk], func=AF.Silu, bias=cbias[:, kind:kind + 1]), reads=[pmtok, "cbias"], writes=["hTc"])
            if stage < 4.12:
                return
            pg, pgtok = ps("g")
            s.op("pe", [lambda e, hh=hh: e.matmul(pg[64 * hh:64 * hh + 64, 0:128], lhsT=w2b[:, 0, :], rhs=hTc[:, 0, hh, :], start=True, stop=True) for hh in range(2)]
                 + [lambda e, hh=hh: e.matmul(pg[:, 128 + 64 * hh:128 + 64 * hh + 64], lhsT=hTc[:, 1, hh, :], rhs=w2b[:, 1, :], start=True, stop=True) for hh in range(2)],
                 reads=["hTc", "w2b"], writes=[pgtok])
            if stage < 4.13:
                return
            copy_op("act", cckT[:, :], pg[:, 0:128], [pgtok], ["cckT"])
            copy_op("dve", ccv1[:, :, 0:64], pg[:, 128:256].rearrange("p (h d) -> p h d", h=2), [pgtok], ["ccv1"])

        def attn_branch(T, q_rhs, key_tiles, acc_cols, first, last, acc, acctok, ncols):
            pass

        def nsa_core(T, hh, qsel, key_tiles, W, rd_extra, branch, first_branch, gate_col, colmap=None):
            acc, acctok = ps("acc")
            accv = acc[0:T, 0:4 * W].rearrange("p (g w) -> p g w", g=4)
            nkt = len(key_tiles)
            pend = {}

            def issue_qk(j):
                kt = key_tiles[j]
                nk = kt["nk"]
                pm, pmtok = ps("mm")
                ncol = kt.get("ncol", 4 * T)
                fl = [lambda e, kt=kt, pm=pm, nk=nk, ncol=ncol: e.matmul(pm[0:nk, 0:ncol], lhsT=kt["kT"], rhs=kt["q"], start=True, stop=(len(kt["masks"]) == 0))]
                for mi, (ml, mr) in enumerate(kt["masks"]):
                    fl.append(lambda e, ml=ml, mr=mr, pm=pm, nk=nk, ncol=ncol, mi=mi, nm=len(kt["masks"]): e.matmul(pm[0:nk, 0:ncol], lhsT=ml, rhs=mr, start=False, stop=(mi == nm - 1)))
                s.op("pe", fl, reads=kt["rd"], writes=[pmtok])
                pend[j] = (pm, pmtok)

            issue_qk(0)
            for j, kt in enumerate(key_tiles):
                nk = kt["nk"]
                ncol = kt.get("ncol", 4 * T)
                pm, pmtok = pend.pop(j)
                eb = ctx_et[0]
                ctx_et[0] ^= 1
                etok = "ET%d" % eb
                s.op("act", lambda e, kt=kt, pm=pm, nk=nk, ncol=ncol, eb=eb: e.activation(out=kt["et_out"](eb), in_=kt["sc_in"](pm), func=AF.Exp, scale=0.125), reads=[pmtok], writes=[etok])
                if j + 1 < nkt:
                    issue_qk(j + 1)
                s.op("pe", [lambda e, g=g, kt=kt, eb=eb, nk=nk: e.matmul(accv[:, g, :], lhsT=ET[0:nk, eb, g, 0:T], rhs=kt["v1"], start=(j == 0 and g == 0), stop=(j == nkt - 1), skip_group_check=True) for g in range(4)],
                     reads=[etok] + kt["rdv"], writes=[acctok])
                if TICK[0] is not None:
                    TICK[0]()
            return acc, accv, acctok

        ctx_et = [0]
        TICK = [None]

        def branch_epilogue(T, hh, accv, acctok, W, gate_col, first_branch, want_imp=False, nblk=32):
            s.op("dve", lambda e: e.tensor_scalar(out=rz[0:T, 0:4], in0=accv[:, :, 64], scalar1=1e-30, scalar2=None, op0=ALU.max), reads=[acctok], writes=["rz"])
            s.op("dve", lambda e: e.reciprocal(out=rz[0:T, 0:4], in_=rz[0:T, 0:4]), reads=["rz"], writes=["rz"])
            s.op("dve", lambda e: e.tensor_tensor(out=wgt[0:T, 0:4], in0=rz[0:T, 0:4], in1=gates[0:T, gate_col + 4 * hh:gate_col + 4 * hh + 4], op=ALU.mult), reads=["rz", "gates"], writes=["wgt"])
            wb_ = wgt[0:T, 0:4].unsqueeze(2).to_broadcast([T, 4, 64])
            od = onsa[0:T, 4 * hh:4 * hh + 4, :]
            if first_branch:
                s.op("dve", lambda e: e.tensor_tensor(out=od, in0=accv[:, :, 0:64], in1=wb_, op=ALU.mult), reads=[acctok, "wgt"], writes=["onsa"])
            else:
                s.op("dve", lambda e: e.tensor_tensor(out=otmp[0:T, 0:2, :].rearrange("p a (g d) -> p (a g) d", d=64), in0=accv[:, :, 0:64], in1=wb_, op=ALU.mult), reads=[acctok, "wgt"], writes=["otmp"])
                s.op("dve", lambda e: e.tensor_tensor(out=od, in0=od, in1=otmp[0:T, 0:2, :].rearrange("p a (g d) -> p (a g) d", d=64), op=ALU.add), reads=["otmp", "onsa"], writes=["onsa"])
            if want_imp:
                iv = imp[0:T, hh, 0:nblk]
                s.op("dve", lambda e: e.tensor_scalar(out=iv, in0=accv[:, 0, 65:65 + nblk], scalar1=rz[0:T, 0:1], scalar2=None, op0=ALU.mult), reads=[acctok, "rz"], writes=["imp"])
                for g in range(1, 4):
                    s.op("dve", lambda e, g=g: e.scalar_tensor_tensor(out=iv, in0=accv[:, g, 65:65 + nblk], scalar=rz[0:T, g:g + 1], in1=iv, op0=ALU.mult, op1=ALU.add), reads=[acctok, "rz", "imp"], writes=["imp"])

        def select_blocks(T, hh, it, sfx, nblk):
            iv = imp[0:T, hh, 0:nblk]
            s.op("dve", lambda e: e.tensor_tensor(out=iv, in0=iv, in1=K["valid_" + sfx][0:T, it, :], op=ALU.mult), reads=["imp", "Kvalid_" + sfx], writes=["imp"])
            s.op("dve", lambda e: e.tensor_tensor(out=iv, in0=iv, in1=K["bonus_" + sfx][0:T, it, :], op=ALU.add), reads=["imp", "Kbonus_" + sfx], writes=["imp"])
            s.op("dve", lambda e: e.max(out=m8[0:T, :], in_=iv), reads=["imp"], writes=["m8"])
            s.op("dve", lambda e: e.tensor_scalar(out=selneg[0:T, hh, 0:nblk], in0=iv, scalar1=m8[0:T, 7:8], scalar2=NEG, op0=ALU.is_lt, op1=ALU.mult), reads=["imp", "m8"], writes=["selneg"])
            tp, tptok = ps("tp")
            tpb = tp[:].bitcast(BF16)
            s.op("pe", lambda e: e.transpose(out=tpb[0:nblk, 0:T], in_=selneg[0:T, hh, 0:nblk], identity=identb[0:T, 0:T]), reads=["selneg", "Kidentb"], writes=[tptok])
            copy_op("dve", selT[0:nblk, hh, 0:T], tpb[0:nblk, 0:T], [tptok], ["selT"])

        def std_tile(T, hh, qw, kT_ap, nk, v1_ap, masks, rd, rdv):
            return dict(kT=kT_ap, nk=nk, q=qT[64 * hh:64 * hh + 64, qw, :, 0:T], masks=masks, v1=v1_ap, rd=["qT"] + rd, rdv=rdv,
                        et_out=lambda eb, nk=nk: ET[0:nk, eb, :, 0:T], sc_in=lambda pm, nk=nk: pm[0:nk, 0:4 * T].rearrange("p (g t) -> p g t", g=4))

        def nsa_prompt(ctx, T, it):
            c0 = 0 if it == 0 else 8 * it - 1
            c1 = 8 * it + 6
            if stage < 4.1:
                return
            compress(kT_all[:, 0, :], kT_all[:, 1, :], "kT_all", c0, c1 - c0 + 1)
            ident4 = lambda M: M.unsqueeze(1).to_broadcast([128, 4, 128])
            if stage < 4.2:
                return
            res_c = []
            for hh in range(2):
                kt = std_tile(T, hh, 0, cckT[64 * hh:64 * hh + 64, :], 128, ccv1[:, hh, :],
                              [(identb[:, :], K["cmask"][:, it, :].unsqueeze(1).to_broadcast([128, 4, 128]))], ["cckT", "Kcmask", "Kidentb"], ["ccv1"])
                res_c.append(nsa_core(T, hh, 0, [kt], 98, [], "cmp", True, 0))
            for hh in range(2):
                acc, accv, acctok = res_c[hh]
                branch_epilogue(T, hh, accv, acctok, 98, 0, True, want_imp=True, nblk=32)
                select_blocks(T, hh, it, "p", 32)
            for hh in range(2):
                tiles = []
                for j in range(max(0, it - 4), it + 1):
                    masks = []
                    if j == it:
                        masks.append((identb[:, :], ident4(K["causneg"][:, :])))
                    elif j == it - 4:
                        masks.append((identb[:, :], ident4(K["edgeneg"][:, :])))
                    tiles.append(std_tile(T, hh, 1, kT_all[64 * hh:64 * hh + 64, 3, 128 * j:128 * j + 128], 128, v1_all[:, 1, j, hh, :], masks,
                                          ["kT_all", "Kidentb", "Kcausneg", "Kedgeneg"], ["v1_all"]))
                acc, accv, acctok = nsa_core(T, hh, 1, tiles, 65, [], "win", False, 16)
                branch_epilogue(T, hh, accv, acctok, 65, 16, False)
            for hh in range(2):
                tiles = []
                for j in range(it + 1):
                    masks = [(K["expand_p"][:, j, :], selT[:, hh, 0:T].unsqueeze(1).to_broadcast([128, 4, T]))]
                    if j == it:
                        masks.append((identb[:, :], ident4(K["causneg"][:, :])))
                    tiles.append(std_tile(T, hh, 1, kT_all[64 * hh:64 * hh + 64, 2, 128 * j:128 * j + 128], 128, v1_all[:, 0, j, hh, :], masks,
                                          ["kT_all", "selT", "Kexpand_p", "Kidentb", "Kcausneg"], ["v1_all"]))
                acc, accv, acctok = nsa_core(T, hh, 1, tiles, 65, [], "slc", False, 8)
                branch_epilogue(T, hh, accv, acctok, 65, 8, False)

        def gdn_tile(ctx, T, it, h, htok):
            sample = ctx["sample"]
            sfx = "s" if sample else "p"
            nlev = 2 if sample else 6
            def prep_gen():
                for grp in range(3):
                    tp, tptok = ps("tp")
                    s.op("pe", [lambda e, c=c, grp=grp, tp=tp: e.transpose(out=tp[:, 128 * (c - 4 * grp):128 * (c - 4 * grp) + T], in_=h[0:T, C_QKV + 128 * c:C_QKV + 128 * (c + 1)], identity=identf[0:T, 0:T])
                                for c in range(4 * grp, 4 * grp + 4)], reads=[htok, "Kidentf"], writes=[tptok])
                    src = tp[:].rearrange("p (c t) -> p c t", t=128)[:, :, 0:T]
                    if not sample:
                        copy_op(evac_eng(), convb[:, 4 * grp:4 * grp + 4, 3:3 + T], src, [tptok], ["convb"])
                        yield
                    else:
                        dst = convb[:, 4 * grp:4 * grp + 4, 0:112].rearrange("p c (b r) -> p c b r", r=7)[:, :, :, 3:7]
                        copy_op(evac_eng(), dst, src.rearrange("p c (b t) -> p c b t", t=4), [tptok], ["convb"])
                        yield
                if sample:
                    s.dma("sp", lambda e: e.dma_start(out=cacc[0:48, :, :].rearrange("p c t -> p (c t)"), in_=D["sconv"]), writes=["cacc"])
                    yield
                    for grp in range(3):
                        tp, tptok = ps("tp")
                        s.op("pe", [lambda e, c=c, grp=grp, tp=tp: e.transpose(out=tp[:, 128 * (c - 4 * grp):128 * (c - 4 * grp) + 48], in_=cacc[0:48, c, :], identity=identf[0:48, 0:48])
                                    for c in range(4 * grp, 4 * grp + 4)], reads=["cacc", "Kidentf"], writes=[tptok])
                        src = tp[:].rearrange("p (c t) -> p c t", t=128)[:, :, 0:48].rearrange("p c (b r) -> p c b r", r=3)
                        dst = convb[:, 4 * grp:4 * grp + 4, 0:112].rearrange("p c (b r) -> p c b r", r=7)[:, :, :, 0:3]
                        copy_op(evac_eng(), dst, src, [tptok], ["convb"])
                        yield
                elif it == 0:
                    s.op("pool", lambda e: e.memset(convb[:, :, 0:3], 0.0), writes=["convb"])
                    yield

                def shifted(j):
                    if not sample:
                        return convb[:, :, j:j + T]
                    return convb[:, :, 0:112].rearrange("p c (b r) -> p c b r", r=7)[:, :, :, j:j + 4]

                def shp(ap):
                    return ap if not sample else ap.rearrange("p c (b t) -> p c b t", t=4)

                def wj(j):
                    w = wconv[:, :, j:j + 1]
                    return w.to_broadcast([128, 12, T]) if not sample else w.unsqueeze(3).to_broadcast([128, 12, 16, 4])
                s.op("dve", lambda e: e.tensor_tensor(out=shp(cacc[:, :, 0:T]), in0=shifted(0), in1=wj(0), op=ALU.mult), reads=["convb", "wconv"], writes=["cacc"])
                yield
                for j in range(1, 4):
                    eng = "pool" if j % 2 == 1 else "dve"
                    s.op(eng, lambda e, j=j: e.tensor_tensor(out=shp(ctmp[:, :, 0:T]), in0=shifted(j), in1=wj(j), op=ALU.mult), reads=["convb", "wconv"], writes=["ctmp"])
                    yield
                    s.op("dve", lambda e: e.tensor_tensor(out=cacc[:, :, 0:T], in0=cacc[:, :, 0:T], in1=ctmp[:, :, 0:T], op=ALU.add), reads=["cacc", "ctmp"], writes=["cacc"])
                    yield
                if not sample:
                    s.op("pool", lambda e: e.tensor_copy(out=ctmp[:, :, 0:3], in_=convb[:, :, T:T + 3]), reads=["convb"], writes=["ctmp"])
                    yield
                    s.op("pool", lambda e: e.tensor_copy(out=convb[:, :, 0:3], in_=ctmp[:, :, 0:3]), reads=["ctmp"], writes=["convb"])
                    yield
                s.op("act", lambda e: e.activation(out=gact[:, :, 0:T], in_=cacc[:, :, 0:T], func=AF.Silu), reads=["cacc"], writes=["gact"])
                yield
                s.op("pool", lambda e: e.tensor_tensor(out=gsq[:, :, 0:T], in0=gact[:, 0:8, 0:T], in1=gact[:, 0:8, 0:T], op=ALU.mult), reads=["gact"], writes=["gsq"])
                yield
                for half in range(2):
                    pm, pmtok = ps("g")
                    s.op("pe", [lambda e, c=c, pm=pm, half=half: e.matmul(pm[:, 128 * (c - 4 * half):128 * (c - 4 * half) + T], lhsT=K["onesb"][:, :], rhs=gsq[:, c, 0:T], start=True, stop=True) for c in range(4 * half, 4 * half + 4)],
                         reads=["gsq", "Konesb"], writes=[pmtok])
                    src = pm[:].rearrange("p (c t) -> p c t", t=128)[:, :, 0:T]
                    dst = gqk[:, 4 * half:4 * half + 4, 0:T]
                    s.op("dve", lambda e, src=src, dst=dst: e.tensor_scalar(out=dst, in0=src, scalar1=1e-6, scalar2=None, op0=ALU.add), reads=[pmtok], writes=["gqk"])
                    yield
                    s.op("act", lambda e, dst=dst: e.activation(out=dst, in_=dst, func=AF.Sqrt), reads=["gqk"], writes=["gqk"])
                    yield
                    s.op("dve", lambda e, dst=dst: e.reciprocal(out=dst, in_=dst), reads=["gqk"], writes=["gqk"])
                    yield
                    if half == 0:
                        s.op("dve", lambda e, dst=dst: e.scalar_tensor_tensor(out=dst, in0=dst, scalar=float(128 ** -0.5), in1=gact[:, 0:4, 0:T], op0=ALU.mult, op1=ALU.mult), reads=["gqk", "gact"], writes=["gqk"])
                        yield
                    else:
                        s.op("dve", lambda e, dst=dst: e.tensor_tensor(out=dst, in0=dst, in1=gact[:, 4:8, 0:T], op=ALU.mult), reads=["gqk", "gact"], writes=["gqk"])
                        yield
                s.op("act", lambda e: e.copy(out=gqkb[:, :, 0:T], in_=gqk[:, :, 0:T]), reads=["gqk"], writes=["gqkb"])
                yield
                for half in range(2):
                    tp, tptok = ps("tp")
                    srcs = [gqk[:, 4 + c, 0:T] for c in range(4)] if half == 0 else [gact[:, 8 + c, 0:T] for c in range(4)]
                    s.op("pe", [lambda e, c=c, tp=tp, src=src: e.transpose(out=tp[0:T, 128 * c:128 * (c + 1)], in_=src, identity=identf[:, :]) for c, src in enumerate(srcs)],
                         reads=["gqk", "gact", "Kidentf"], writes=[tptok])
                    copy_op(evac_eng(), gtok[0:T, 4 * half:4 * half + 4, :], tp[0:T, :].rearrange("p (c d) -> p c d", d=128), [tptok], ["gtok"])
                    yield
                s.op("act", lambda e: e.activation(out=gsc[0:T, 0:4], in_=h[0:T, C_B:C_B + 4], func=AF.Sigmoid), reads=[htok], writes=["gsc"])
                yield
                s.op("dve", lambda e: e.tensor_tensor(out=gsc[0:T, 4:8], in0=h[0:T, C_A:C_A + 4], in1=dtb_bc[0:T, :], op=ALU.add), reads=[htok, "dtb_bc"], writes=["gsc"])
                yield
                s.op("act", lambda e: e.activation(out=gsc[0:T, 8:12], in_=gsc[0:T, 4:8], func=AF.Abs), reads=["gsc"], writes=["gsc"])
                yield
                s.op("act", lambda e: e.activation(out=gsc[0:T, 8:12], in_=gsc[0:T, 8:12], func=AF.Exp, scale=-1.0), reads=["gsc"], writes=["gsc"])
                yield
                s.op("act", lambda e: e.activation(out=gsc[0:T, 8:12], in_=gsc[0:T, 8:12], func=AF.Ln, bias=1.0), reads=["gsc"], writes=["gsc"])
                yield
                s.op("dve", lambda e: e.scalar_tensor_tensor(out=gsc[0:T, 4:8], in0=gsc[0:T, 4:8], scalar=0.0, in1=gsc[0:T, 8:12], op0=ALU.max, op1=ALU.add), reads=["gsc"], writes=["gsc"])
                yield
                s.op("dve", lambda e: e.tensor_tensor(out=gsc[0:T, 4:8], in0=gsc[0:T, 4:8], in1=nea_bc[0:T, :], op=ALU.mult), reads=["gsc", "nea_bc"], writes=["gsc"])
                yield
                pg, pgtok = ps("g")
                s.op("pe", [lambda e: e.matmul(pg[0:T, 0:4], lhsT=K["tri_" + sfx][0:T, 0:T], rhs=gsc[0:T, 4:8], start=True, stop=True),
                            lambda e: e.matmul(pg[0:T, 4:8], lhsT=K["chk_" + sfx][0:T, 0:T], rhs=gsc[0:T, 4:8], start=True, stop=True)],
                     reads=["gsc", "Ktri_" + sfx, "Kchk_" + sfx], writes=[pgtok])
                s.op("dve", lambda e: e.tensor_copy(out=gsc[0:T, 12:20], in_=pg[0:T, 0:8]), reads=[pgtok], writes=["gsc"])
                s.op("act", lambda e: e.activation(out=gsc[0:T, 20:24], in_=gsc[0:T, 12:16], func=AF.Exp), reads=["gsc"], writes=["gsc"])
                yield
                s.op("dve", lambda e: e.tensor_tensor(out=gsc[0:T, 24:28], in0=gsc[0:T, 16:20], in1=gsc[0:T, 12:16], op=ALU.subtract), reads=["gsc"], writes=["gsc"])
                yield
                s.op("act", lambda e: e.activation(out=gsc[0:T, 24:28], in_=gsc[0:T, 24:28], func=AF.Exp), reads=["gsc"], writes=["gsc"])
                yield
                s.op("dve", lambda e: e.tensor_tensor(out=gsc[0:T, 20:24], in0=gsc[0:T, 20:24], in1=gsc[0:T, 0:4], op=ALU.mult), reads=["gsc"], writes=["gsc"])
                yield
                s.op("dve", lambda e: e.tensor_scalar(out=gsc[0:T, 28:32], in0=gsc[0:T, 0:4], scalar1=-1.0, scalar2=None, op0=ALU.mult), reads=["gsc"], writes=["gsc"])

            gp = prep_gen()
            ctx["gdn_prep"] = gp
            return

        def gdn_tile2(ctx, T, it, h, htok):
            sample = ctx["sample"]
            sfx = "s" if sample else "p"
            nlev = 2 if sample else 6
            for _ in ctx["gdn_prep"]:
                pass
            nchunk = 16 if sample else 2
            if sample:
                set0 = [k_ + "_0" for k_ in "Gb Eup Elo etmp EDrow uf wTf qdT qkT kd vnew GLb Nm0 Nm1 Mm0 Mm1 Xb0 Xb1".split()]
                s.op("pool", lambda e: e.memset(HBBIG[0][:, 0:16], 0.0), writes=["stage2"] + set0)
            banksA = PS["g"][0] + PS["acc"][0]
            banksB = PS["tp"][0] + PS["mm"][0]
            for i_ in range(NSET):
                HB[i_]["bankA"] = banksA[i_]
                HB[i_]["bankB"] = banksB[i_]

            def head_gen(hd, B):
                Gb, Eup, Elo, etmp, EDrow, uf, wTf, qdT, qkT, kd, vnew, GLb, Nm, Mm, Xb = (B[k_] for k_ in "Gb Eup Elo etmp EDrow uf wTf qdT qkT kd vnew GLb Nm Mm Xb".split())
                sx = "_%d" % B["i"]
                s.op("dve", lambda e, hd=hd: e.tensor_scalar(out=Gb[0:T, :], in0=K["onesf"][0:T, :], scalar1=gsc[0:T, 4 + hd:5 + hd], scalar2=None, op0=ALU.mult), reads=["gsc", "Konesf"], writes=["Gb" + sx])
                yield
                pg, pgtok = B["bankA"]
                s.op("pe", [lambda e, pg=pg: e.matmul(pg[:, 0:T], lhsT=Gb[0:T, :], rhs=K["tri_" + sfx][0:T, 0:T], start=True, stop=True),
                            lambda e, pg=pg: e.matmul(pg[:, 128:128 + nchunk], lhsT=Gb[0:T, :], rhs=K["chsel_" + sfx][0:T, :], start=True, stop=True),
                            lambda e, pg=pg, hd=hd: e.matmul(pg[0:T, 256:256 + T], lhsT=gqkb[:, 4 + hd, 0:T], rhs=gqkb[:, 4 + hd, 0:T], start=True, stop=True),
                            lambda e, pg=pg, hd=hd: e.matmul(pg[0:T, 384:384 + T], lhsT=gqkb[:, 4 + hd, 0:T], rhs=gqkb[:, hd, 0:T], start=True, stop=True)],
                     reads=["Gb" + sx, "Ktri_" + sfx, "Kchsel_" + sfx, "gqkb"], writes=[pgtok])
                yield
                s.op("act", lambda e, pg=pg: e.activation(out=GLb[:, 0:nchunk], in_=pg[:, 128:128 + nchunk], func=AF.Exp), reads=[pgtok], writes=["GLb" + sx])
                yield
                s.op("act", lambda e, pg=pg: e.activation(out=EDrow[:, 0:T], in_=pg[:, 0:T], func=AF.Exp), reads=[pgtok], writes=["EDrow" + sx])
                yield
                s.op("dve", lambda e, pg=pg, hd=hd: e.scalar_tensor_tensor(out=etmp[0:T, 0:T], in0=pg[0:T, 0:T], scalar=gsc[0:T, 12 + hd:13 + hd], in1=K["negU_" + sfx][0:T, 0:T], op0=ALU.subtract, op1=ALU.add),
                     reads=[pgtok, "gsc", "KnegU_" + sfx], writes=["etmp" + sx])
                yield
                s.op("act", lambda e: e.activation(out=Eup[0:T, 0:T], in_=etmp[0:T, 0:T], func=AF.Exp), reads=["etmp" + sx], writes=["Eup" + sx])
                yield
                s.op("dve", lambda e, pg=pg, hd=hd: e.scalar_tensor_tensor(out=etmp[0:T, 0:T], in0=pg[0:T, 0:T], scalar=-1.0, in1=K["negL_" + sfx][0:T, 0:T], op0=ALU.mult, op1=ALU.add),
                     reads=[pgtok, "KnegL_" + sfx, "Eup" + sx], writes=["etmp" + sx])
                yield
                s.op("act", lambda e, hd=hd: e.activation(out=Elo[0:T, 0:T], in_=etmp[0:T, 0:T], func=AF.Exp, bias=gsc[0:T, 12 + hd:13 + hd]), reads=["etmp" + sx, "gsc"], writes=["Elo" + sx])
                yield
                s.op("pool", lambda e: e.tensor_tensor(out=Elo[0:T, 0:T], in0=Elo[0:T, 0:T], in1=K["strictL_" + sfx][0:T, 0:T], op=ALU.mult), reads=["Elo" + sx, "KstrictL_" + sfx], writes=["Elo" + sx])
                yield
                s.op("dve", lambda e, pg=pg, hd=hd: e.scalar_tensor_tensor(out=Nm[0][0:T, 0:T], in0=pg[0:T, 256:256 + T], scalar=gsc[0:T, 28 + hd:29 + hd], in1=Elo[0:T, 0:T], op0=ALU.mult, op1=ALU.mult),
                     reads=[pgtok, "gsc", "Elo" + sx], writes=["Nm0" + sx])
                yield
                s.op("dve", lambda e, pg=pg: e.tensor_tensor(out=qkT[0:T, 0:T], in0=pg[0:T, 384:384 + T], in1=Eup[0:T, 0:T], op=ALU.mult), reads=[pgtok, "Eup" + sx], writes=["qkT" + sx])
                yield
                s.op("pool", lambda e, hd=hd: e.tensor_tensor(out=qdT[:, 0:T], in0=gqk[:, hd, 0:T], in1=EDrow[:, 0:T], op=ALU.mult), reads=["gqk", "EDrow" + sx], writes=["qdT" + sx])
                yield
                s.op("dve", lambda e, hd=hd: e.tensor_scalar(out=Xb[0][0:T, 0:128], in0=gtok[0:T, 4 + hd, :], scalar1=gsc[0:T, hd:hd + 1], scalar2=None, op0=ALU.mult), reads=["gtok", "gsc"], writes=["Xb0" + sx])
                yield
                s.op("dve", lambda e, hd=hd: e.tensor_scalar(out=Xb[0][0:T, 128:256], in0=gtok[0:T, hd, :], scalar1=gsc[0:T, 20 + hd:21 + hd], scalar2=None, op0=ALU.mult), reads=["gtok", "gsc"], writes=["Xb0" + sx])
                yield
                s.op("pool", lambda e, hd=hd: e.tensor_scalar(out=kd[0:T, :], in0=gtok[0:T, hd, :], scalar1=gsc[0:T, 24 + hd:25 + hd], scalar2=None, op0=ALU.mult), reads=["gtok", "gsc"], writes=["kd" + sx])
                yield
                tp, tptok = B["bankB"]
                tpb = tp[:].bitcast(BF16)
                s.op("pe", lambda e, tpb=tpb: e.transpose(out=tpb[0:T, 0:T], in_=Nm[0][0:T, 0:T], identity=identb[0:T, 0:T]), reads=["Nm0" + sx, "Kidentb"], writes=[tptok])
                yield
                copy_op("act", Mm[0][0:T, 0:T], tpb[0:T, 0:T], [tptok], ["Mm0" + sx])
                yield
                cur = 0
                for lev in range(nlev):
                    pg2, pg2tok = B["bankA"]
                    fl = [lambda e, pg2=pg2, cur=cur: e.matmul(pg2[0:T, 0:256], lhsT=Mm[cur][0:T, 0:T], rhs=Xb[cur][0:T, :], start=True, stop=True)]
                    lastlev = lev == nlev - 1
                    if not lastlev:
                        fl.append(lambda e, pg2=pg2, cur=cur: e.matmul(pg2[0:T, 256:256 + T], lhsT=Nm[cur][0:T, 0:T], rhs=Mm[cur][0:T, 0:T], start=True, stop=True))
                        if lev < nlev - 2:
                            fl.append(lambda e, pg2=pg2, cur=cur: e.matmul(pg2[0:T, 384:384 + T], lhsT=Mm[cur][0:T, 0:T], rhs=Nm[cur][0:T, 0:T], start=True, stop=True))
                    s.op("pe", fl, reads=["Mm%d" % cur + sx, "Nm%d" % cur + sx, "Xb%d" % cur + sx], writes=[pg2tok])
                    yield
                    nxt = cur ^ 1
                    if not lastlev:
                        s.op("dve", lambda e, pg2=pg2, cur=cur, nxt=nxt: e.tensor_tensor(out=Xb[nxt][0:T, :], in0=pg2[0:T, 0:256], in1=Xb[cur][0:T, :], op=ALU.add), reads=[pg2tok, "Xb%d" % cur + sx], writes=["Xb%d" % nxt + sx])
                        yield
                        copy_op("act", Mm[nxt][0:T, 0:T], pg2[0:T, 256:256 + T], [pg2tok], ["Mm%d" % nxt + sx])
                        yield
                        if lev < nlev - 2:
                            copy_op("pool" if False else "dve", Nm[nxt][0:T, 0:T], pg2[0:T, 384:384 + T], [pg2tok], ["Nm%d" % nxt + sx])
                            yield
                    else:
                        s.op("dve", lambda e, pg2=pg2, cur=cur: e.tensor_tensor(out=uf[0:T, :], in0=pg2[0:T, 0:128], in1=Xb[cur][0:T, 0:128], op=ALU.add), reads=[pg2tok, "Xb%d" % cur + sx], writes=["uf" + sx])
                        yield
                        s.op("dve", lambda e, pg2=pg2, cur=cur: e.tensor_tensor(out=etmp[0:T, :], in0=pg2[0:T, 128:256], in1=Xb[cur][0:T, 128:256], op=ALU.add), reads=[pg2tok, "Xb%d" % cur + sx, "Eup" + sx, "Elo" + sx], writes=["etmp" + sx])
                        yield
                    cur = nxt
                tp, tptok = B["bankB"]
                s.op("pe", lambda e, tp=tp: e.transpose(out=tp[:, 0:T], in_=etmp[0:T, :], identity=identf[0:T, 0:T]), reads=["etmp" + sx, "Kidentf"], writes=[tptok])
                yield
                copy_op("act", wTf[:, 0:T], tp[:, 0:T], [tptok], ["wTf" + sx])
                yield
                Sh, Stok = Sst[hd], "S%d" % hd
                if not sample:
                    if it == 0:
                        s.op("pool", lambda e, Sh=Sh: e.memset(Sh[:], 0.0), writes=[Stok])
                    for j in range(2):
                        R = slice(64 * j, 64 * j + 64)
                        pg3, pg3tok = B["bankA"]
                        s.op("pe", [lambda e, pg3=pg3, R=R, Sh=Sh: e.matmul(pg3[R, 0:128], lhsT=wTf[:, R], rhs=Sh[:, :], start=True, stop=True),
                                    lambda e, pg3=pg3, R=R, Sh=Sh: e.matmul(pg3[R, 128:256], lhsT=qdT[:, R], rhs=Sh[:, :], start=True, stop=True)],
                             reads=["wTf" + sx, "qdT" + sx, Stok], writes=[pg3tok])
                        yield
                        s.op("dve", lambda e, pg3=pg3, R=R: e.tensor_tensor(out=vnew[R, :], in0=uf[R, :], in1=pg3[R, 0:128], op=ALU.subtract), reads=[pg3tok, "uf" + sx], writes=["vnew" + sx])
                        yield
                        copy_op("act", ogdn[R, hd, :], pg3[R, 128:256], [pg3tok], ["ogdn%d" % hd])
                        yield
                        s.op("pe", [lambda e, pg3=pg3, R=R: e.matmul(pg3[R, 384:512], lhsT=qkT[R, R], rhs=vnew[R, :], start=True, stop=True),
                                    lambda e, pg3=pg3, R=R: e.matmul(pg3[:, 256:384], lhsT=kd[R, :], rhs=vnew[R, :], start=True, stop=True)],
                             reads=["qkT" + sx, "vnew" + sx, "kd" + sx], writes=[pg3tok])
                        yield
                        s.op("dve", lambda e, pg3=pg3, R=R, hd=hd: e.tensor_tensor(out=ogdn[R, hd, :], in0=ogdn[R, hd, :], in1=pg3[R, 384:512], op=ALU.add), reads=[pg3tok, "ogdn%d" % hd], writes=["ogdn%d" % hd])
                        yield
                        s.op("dve", lambda e, pg3=pg3, Sh=Sh, j=j: e.scalar_tensor_tensor(out=Sh[:, :], in0=Sh[:, :], scalar=GLb[:, j:j + 1], in1=pg3[:, 256:384], op0=ALU.mult, op1=ALU.add),
                             reads=[pg3tok, Stok, "GLb" + sx], writes=[Stok])
                        yield
                    if it == ctx["ntiles"] - 1:
                        s.dma("sp", lambda e, hd=hd, Sh=Sh: e.dma_start(out=D["gdn_p"][hd], in_=Sh[:, :]), reads=[Stok])
                else:
                    yield from gdn_sample_scan(T, hd, B)

            hb_free = list(range(NSET))
            for h0_ in range(0, 4, NSET):
                gens = [head_gen(hd, HB[hd - h0_]) for hd in range(h0_, min(4, h0_ + NSET))]
                STAG = int(os.environ.get("STAG", "0"))
                rnd = 0
                started = {id(g_): k_ * STAG for k_, g_ in enumerate(gens)}
                while gens:
                    for g_ in gens[:]:
                        if rnd < started[id(g_)]:
                            continue
                        try:
                            next(g_)
                        except StopIteration:
                            gens.remove(g_)
                    rnd += 1
            s.op("pool", lambda e: e.tensor_tensor(out=otmp[0:T, :, :], in0=ogdn[0:T, :, :], in1=ogdn[0:T, :, :], op=ALU.mult), reads=["ogdn0", "ogdn1", "ogdn2", "ogdn3"], writes=["otmp"])
            s.op("dve", lambda e: e.tensor_reduce(out=small[0:T, 8:12], in_=otmp[0:T, :, :], axis=AX.X, op=ALU.add), reads=["otmp"], writes=["small"])
            s.op("dve", lambda e: e.tensor_scalar(out=small[0:T, 8:12], in0=small[0:T, 8:12], scalar1=1.0 / 128, scalar2=1e-6, op0=ALU.mult, op1=ALU.add), reads=["small"], writes=["small"])
            s.op("act", lambda e: e.activation(out=small[0:T, 8:12], in_=small[0:T, 8:12], func=AF.Sqrt), reads=["small"], writes=["small"])
            s.op("dve", lambda e: e.reciprocal(out=small[0:T, 8:12], in_=small[0:T, 8:12]), reads=["small"], writes=["small"])
            s.op("dve", lambda e: e.tensor_tensor(out=otmp[0:T, :, :], in0=ogdn[0:T, :, :], in1=small[0:T, 8:12].unsqueeze(2).to_broadcast([T, 4, 128]), op=ALU.mult), reads=["ogdn0", "ogdn1", "ogdn2", "ogdn3", "small"], writes=["otmp"])
            s.op("pool", lambda e: e.tensor_tensor(out=otmp[0:T, :, :], in0=otmp[0:T, :, :], in1=wg_bc[0:T, :].unsqueeze(1).to_broadcast([T, 4, 128]), op=ALU.mult), reads=["otmp", "wg_bc"], writes=["otmp"])
            s.op("dve", lambda e: e.tensor_tensor(out=mixb[0:T, 512:1024], in0=otmp[0:T, :, :].rearrange("p a d -> p (a d)"), in1=zsil[0:T, 1, :], op=ALU.mult), reads=["otmp", "zsil"], writes=["mixb"])

        sample_env = {}

        def gdn_sample_scan(T, hd, B):
            uf, wTf, qdT, qkT, kd, vnew, GLb = (B[k_] for k_ in "uf wTf qdT qkT kd vnew GLb".split())
            sx = "_%d" % B["i"]
            mask = K["chk_s"]
            lm = [(B["Gb"], "Gb" + sx), (B["etmp"], "etmp" + sx)]
            stg = [(B["Elo"], "Elo" + sx), (B["EDrow"], "EDrow" + sx)]
            ohi, ohitok = B["Eup"], "Eup" + sx
            pgA, pgAtok = B["bankA"]
            for b in range(NB):
                St, Sttok = stg[b % 2]
                mk, mktok = lm[b % 2]
                s.dma("sp", lambda e, b=b, St=St: e.dma_start(out=St[:, :], in_=D["sgdn"][b, hd]), writes=[Sttok])
                s.op("pool", lambda e, mk=mk: e.memset(mk[:, :], 0.0), writes=[mktok])
                s.op("pool", lambda e, b=b, mk=mk: e.tensor_copy(out=mk[:, 4 * b:4 * b + 4], in_=wTf[:, 4 * b:4 * b + 4]), reads=["wTf" + sx], writes=[mktok])
                s.op("pool", lambda e, b=b, mk=mk: e.tensor_copy(out=mk[:, 64 + 4 * b:64 + 4 * b + 4], in_=qdT[:, 4 * b:4 * b + 4]), reads=["qdT" + sx], writes=[mktok])
                s.op("pe", lambda e, b=b, St=St, mk=mk: e.matmul(pgA[:, 0:128], lhsT=mk[:, :], rhs=St[:, :], start=(b == 0), stop=(b == NB - 1)), reads=[mktok, Sttok], writes=[pgAtok])
                yield
            s.op("dve", lambda e: e.tensor_tensor(out=vnew[0:64, :], in0=uf[0:64, :], in1=pgA[0:64, 0:128], op=ALU.subtract), reads=[pgAtok, "uf" + sx], writes=["vnew" + sx])
            copy_op("act", ohi[64:128, :], pgA[64:128, 0:128], [pgAtok], [ohitok])
            s.op("pe", lambda e: e.matmul(pgA[64:128, 128:256], lhsT=qkT[0:64, 0:64], rhs=vnew[0:64, :], start=True, stop=True), reads=["qkT" + sx, "vnew" + sx], writes=[pgAtok])
            s.op("dve", lambda e: e.tensor_tensor(out=ohi[64:128, :], in0=ohi[64:128, :], in1=pgA[64:128, 128:256], op=ALU.add), reads=[pgAtok, ohitok], writes=[ohitok])
            s.dma("sp", lambda e, hd=hd: e.dma_start(out=ogdn[0:64, hd, :], in_=ohi[64:128, :]), reads=[ohitok], writes=["ogdn%d" % hd])
            yield
            pgB, pgBtok = B["bankB"]
            for b in range(NB):
                St, Sttok = stg[b % 2]
                km, kmtok = lm[b % 2]
                s.dma("sp", lambda e, b=b, St=St: e.dma_start(out=St[:, :], in_=D["sgdn"][b, hd]), writes=[Sttok])
                s.op("pool", lambda e, b=b, km=km: e.tensor_scalar(out=km[0:64, :], in0=kd[0:64, :], scalar1=mask[0:64, 4 * b:4 * b + 1], scalar2=None, op0=ALU.mult), reads=["kd" + sx, "Kchk_s"], writes=[kmtok])
                s.op("pe", lambda e, km=km: e.matmul(pgB[:, 0:128], lhsT=km[0:64, :], rhs=vnew[0:64, :], start=True, stop=True), reads=[kmtok, "vnew" + sx], writes=[pgBtok])
                So, Sotok = sample_env["Sout"][b % 2], "Sout%d" % (b % 2)
                s.op("dve", lambda e, St=St, So=So, b=b: e.scalar_tensor_tensor(out=So[:, :], in0=St[:, :], scalar=GLb[:, b:b + 1], in1=pgB[:, 0:128], op0=ALU.mult, op1=ALU.add),
                     reads=[pgBtok, Sttok, "GLb" + sx], writes=[Sotok])
                s.dma("sp", lambda e, b=b, So=So, hd=hd: e.dma_start(out=D["gdn_s"][b, hd], in_=So[:, :]), reads=[Sotok])
                yield

        def nsa_sample(ctx, T):
            env = sample_env
            kTn, skTt, v1s, idx = env["kTn"], env["skTt"], env["v1s"], env["idx"]
            stages = [(env["stage"], "stage"), (HBBIG[0][:, 0:2048], "stage2")]
            s.op("pool", lambda e: e.memset(ET[:], 0.0), writes=["ET0", "ET1", "kT_all", "v1_all", "wstage0", "wstage1", "stage", "stage2", "kTn", "skTt", "wkTs", "v1s", "wv1s", "part"] + [k_ + "_0" for k_ in "Gb Eup Elo etmp EDrow uf wTf qdT qkT kd vnew GLb Nm0 Nm1 Mm0 Mm1 Xb0 Xb1".split()])
            s.op("pool", lambda e: e.memset(v1_all[:], 1.0), reads=["ET0"], writes=["v1s", "wv1s", "v1_all"])
            for b in range(NB):
                cnames = ("cck", "ccv", "csk", "csv")

                def issue_gather(bb, ci):
                    stg_, stok_ = stages[ci % 2]
                    s.dma("pool", lambda e: e.indirect_dma_start(out=stg_[:, :], out_offset=None, in_=D[cnames[ci]], in_offset=bass.IndirectOffsetOnAxis(ap=idx[:, bb:bb + 1], axis=0)),
                          reads=["idx"], writes=[stok_])
                if b == 0:
                    issue_gather(0, 0)
                    issue_gather(0, 1)
                for ci in range(4):
                    stg_, stok_ = stages[ci % 2]
                    if ci < 3:
                        for q4 in range(4):
                            tp, tptok = ps("tp")
                            s.op("pe", [lambda e, r=r: e.transpose(out=tp[:, 128 * (r - 4 * q4):128 * (r - 4 * q4 + 1)], in_=stg_[:, 128 * r:128 * (r + 1)], identity=identf[:, :]) for r in range(4 * q4, 4 * q4 + 4)],
                                 reads=[stok_, "Kidentf"], writes=[tptok])
                            src = tp[:].rearrange("p (r m) -> p r m", m=128)
                            if ci < 2:
                                dst = kTn[:, ci, :].rearrange("p (m r) -> p r m", r=16)[:, 4 * q4:4 * q4 + 4, :]
                                copy_op(evac_eng(), dst, src, [tptok], ["kTn"])
                            else:
                                copy_op(evac_eng(), skTt[:, 4 * q4:4 * q4 + 4, :], src, [tptok], ["skTt"])
                    else:
                        s.op("pool", lambda e: e.tensor_copy(out=v1s[:, :, :, 0:64], in_=stg_[:, :].rearrange("p (r h d) -> p r h d", r=16, h=2)), reads=[stok_], writes=["v1s"])
                    if ci < 2:
                        issue_gather(b, ci + 2)
                    elif b + 1 < NB:
                        issue_gather(b + 1, ci - 2)
                wst, wkTs, wv1s = env["ctmp"], env["wkTs"], env["wv1s"]
                s.dma("sp", [lambda e, b=b: e.dma_start(out=wst[:, 0, :, :], in_=D["cwk"][b].rearrange("(a p) f -> p a f", p=128)),
                             lambda e, b=b: e.dma_start(out=wst[:, 1, :, :], in_=D["cwv"][b].rearrange("(a p) f -> p a f", p=128))], writes=["ctmp"])
                s.dma("sp", [lambda e, b=b: e.dma_start(out=D["wk_s"][b, 0:508, :], in_=D["cwk"][b, 4:512, :]),
                             lambda e, b=b: e.dma_start(out=D["wv_s"][b, 0:508, :], in_=D["cwv"][b, 4:512, :])])
                tp, tptok = ps("tp")
                s.op("pe", [lambda e, a=a, tp=tp: e.transpose(out=tp[:, 128 * a:128 * (a + 1)], in_=wst[:, 0, a, :], identity=identf[:, :]) for a in range(4)], reads=["ctmp", "Kidentf"], writes=[tptok])
                copy_op(evac_eng(), wkTs[:, :, :], tp[:].rearrange("p (a m) -> p a m", m=128), [tptok], ["wkTs"])
                s.op("pool", lambda e: e.tensor_copy(out=wv1s[:, :, :, 0:64], in_=wst[:, 1, :, :].rearrange("p a (h d) -> p a h d", h=2)), reads=["ctmp"], writes=["wv1s"])
                compress(kTn[:, 0, :], kTn[:, 1, :], "kTn", 0, 127)
                cols = slice(4 * b, 4 * b + 4)
                for hh in range(2):
                    def smp_tile(qw, kT_ap, nk, v1_ap, masks, rd, rdv, first, last):
                        return dict(kT=kT_ap, nk=nk, q=qT[64 * hh:64 * hh + 64, qw, :, cols], masks=masks, v1=v1_ap, rd=["qT"] + rd, rdv=rdv, ncol=16,
                                    et_out=lambda eb, nk=nk: ET[0:nk, eb, :, cols], sc_in=lambda pm, nk=nk: pm[0:nk, 0:16].rearrange("p (g t) -> p g t", g=4),
                                    first=first, last=last)
                    for bi, br in ((0, "cmp"), (2, "win"), (1, "slc")):
                        if br == "cmp":
                            tiles = [smp_tile(0, cckT[64 * hh:64 * hh + 64, 0:127], 127, ccv1[0:127, hh, :], [], ["cckT"], ["ccv1"], True, True)]
                            W = 98
                        elif br == "slc":
                            tiles = []
                            selr = selT[:, hh, cols].unsqueeze(1).to_broadcast([128, 4, 4])
                            for r in range(16):
                                tiles.append(smp_tile(1, skTt[64 * hh:64 * hh + 64, r, :], 128, v1s[:, r, hh, :], [(K["expand_s"][:, :], selr)], ["skTt", "selT", "Kexpand_s"], ["v1s"], True, True))
                            seln = selT[:, hh, cols].unsqueeze(1).to_broadcast([128, 4, 4])
                            tiles.append(smp_tile(1, env["kT_new"][64 * hh:64 * hh + 64, 2, :], 64, env["v1_new"][:, 0, hh, :],
                                                  [(K["expand_n"][:, :], seln), (identb[:, 0:64], K["newmask_s"][:, b, :, :])], ["kT_new", "selT", "Kexpand_n", "Knewmask_s", "Kidentb"], ["v1_new"], True, True))
                            W = 65
                        else:
                            tiles = []
                            for a in range(4):
                                masks = [(identb[:, :], K["winmask_s"][:, :, :])] if a == 0 else []
                                tiles.append(smp_tile(1, wkTs[64 * hh:64 * hh + 64, a, :], 128, wv1s[:, a, hh, :], masks, ["wkTs", "Kidentb", "Kwinmask_s"], ["wv1s"], True, True))
                            tiles.append(smp_tile(1, env["kT_new"][64 * hh:64 * hh + 64, 3, :], 64, env["v1_new"][:, 1, hh, :],
                                                  [(identb[:, 0:64], K["newmask_s"][:, b, :, :])], ["kT_new", "Knewmask_s", "Kidentb"], ["v1_new"], True, True))
                            W = 65
                        acc, accv, acctok = nsa_core(T, hh, 0, tiles, W, [], br, True, 0)
                        pv = env["partv"](bi, hh)
                        if b == 0:
                            s.op("dve", lambda e, pv=pv, accv=accv: e.tensor_copy(out=pv, in_=accv), reads=[acctok], writes=["part"])
                        else:
                            s.op("dve", lambda e, pv=pv, accv=accv: e.tensor_tensor(out=pv, in0=pv, in1=accv, op=ALU.add), reads=[acctok, "part"], writes=["part"])
                        s.op("pool", lambda e, cols=cols: e.memset(ET[:, :, :, cols], 0.0), reads=[], writes=["ET0", "ET1"])
                        if br == "cmp":
                            if True:
                                pacc = env["partv"](0, hh)
                                s.op("dve", lambda e, pacc=pacc: e.tensor_scalar(out=rz[0:T, 0:4], in0=pacc[:, :, 64], scalar1=1e-30, scalar2=None, op0=ALU.max), reads=["part"], writes=["rz"])
                                s.op("dve", lambda e: e.reciprocal(out=rz[0:T, 0:4], in_=rz[0:T, 0:4]), reads=["rz"], writes=["rz"])
                                iv = imp[0:T, hh, 0:33]
                                s.op("dve", lambda e, pacc=pacc, iv=iv: e.tensor_scalar(out=iv, in0=pacc[:, 0, 65:98], scalar1=rz[0:T, 0:1], scalar2=None, op0=ALU.mult), reads=["part", "rz"], writes=["imp"])
                                for g in range(1, 4):
                                    s.op("dve", lambda e, g=g, pacc=pacc, iv=iv: e.scalar_tensor_tensor(out=iv, in0=pacc[:, g, 65:98], scalar=rz[0:T, g:g + 1], in1=iv, op0=ALU.mult, op1=ALU.add), reads=["part", "rz", "imp"], writes=["imp"])
                                select_blocks(T, hh, 0, "s", 33)
            for hh in range(2):
                for bi in range(3):
                    W = 98 if bi == 0 else 65
                    pacc = env["partv"](bi, hh)
                    branch_epilogue(T, hh, pacc, "part", W, 8 * bi, bi == 0)

        if do_prompt and stage >= 1:
            for it in range(nt_prompt):
                ctx = dict(T=128, it=it, slot=it % 2, sample=False, row0=128 * it, ntiles=nt_prompt,
                           x_src=D["xp"][128 * it:128 * (it + 1), :], y_dst=D["y_p"][128 * it:128 * (it + 1), :],
                           kT_dst=kT_all[:, :, 128 * it:128 * (it + 1)], kT_tok="kT_all",
                           v1_dst=v1_all[:, :, it, :, :], v1_tok="v1_all")
                run_tile(ctx)

        if do_sample:
            env = sample_env
            wsf0 = wstage[0][:].rearrange("p k n -> p (k n)").bitcast(F32)
            wsf1 = wstage[1][:].rearrange("p k n -> p (k n)").bitcast(F32)
            env["stage"] = wsf1
            env["kTn"] = kT_all[:, 0:2, :]
            env["skTt"] = kT_all[:, 2, :].rearrange("p (r m) -> p r m", m=128)
            env["wkTs"] = kT_all[:, 3, 0:512].rearrange("p (a m) -> p a m", m=128)
            env["v1s"] = v1_all[:, 0, :, :, :]
            env["wv1s"] = v1_all[:, 1, 0:4, :, :]
            env["ctmp"] = ctmp[:, 0:8, :].rearrange("p (w a) f -> p w a f", w=2)
            pc_ = wsf0[0:64, 0:784].rearrange("p (h g w) -> p h g w", h=2, g=4)
            ps_ = wsf0[0:64, 784:1824].rearrange("p (b h g w) -> p b h g w", b=2, h=2, g=4)
            env["partv"] = lambda bi, hh: (pc_[:, hh, :, :] if bi == 0 else ps_[:, bi - 1, hh, :, :])
            env["kT_new"] = sb("kT_new", [128, 4, 64], BF16)
            env["v1_new"] = sb("v1_new", [64, 2, 2, 65], BF16)
            env["idx"] = sb("idx", [128, 16], I32)
            env["Sstage"] = [sb("Sstage0", [128, 128]), sb("Sstage1", [128, 128])]
            env["Sout"] = [sb("Sout0", [128, 128]), sb("Sout1", [128, 128])]
            env["lmask"] = sb("lmask", [128, 128])
            env["kmask"] = sb("kmask", [64, 128])
            env["ohi"] = sb("ohi", [128, 128])
            idxf = sb("idxf", [128, 16])
            ptl_sb = sb("ptl_sb", [128, 16], I32)
            s.dma("sp", lambda e: e.dma_start(out=ptl_sb[:], in_=D["ptl"]), writes=["ptl_sb"])
            s.op("dve", lambda e: e.tensor_copy(out=idxf[:], in_=ptl_sb[:]), reads=["ptl_sb"], writes=["idxf"])
            s.op("dve", lambda e: e.tensor_scalar(out=idxf[:], in0=idxf[:], scalar1=8.0, scalar2=K["rgcol"][:, 0:1], op0=ALU.mult, op1=ALU.add), reads=["idxf", "Krgcol"], writes=["idxf"])
            s.op("dve", lambda e: e.tensor_copy(out=env["idx"][:], in_=idxf[:]), reads=["idxf"], writes=["idx"])
            s.op("pool", lambda e: e.memset(env["v1_new"][:], 1.0), writes=["v1_new"])
            ctx = dict(T=TS, it=0, slot=0, sample=True, row0=0, ntiles=1, ws_extra=["part", "stage"],
                       x_src=D["xs"][:, :], y_dst=D["y_s"][:, :],
                       kT_dst=env["kT_new"][:, :, :], kT_tok="kT_new",
                       v1_dst=env["v1_new"][:, :, :, :], v1_tok="v1_new")
            run_tile(ctx)

        s.finish()
        s.emit()
        if os.environ.get('DBG_SBUF'):
            print('SBUF remaining', nc.sbuf_bytes_remaining, 'ops', s.nops, {e: s.cnt[e] for e in s.ENGS})
    return nc, s


_CACHE = {}
CORES = list(range(NCORES))
BUILD_KW = {}
TRACE = False
LAST = {}


def kernel(x_prompt, x_sample, cache_cmp_k, cache_cmp_v, cache_slc_k, cache_slc_v, cache_win_k, cache_win_v,
           state_conv, state_gdn, page_table, w_norm, w_in, pe_cmp_k, w_cmp_k1, w_cmp_k2, pe_cmp_v, w_cmp_v1,
           w_cmp_v2, w_conv, a_log, dt_bias, w_gdn_norm, w_out, w_final_norm):
    f = lambda a: np.ascontiguousarray(np.asarray(a, dtype=np.float32))
    consts = make_consts()
    if "nc" not in _CACHE:
        _CACHE["nc"] = build_program(consts, **BUILD_KW)[0]
    nc = _CACHE["nc"]
    shared = {
        "cck": f(cache_cmp_k).reshape(2560 * 8, 2048), "ccv": f(cache_cmp_v).reshape(2560 * 8, 2048),
        "csk": f(cache_slc_k).reshape(2560 * 8, 2048), "csv": f(cache_slc_v).reshape(2560 * 8, 2048),
        "w_norm": f(w_norm).reshape(1, 1024), "w_in": f(w_in).reshape(1024, NIN),
        "pe_k": f(pe_cmp_k).reshape(32, 64), "w1k": f(w_cmp_k1).reshape(2048, 128), "w2k": f(w_cmp_k2).reshape(128, 64),
        "pe_v": f(pe_cmp_v).reshape(32, 64), "w1v": f(w_cmp_v1).reshape(2048, 128), "w2v": f(w_cmp_v2).reshape(128, 64),
        "w_conv": f(w_conv).reshape(4, 1536), "a_log": f(a_log).reshape(1, 4), "dt_bias": f(dt_bias).reshape(1, 4),
        "wg": f(w_gdn_norm).reshape(1, 128), "w_out": f(w_out).reshape(1024, 1024), "w_fn": f(w_final_norm).reshape(1, 1024),
    }
    for k, v in consts.items():
        shared["k_" + k] = v
    xp = f(x_prompt); xs = f(x_sample)
    cwk = f(cache_win_k).reshape(128, 512, 128); cwv = f(cache_win_v).reshape(128, 512, 128)
    sconv = f(state_conv).reshape(128, 3, 1536); sgdn = f(state_gdn).reshape(128, 4, 128, 128)
    pt = np.asarray(page_table, dtype=np.int32)
    in_maps = []
    for c in CORES:
        bs = slice(16 * c, 16 * (c + 1))
        m = dict(shared)
        m["xp"] = xp[c]
        m["xs"] = xs[bs].reshape(64, 1024)
        m["cwk"] = cwk[bs]; m["cwv"] = cwv[bs]
        m["sconv"] = sconv[bs].reshape(48, 1536)
        m["sgdn"] = sgdn[bs]
        m["ptl"] = np.ascontiguousarray(np.repeat(pt[bs].T, 8, axis=0)).astype(np.int32)
        in_maps.append(m)
    res = run_bass_kernel_spmd(nc, in_maps, core_ids=list(range(len(CORES))), **({'trace': True} if TRACE else {}))
    LAST['res'] = res
    R = res.results
    cat = lambda k: np.concatenate([r[k][None] for r in R], 0)
    y_p = cat("y_p")
    y_s = cat("y_s").reshape(-1, 4, 1024)
    outs = [y_p, y_s]
    for nm in ("ck", "cv", "sk", "sv"):
        outs.append(cat(nm + "_p").reshape(1, -1, 2048, 2, 64))
        outs.append(cat(nm + "_s").reshape(1, -1, 4, 2, 64))
    for nm in ("wk", "wv"):
        outs.append(cat(nm + "_p").reshape(1, -1, 512, 2, 64))
        outs.append(cat(nm + "_s").reshape(1, -1, 512, 2, 64))
    outs.append(cat("conv_p").reshape(1, -1, 3, 1536))
    outs.append(cat("conv_s").reshape(1, -1, 3, 1536))
    outs.append(cat("gdn_p").reshape(1, -1, 4, 128, 128))
    outs.append(cat("gdn_s").reshape(1, -1, 4, 128, 128))
    return tuple(np.ascontiguousarray(o, dtype=np.float32) for o in outs)
```

```python
import os
import numpy as np
import ml_dtypes
import concourse.bass as bass
import concourse.mybir as mybir
from concourse.bass_utils import run_bass_kernel_spmd
from contextlib import ExitStack

F32 = mybir.dt.float32
BF16 = mybir.dt.bfloat16
I32 = mybir.dt.int32
ALU = mybir.AluOpType
AF = mybir.ActivationFunctionType
AX = mybir.AxisListType
NEG = -1.0e30
NCORES = 8
NT = 16
TS = 64
NB = 16
NIN = 3872
C_GATE, C_ZN, C_QKV, C_B, C_A, C_ZG = 1280, 1304, 1816, 3352, 3356, 3360


class Sch:
    ENGS = ("pe", "act", "dve", "pool", "sp")
    NRING = 8

    def __init__(self, nc, es):
        self.nc = nc
        self.prog = {e: [] for e in self.ENGS}
        self.sems = {}
        for e in self.ENGS:
            self.sems[e] = es.enter_context(nc.semaphore("c_" + e))
        self.cnt = {e: 0 for e in self.ENGS}
        self.waited = {e: {} for e in self.ENGS}
        self.ring, self.ring_val, self.ring_i = {}, {}, {}
        for q in ("sp", "act", "pool"):
            self.ring[q] = [es.enter_context(nc.semaphore(f"d_{q}{i}")) for i in range(self.NRING)]
            for i in range(self.NRING):
                self.sems[("d", q, i)] = self.ring[q][i]
            self.ring_val[q] = [0] * self.NRING
            self.ring_i[q] = 0
        self.lastw, self.readers = {}, {}
        self.nops = 0
        self.eo = {"pe": nc.tensor, "act": nc.scalar, "dve": nc.vector, "pool": nc.gpsimd, "sp": nc.sync}

    def _deps(self, reads, writes):
        deps = {}

        def add(k, v):
            if deps.get(k, 0) < v:
                deps[k] = v
        for t in reads:
            if t in self.lastw:
                add(*self.lastw[t])
        for t in writes:
            if t in self.lastw:
                add(*self.lastw[t])
            for k, v in self.readers.get(t, {}).items():
                add(k, v)
        return deps

    def _emit_waits(self, eng, deps, same_engine=True):
        for k, v in deps.items():
            if k == eng and not same_engine:
                continue
            if self.waited[eng].get(k, 0) >= v:
                continue
            self.waited[eng][k] = v
            sem = self.sems[k]
            self.eo[eng].wait_ge(sem, v)

    def _record(self, ev, reads, writes):
        for t in writes:
            self.lastw[t] = ev
            self.readers[t] = {}
        for t in reads:
            if t in writes:
                continue
            r = self.readers.setdefault(t, {})
            if r.get(ev[0], 0) < ev[1]:
                r[ev[0]] = ev[1]

    def op(self, eng, fns, reads=(), writes=()):
        if callable(fns):
            fns = [fns]
        writes = list(writes) + [t for t in reads if isinstance(t, str) and t.startswith("ps_")]
        reads = [t for t in reads if not (isinstance(t, str) and t.startswith("ps_"))]
        deps = self._deps(reads, writes)
        self._emit_waits(eng, deps, same_engine=(eng != "pe") and not os.environ.get("NOSAME"))
        self.cnt[eng] += 1
        sem = self.sems[eng]
        n = len(fns)
        for i, f in enumerate(fns):
            if i == n - 1:
                f(self.eo[eng]).then_inc(sem, 1)
            else:
                f(self.eo[eng])
        self.nops += n
        ev = (eng, self.cnt[eng])
        self._record(ev, reads, writes)
        return ev

    def dma(self, q, fns, reads=(), writes=()):
        if callable(fns):
            fns = [fns]
        i = self.ring_i[q]
        self.ring_i[q] = (i + 1) % self.NRING
        key = ("d", q, i)
        deps = self._deps(reads, writes)
        if self.ring_val[q][i] > 0:
            deps[key] = max(deps.get(key, 0), self.ring_val[q][i])
        self._emit_waits(q, deps)
        sem = self.ring[q][i]
        for f in fns:
            f(self.eo[q]).then_inc(sem, 16)
        self.ring_val[q][i] += 16 * len(fns)
        self.nops += len(fns)
        ev = (key, self.ring_val[q][i])
        self._record(ev, reads, writes)
        return ev

    def finish(self):
        deps = {}
        for q in self.ring:
            for i in range(self.NRING):
                if self.ring_val[q][i] > 0:
                    deps[("d", q, i)] = self.ring_val[q][i]
        for e in self.ENGS:
            if e != "sp" and self.cnt[e] > 0:
                deps[e] = self.cnt[e]
        self._emit_waits("sp", deps)

    def emit(self):
        pass


def _bf(a):
    return np.asarray(a, np.float32).astype(ml_dtypes.bfloat16)


def make_consts():
    c = {}
    i128 = np.arange(128)
    c["identb"] = _bf(np.eye(128))
    c["identf"] = np.eye(128, dtype=np.float32)
    c["onesb"] = _bf(np.ones((128, 128)))
    c["onesf"] = np.ones((128, 128), np.float32)
    inv = (500000.0 ** (-(np.arange(8, dtype=np.float32) * 2.0 / 16))).astype(np.float32)
    pos = np.arange(2048, dtype=np.float32)
    ang = (pos[:, None] * inv[None, :]).astype(np.float32)
    cs = np.concatenate([np.cos(ang), np.sin(ang)], -1).astype(np.float32)
    c["cs_p"] = np.ascontiguousarray(cs.reshape(16, 128, 16).transpose(1, 0, 2))
    poss = (2048 + np.arange(4, dtype=np.float32))
    angs = (poss[:, None] * inv[None, :]).astype(np.float32)
    css = np.concatenate([np.cos(angs), np.sin(angs)], -1).astype(np.float32)
    c["cs_s"] = np.ascontiguousarray(np.tile(css, (16, 1)).reshape(64, 1, 16))
    cc = np.arange(128)
    t_abs = np.arange(2048)
    cm = np.where((16 * cc[:, None] + 31) <= t_abs[None, :], 0.0, NEG)
    cm[127, :] = NEG
    c["cmask"] = _bf(cm.reshape(128, 16, 128))
    n = np.arange(33)
    ov = np.minimum(16 * cc[:, None] + 32, 64 * n[None, :] + 64) - np.maximum(16 * cc[:, None], 64 * n[None, :])
    ov = np.maximum(ov, 0).astype(np.float32) / 32.0
    ov[127, :] = 0.0
    c["ovl"] = _bf(np.concatenate([np.ones((128, 1), np.float32), ov], 1))
    qb = t_abs // 64
    blk = np.arange(32)
    valid = (blk[None, :] <= qb[:, None])
    forced = (blk[None, :] == 0) | (blk[None, :] == qb[:, None]) | (blk[None, :] == qb[:, None] - 1)
    bonus = np.where(valid, 1.0e4 * forced, NEG).astype(np.float32)
    c["valid_p"] = _bf(np.ascontiguousarray(valid.astype(np.float32).reshape(16, 128, 32).transpose(1, 0, 2)))
    c["bonus_p"] = np.ascontiguousarray(bonus.reshape(16, 128, 32).transpose(1, 0, 2))
    blk33 = np.arange(33)
    forced_s = ((blk33 == 0) | (blk33 == 32) | (blk33 == 31)).astype(np.float32)
    c["valid_s"] = _bf(np.ones((64, 1, 33), np.float32))
    c["bonus_s"] = np.ascontiguousarray(np.tile(1.0e4 * forced_s[None, None, :], (64, 1, 1))).astype(np.float32)
    c["causneg"] = _bf(np.where(i128[:, None] <= i128[None, :], 0.0, NEG))
    c["edgeneg"] = _bf(np.where(i128[:, None] >= i128[None, :], 0.0, NEG))
    irs = np.zeros((64, 16, 4, 4), np.float32)
    for b in range(16):
        for t in range(4):
            irs[4 * b + t, b, :, t] = 1.0
    c["irep_s"] = _bf(irs)
    nm = np.zeros((128, 16, 4, 4), np.float32)
    nm[:64] = NEG
    for b in range(16):
        for tk in range(4):
            for t in range(4):
                if tk <= t:
                    nm[4 * b + tk, b, :, t] = 0.0
    c["newmask_s"] = _bf(nm)
    wm = np.zeros((128, 4, 4), np.float32)
    for t in range(4):
        wm[:t, :, t] = NEG
    c["winmask_s"] = _bf(wm)
    for nm_, cs_ in (("p", 64), ("s", 4)):
        ch = i128 // cs_
        same = ch[:, None] == ch[None, :]
        c["tri_" + nm_] = (same & (i128[:, None] <= i128[None, :])).astype(np.float32)
        c["chk_" + nm_] = same.astype(np.float32)
        c["negU_" + nm_] = np.where(same & (i128[:, None] <= i128[None, :]), 0.0, NEG).astype(np.float32)
        c["negL_" + nm_] = np.where(same & (i128[:, None] >= i128[None, :]), 0.0, NEG).astype(np.float32)
        c["strictL_" + nm_] = (same & (i128[:, None] > i128[None, :])).astype(np.float32)
    c["chsel_p"] = (i128[:, None] // 64 == np.arange(2)[None, :]).astype(np.float32)
    c["chsel_s"] = (i128[:64, None] // 4 == np.arange(16)[None, :]).astype(np.float32)
    c["rgcol"] = (i128 % 8).astype(np.float32).reshape(128, 1)
    n32 = np.arange(32)
    ex = np.zeros((128, 16, 128), np.float32)
    for j in range(16):
        ex[:32, j, :] = (n32[:, None] == (2 * j + i128[None, :] // 64))
    c["expand_p"] = _bf(ex)
    exs = np.zeros((128, 128), np.float32)
    exs[:32] = (n32[:, None] == (i128[None, :] // 4))
    c["expand_s"] = _bf(exs)
    exn = np.zeros((128, 64), np.float32)
    exn[32, :] = 1.0
    c["expand_n"] = _bf(exn)
    return c


CONST_DT = {"valid_p": BF16, "valid_s": BF16, "identb": BF16, "onesb": BF16, "cmask": BF16, "ovl": BF16, "causneg": BF16, "edgeneg": BF16,
            "irep_s": BF16, "newmask_s": BF16, "winmask_s": BF16, "expand_p": BF16, "expand_s": BF16, "expand_n": BF16}


def build_program(consts, do_prompt=True, do_sample=True, nt_prompt=NT, stage=99):
    nc = bass.Bass("TRN2", target_bir_lowering=False)
    es = ExitStack()
    D = {}

    def din(name, shape, dt=F32):
        D[name] = nc.dram_tensor(name, list(shape), dt, kind="ExternalInput").ap()
        return D[name]

    def dout(name, shape):
        D[name] = nc.dram_tensor(name, list(shape), F32, kind="ExternalOutput").ap()
        return D[name]

    din("xp", [2048, 1024]); din("xs", [64, 1024])
    for nm in ("cck", "ccv", "csk", "csv"):
        din(nm, [2560 * 8, 2048])
    din("cwk", [16, 512, 128]); din("cwv", [16, 512, 128])
    din("sconv", [48, 1536]); din("sgdn", [16, 4, 128, 128])
    din("ptl", [128, 16], I32)
    din("w_norm", [1, 1024]); din("w_in", [1024, NIN])
    din("pe_k", [32, 64]); din("w1k", [2048, 128]); din("w2k", [128, 64])
    din("pe_v", [32, 64]); din("w1v", [2048, 128]); din("w2v", [128, 64])
    din("w_conv", [4, 1536]); din("a_log", [1, 4]); din("dt_bias", [1, 4]); din("wg", [1, 128])
    din("w_out", [1024, 1024]); din("w_fn", [1, 1024])
    for k, v in consts.items():
        din("k_" + k, v.shape, CONST_DT.get(k, F32))
    dout("y_p", [2048, 1024]); dout("y_s", [64, 1024])
    for nm in ("ck", "cv", "sk", "sv"):
        dout(nm + "_p", [2048, 128]); dout(nm + "_s", [64, 128])
    dout("wk_p", [512, 128]); dout("wv_p", [512, 128])
    dout("wk_s", [16, 512, 128]); dout("wv_s", [16, 512, 128])
    dout("conv_p", [3, 1536]); dout("conv_s", [16, 3, 1536])
    dout("gdn_p", [4, 128, 128]); dout("gdn_s", [16, 4, 128, 128])

    with es:
        s = Sch(nc, es)

        def sb(name, shape, dt=F32):
            return es.enter_context(nc.sbuf_tensor(name, list(shape), dt))

        def psb(name, shape, dt=F32):
            return es.enter_context(nc.psum_tensor(name, list(shape), dt))

        K = {}
        for k, v in consts.items():
            K[k] = sb("K" + k, v.shape, CONST_DT.get(k, F32))
            s.dma("sp", lambda e, k=k: e.dma_start(out=K[k][:], in_=D["k_" + k]), writes=["K" + k])
        identb, identf = K["identb"], K["identf"]

        def bcast_load(name, src, n):
            t = sb(name, [128, n])
            s.dma("sp", lambda e: e.dma_start(out=t[:], in_=src.partition_broadcast(128)), writes=[name])
            return t
        wnorm_bc = bcast_load("wnorm_bc", D["w_norm"][0:1, :], 1024)
        wfn_bc = bcast_load("wfn_bc", D["w_fn"][0:1, :], 1024)
        wg_bc = bcast_load("wg_bc", D["wg"][0:1, :], 128)
        alog_bc = bcast_load("alog_bc", D["a_log"][0:1, :], 4)
        dtb_bc = bcast_load("dtb_bc", D["dt_bias"][0:1, :], 4)
        nea_bc = sb("nea_bc", [128, 4])
        s.op("act", lambda e: e.activation(out=nea_bc[:], in_=alog_bc[:], func=AF.Exp), reads=["alog_bc"], writes=["nea_bc"])
        s.op("dve", lambda e: e.tensor_scalar(out=nea_bc[:], in0=nea_bc[:], scalar1=-1.0, scalar2=None, op0=ALU.mult), reads=["nea_bc"], writes=["nea_bc"])
        wconv = sb("wconv", [128, 12, 4])
        s.dma("sp", [lambda e, j=j: e.dma_start(out=wconv[:, :, j], in_=D["w_conv"][j].rearrange("(c p) -> p c", p=128), allow_slow_non_contiguous=True) for j in range(4)], writes=["wconv"])

        h0_ = sb("h0", [128, NIN])
        hbuf = [h0_, h0_]
        wscr = nc.dram_tensor("wscr", [8, 128, NIN], BF16, kind="Internal").ap()
        wstage = [sb("wstage0", [128, 8, 512], BF16), sb("wstage1", [128, 8, 512], BF16)]
        WTMP = []
        wscr2 = nc.dram_tensor("wscr2", [8, 128, 1024], BF16, kind="Internal").ap()
        w1b = sb("w1b", [128, 2, 32, 128], BF16)
        for kind, nm in enumerate(("w1k", "w1v")):
            st = hbuf[0]
            tok = "h0"
            for half in range(2):
                src = D[nm][1024 * half:1024 * (half + 1), :].rearrange("(p d) h -> d p h", d=64)
                s.dma("sp", [lambda e, st=st, src=src: e.dma_start(out=st[0:64, 0:2048].rearrange("d (p h) -> d p h", h=128), in_=src),
                             lambda e, st=st, src=src: e.dma_start(out=st[64:128, 0:2048].rearrange("d (p h) -> d p h", h=128), in_=src)], writes=[tok])
                s.op("dve", lambda e, st=st, kind=kind, half=half: e.tensor_copy(out=w1b[:, kind, 16 * half:16 * (half + 1), :], in_=st[:, 0:2048].rearrange("d (p h) -> d p h", h=128)), reads=[tok], writes=["w1b"])
        w2f = sb("w2f", [128, 2, 64])
        w2b = sb("w2b", [128, 2, 64], BF16)
        s.dma("sp", [lambda e: e.dma_start(out=w2f[:, 0, :], in_=D["w2k"]), lambda e: e.dma_start(out=w2f[:, 1, :], in_=D["w2v"])], writes=["w2f"])
        s.op("dve", lambda e: e.tensor_copy(out=w2b[:], in_=w2f[:]), reads=["w2f"], writes=["w2b"])
        pef = sb("pef", [128, 2, 32])
        peb = sb("peb", [128, 2, 32], BF16)
        s.dma("sp", [lambda e, hh=hh, kind=kind, nm=nm: e.dma_start(out=pef[64 * hh:64 * hh + 64, kind, :], in_=D[nm].rearrange("p d -> d p"), allow_slow_non_contiguous=True)
                     for hh in range(2) for kind, nm in enumerate(("pe_k", "pe_v"))], writes=["pef"])
        s.op("dve", lambda e: e.tensor_copy(out=peb[:], in_=pef[:]), reads=["pef"], writes=["peb"])

        PS = {}

        def mkps(name, n, dt=F32, cols=512):
            PS[name] = [[(psb(f"ps_{name}{i}", [128, cols], dt), f"ps_{name}{i}") for i in range(n)], 0]

        mkps("mm", 2)
        mkps("tp", 2)
        mkps("acc", 2)
        mkps("g", 2)

        def ps(name):
            lst, i = PS[name]
            PS[name][1] = (i + 1) % len(lst)
            return lst[i]

        rr = [0]

        def evac_eng():
            rr[0] ^= 1
            return "act" if rr[0] else "dve"

        def copy_op(eng, out, in_, reads, writes):
            if eng == "act":
                s.op("act", lambda e: e.copy(out=out, in_=in_), reads=reads, writes=writes)
            else:
                s.op(eng, lambda e: e.tensor_copy(out=out, in_=in_), reads=reads, writes=writes)

        cbias = sb("cbias", [128, 2])
        pt_, ptok = ps("g")
        for kind in range(2):
            s.op("pe", [lambda e, kind=kind, p=p: e.matmul(pt_[:, kind:kind + 1], lhsT=w1b[0:64, kind, p, :], rhs=peb[0:64, kind, p:p + 1], start=(p == 0), stop=(p == 31)) for p in range(32)],
                 reads=["w1b", "peb"], writes=[ptok])
        s.op("dve", lambda e: e.tensor_copy(out=cbias[:], in_=pt_[:, 0:2]), reads=[ptok], writes=["cbias"])

        xt0_ = sb("xt0", [128, 1024])
        xt = [xt0_, xt0_]
        small = sb("small", [128, 64])
        xnb = sb("xnb", [128, 1024], BF16)
        xnT = sb("xnT", [128, 8, 128], BF16)
        ropet = sb("ropet", [128, 4, 8, 8])
        qb16 = sb("qb16", [128, 2, 512], BF16)
        kvb16 = sb("kvb16", [128, 768], BF16)
        qT = sb("qT", [128, 2, 4, 128], BF16)
        gates = sb("gates", [128, 24])
        zsil = sb("zsil", [128, 2, 512], BF16)
        onsa = sb("onsa", [128, 8, 64])
        mixb = sb("mixb", [128, 1024], BF16)
        mixT = sb("mixT", [128, 8, 128], BF16)
        hTc = sb("hTc", [128, 2, 2, 128], BF16)
        cckT = sb("cckT", [128, 128], BF16)
        ccv1 = sb("ccv1", [128, 2, 98], BF16)
        ET = sb("ET", [128, 2, 4, 128], BF16)
        imp = sb("imp", [128, 2, 33])
        m8 = sb("m8", [128, 8])
        selneg = sb("selneg", [128, 2, 33], BF16)
        selT = sb("selT", [128, 2, 128], BF16)
        s.op("pool", lambda e: e.memset(selT[:], 0.0), writes=["selT"])
        rz = sb("rz", [128, 8])
        wgt = sb("wgt", [128, 8])
        kT_all = sb("kT_all", [128, 4, 2048], BF16)
        v1_all = sb("v1_all", [128, 2, 16, 2, 65], BF16)
        convb = sb("convb", [128, 12, 131])
        cacc = sb("cacc", [128, 12, 128])
        ctmp = sb("ctmp", [128, 12, 128])
        gact = sb("gact", [128, 12, 128])
        gsq = sb("gsq", [128, 8, 128], BF16)
        gqk = sb("gqk", [128, 8, 128])
        gqkb = sb("gqkb", [128, 8, 128], BF16)
        gtok = sb("gtok", [128, 8, 128])
        gsc = sb("gsc", [128, 32])
        NSET = int(os.environ.get("NSET", "4"))
        HB = []
        HBBIG = []
        for i in range(NSET):
            big = sb("hbbig%d" % i, [128, 2048 if i == 0 else 1936])
            HBBIG.append(big)
            o_ = [0]

            def carve(n, dt=F32, big=big, o_=o_):
                w_ = n if dt == F32 else n // 2
                ap = big[:, o_[0]:o_[0] + w_]
                o_[0] += w_
                return ap if dt == F32 else ap.bitcast(BF16)
            d_ = {k_: carve(128) for k_ in "Gb Eup Elo etmp EDrow uf wTf qdT qkT kd vnew".split()}
            d_["GLb"] = carve(16)
            d_["Nm"] = [carve(128, BF16), carve(128, BF16)]
            d_["Mm"] = [carve(128, BF16), carve(128, BF16)]
            d_["Xb"] = [carve(256, BF16), carve(256, BF16)]
            d_["i"] = i
            HB.append(d_)
        Sst = [sb("S%d" % i, [128, 128]) for i in range(4)]
        ogdn = sb("ogdn", [128, 4, 128])
        otmp = sb("otmp", [128, 4, 128])
        v1flat = v1_all[:].rearrange("p a b c d -> p (a b c d)")
        kTf32 = kT_all[:].rearrange("p a n -> p (a n)").bitcast(F32)
        ws0flat = wstage[0][:].rearrange("p k n -> p (k n)")
        pairs = [(hbuf[0], "h0", v1flat, "v1_all"), (kTf32, "kT_all", ws0flat, "wstage0")]
        for k in range(8):
            st_, sttok_, tb_, tbtok_ = pairs[k % 2]
            s.dma("sp", lambda e, k=k, st_=st_: e.dma_start(out=st_[:, 0:NIN], in_=D["w_in"][128 * k:128 * (k + 1), :]), writes=[sttok_])
            s.op("dve" if k % 2 == 0 else "pool", lambda e, st_=st_, tb_=tb_: e.tensor_copy(out=tb_[:, 0:NIN], in_=st_[:, 0:NIN]), reads=[sttok_], writes=[tbtok_])
            s.dma("sp", lambda e, k=k, tb_=tb_: e.dma_start(out=wscr[k], in_=tb_[:, 0:NIN]), reads=[tbtok_], writes=["wscr"])
        for k in range(8):
            st_, sttok_, tb_, tbtok_ = pairs[k % 2]
            s.dma("sp", lambda e, k=k, st_=st_: e.dma_start(out=st_[:, 0:1024], in_=D["w_out"][128 * k:128 * (k + 1), :]), writes=[sttok_])
            s.op("dve" if k % 2 == 0 else "pool", lambda e, st_=st_, tb_=tb_: e.tensor_copy(out=tb_[:, 0:1024], in_=st_[:, 0:1024]), reads=[sttok_], writes=[tbtok_])
            s.dma("sp", lambda e, k=k, tb_=tb_: e.dma_start(out=wscr2[k], in_=tb_[:, 0:1024]), reads=[tbtok_], writes=["wscr2"])
        v1init = [False]
        s.op("pool", lambda e: e.memset(hTc[:], 0.0), writes=["hTc"])
        for hh_ in range(2):
            s.op("pool", lambda e, hh_=hh_: e.tensor_copy(out=ccv1[:, hh_, 64:98], in_=K["ovl"][:, :]), reads=["Kovl"], writes=["ccv1"])

        def run_tile(ctx):
            T = ctx["T"]
            it = ctx["it"]
            slot = ctx["slot"]
            x_src = ctx["x_src"]
            xtile, xtok = xt[0], "xt0"
            h, htok = hbuf[0], "h0"
            sample = ctx["sample"]
            sfx = "s" if sample else "p"
            s.dma("sp", lambda e: e.dma_start(out=xtile[0:T, :], in_=x_src), writes=[xtok])
            s.op("dve", lambda e: e.memset(small[0:T, 0:8], 0.0), writes=["small"])
            s.op("act", lambda e: e.activation(out=mixb[0:T, :], in_=xtile[0:T, :], func=AF.Square, accum_out=small[0:T, 0:1]), reads=[xtok], writes=["mixb", "small"])
            s.op("dve", lambda e: e.tensor_scalar(out=small[0:T, 1:2], in0=small[0:T, 0:1], scalar1=1.0 / 1024, scalar2=1e-6, op0=ALU.mult, op1=ALU.add), reads=["small"], writes=["small"])
            s.op("act", lambda e: e.activation(out=small[0:T, 2:3], in_=small[0:T, 1:2], func=AF.Sqrt), reads=["small"], writes=["small"])
            s.op("dve", lambda e: e.reciprocal(out=small[0:T, 2:3], in_=small[0:T, 2:3]), reads=["small"], writes=["small"])
            s.op("dve", lambda e: e.scalar_tensor_tensor(out=xnb[0:T, :], in0=xtile[0:T, :], scalar=small[0:T, 2:3], in1=wnorm_bc[0:T, :], op0=ALU.mult, op1=ALU.mult),
                 reads=[xtok, "small", "wnorm_bc"], writes=["xnb"])
            tp, tptok = ps("tp")
            tpb = tp[:].bitcast(BF16)
            s.op("pe", [lambda e, k=k: e.transpose(out=tpb[:, 128 * k:128 * k + T], in_=xnb[0:T, 128 * k:128 * (k + 1)], identity=identb[0:T, 0:T]) for k in range(8)],
                 reads=["xnb", "Kidentb"], writes=[tptok])
            copy_op(evac_eng(), xnT[:, :, 0:T], tpb.rearrange("p (k t) -> p k t", t=128)[:, :, 0:T], [tptok], ["xnT"])
            for gi, c0 in enumerate(range(0, NIN, 512)):
                cw = min(512, NIN - c0)
                pm, pmtok = ps("mm")
                wsl = gi % 2
                wsg, wstok = wstage[wsl], "wstage%d" % wsl
                s.dma("sp", lambda e, c0=c0, cw=cw, wsg=wsg: e.dma_start(out=wsg[:, :, 0:cw], in_=wscr[:, :, c0:c0 + cw].rearrange("k p n -> p k n")), reads=["wscr"], writes=[wstok])
                s.op("pe", [lambda e, k=k, c0=c0, cw=cw, pm=pm, wsg=wsg: e.matmul(pm[0:T, 0:cw], lhsT=xnT[:, k, 0:T], rhs=wsg[:, k, 0:cw], start=(k == 0), stop=(k == 7)) for k in range(8)],
                     reads=["xnT", wstok], writes=[pmtok])
                copy_op(evac_eng(), h[0:T, c0:c0 + cw], pm[0:T, 0:cw], [pmtok], [htok])
            if not sample:
                for gi, c0 in enumerate((0, 512)):
                    wsg, wstok = wstage[gi], "wstage%d" % gi
                    s.dma("sp", lambda e, c0=c0, wsg=wsg: e.dma_start(out=wsg[:, :, :], in_=wscr2[:, :, c0:c0 + 512].rearrange("k p n -> p k n")), reads=["wscr2"], writes=[wstok])
            if stage < 2:
                return
            cs = K["cs_" + sfx]
            cosv = cs[0:T, it, 0:8]
            sinv = cs[0:T, it, 8:16]

            def rope(view, nh_shape, outs):
                A, B = nh_shape
                x1 = view[:, :, :, 0:8]
                x2 = view[:, :, :, 8:16]
                cb = cosv.unsqueeze(1).unsqueeze(1).to_broadcast([T, A, B, 8])
                sbc = sinv.unsqueeze(1).unsqueeze(1).to_broadcast([T, A, B, 8])
                t = [ropet[0:T, j, 0:A * B, :].rearrange("p (a b) d -> p a b d", b=B) for j in range(4)]
                rd = [htok, "K" + "cs_" + sfx]
                s.op("dve", lambda e: e.tensor_tensor(out=t[0], in0=x1, in1=cb, op=ALU.mult), reads=rd, writes=["ropet"])
                s.op("dve", lambda e: e.tensor_tensor(out=t[1], in0=x2, in1=sbc, op=ALU.mult), reads=rd, writes=["ropet"])
                s.op("dve", lambda e: e.tensor_tensor(out=t[2], in0=x2, in1=cb, op=ALU.mult), reads=rd, writes=["ropet"])
                s.op("dve", lambda e: e.tensor_tensor(out=t[3], in0=x1, in1=sbc, op=ALU.mult), reads=rd, writes=["ropet"])
                return t
            hq = h[0:T, 0:512].rearrange("p (a g d) -> p a g d", a=2, g=4)
            qraw = qb16[0:T, 0, :].rearrange("p (g a d) -> p a g d", g=4, a=2)
            qo = qb16[0:T, 1, :].rearrange("p (g a d) -> p a g d", g=4, a=2)
            s.op("act", lambda e: e.copy(out=qraw, in_=hq), reads=[htok], writes=["qb16"])
            s.op("pool", lambda e: e.tensor_copy(out=qo, in_=hq), reads=[htok], writes=["qb16"])
            t = rope(hq[:, :, :, 0:16], (2, 4), None)
            s.op("dve", lambda e: e.tensor_tensor(out=qo[:, :, :, 0:8], in0=t[0], in1=t[1], op=ALU.subtract), reads=["ropet"], writes=["qb16"])
            s.op("dve", lambda e: e.tensor_tensor(out=qo[:, :, :, 8:16], in0=t[2], in1=t[3], op=ALU.add), reads=["ropet"], writes=["qb16"])
            kvw = h[0:T, 768:1280].rearrange("p (a r) -> p a r", a=2)[:, :, 0:128].rearrange("p a (b d) -> p a b d", b=2)
            t = rope(kvw[:, :, :, 0:16], (2, 2), None)
            s.op("dve", lambda e: e.tensor_tensor(out=kvw[:, :, :, 0:8], in0=t[0], in1=t[1], op=ALU.subtract), reads=["ropet"], writes=[htok])
            s.op("dve", lambda e: e.tensor_tensor(out=kvw[:, :, :, 8:16], in0=t[2], in1=t[3], op=ALU.add), reads=["ropet"], writes=[htok])
            r0 = ctx["row0"]
            outs = []
            for j, nm in enumerate(("ck", "cv", "sk", "sv")):
                dst = D[nm + "_" + sfx][r0:r0 + T, :]
                outs.append(lambda e, j=j, dst=dst: e.dma_start(out=dst, in_=h[0:T, 512 + 128 * j:640 + 128 * j]))
            s.dma("sp", outs, reads=[htok])
            if not sample and it >= NT - 4:
                w0 = (it - (NT - 4)) * 128
                s.dma("sp", [lambda e: e.dma_start(out=D["wk_p"][w0:w0 + 128, :], in_=h[0:T, 1024:1152]),
                             lambda e: e.dma_start(out=D["wv_p"][w0:w0 + 128, :], in_=h[0:T, 1152:1280])], reads=[htok])
            if not sample and it == NT - 1:
                s.dma("sp", lambda e: e.dma_start(out=D["conv_p"], in_=h[125:128, C_QKV:C_QKV + 1536]), reads=[htok])
            if sample:
                s.dma("sp", [lambda e, b=b: e.dma_start(out=D["conv_s"][b], in_=h[4 * b + 1:4 * b + 4, C_QKV:C_QKV + 1536]) for b in range(NB)], reads=[htok])
                s.dma("sp", [lambda e, b=b: e.dma_start(out=D["wk_s"][b, 508:512, :], in_=h[4 * b:4 * b + 4, 1024:1152]) for b in range(NB)]
                      + [lambda e, b=b: e.dma_start(out=D["wv_s"][b, 508:512, :], in_=h[4 * b:4 * b + 4, 1152:1280]) for b in range(NB)], reads=[htok])
            if stage < 3:
                return
            s.op("pool", lambda e: e.tensor_copy(out=kvb16[0:T, :], in_=h[0:T, 512:1280]), reads=[htok], writes=["kvb16"])
            tp, tptok = ps("tp")
            tpb = tp[:].bitcast(BF16)
            srcs = [0, 1, 2, 4]
            s.op("pe", [lambda e, j=j, c=c: e.transpose(out=tpb[:, 128 * j:128 * j + T], in_=kvb16[0:T, 128 * c:128 * (c + 1)], identity=identb[0:T, 0:T]) for j, c in enumerate(srcs)],
                 reads=["kvb16", "Kidentb"], writes=[tptok])
            if stage < 3.2:
                return
            kdst = ctx["kT_dst"]
            copy_op(evac_eng(), kdst, tpb[:, 0:512].rearrange("p (k t) -> p k t", t=128)[:, :, 0:T], [tptok], [ctx["kT_tok"]])
            if stage < 3.4:
                return
            if not v1init[0]:
                s.op("pool", lambda e: e.memset(v1_all[:], 1.0), writes=["v1_all"])
                v1init[0] = True
            vd = ctx["v1_dst"]
            s.op("pool", lambda e: e.tensor_copy(out=vd[:, 0, :, 0:64], in_=h[0:T, 896:1024].rearrange("p (h d) -> p h d", h=2)), reads=[htok], writes=[ctx["v1_tok"]])
            s.op("pool", lambda e: e.tensor_copy(out=vd[:, 1, :, 0:64], in_=h[0:T, 1152:1280].rearrange("p (h d) -> p h d", h=2)), reads=[htok], writes=[ctx["v1_tok"]])
            if stage < 3.6:
                return
            tp, tptok = ps("tp")
            tpb = tp[:].bitcast(BF16)
            fl = []
            for w in range(2):
                for g in range(4):
                    src = qb16[0:T, w, 128 * g:128 * (g + 1)]
                    fl.append(lambda e, w=w, g=g, src=src: e.transpose(out=tpb[:, 128 * (4 * w + g):128 * (4 * w + g) + T], in_=src, identity=identb[0:T, 0:T]))
            s.op("pe", fl, reads=["qb16", "Kidentb"], writes=[tptok])
            copy_op(evac_eng(), qT[:, :, :, 0:T], tpb.rearrange("p (w g t) -> p w g t", w=2, g=4)[:, :, :, 0:T], [tptok], ["qT"])
            if stage < 3.8:
                return
            s.op("act", lambda e: e.activation(out=gates[0:T, :], in_=h[0:T, C_GATE:C_GATE + 24], func=AF.Sigmoid), reads=[htok], writes=["gates"])
            if stage < 3.9:
                return
            s.op("act", lambda e: e.activation(out=zsil[0:T, 0, :], in_=h[0:T, C_ZN:C_ZN + 512], func=AF.Silu), reads=[htok], writes=["zsil"])
            s.op("act", lambda e: e.activation(out=zsil[0:T, 1, :], in_=h[0:T, C_ZG:C_ZG + 512], func=AF.Silu), reads=[htok], writes=["zsil"])

            if stage < 4:
                return
            gdn_tile(ctx, T, it, h, htok)
            if not sample:
                gp = ctx["gdn_prep"]

                def tick(n=int(os.environ.get("NTICK", "1"))):
                    for _ in range(n):
                        try:
                            next(gp)
                        except StopIteration:
                            return
                TICK[0] = tick
                nsa_prompt(ctx, T, it)
                TICK[0] = None
            else:
                nsa_sample(ctx, T)
            s.op("dve", lambda e: e.tensor_tensor(out=mixb[0:T, 0:512], in0=onsa[0:T, :, :].rearrange("p a d -> p (a d)"), in1=zsil[0:T, 0, :], op=ALU.mult), reads=["onsa", "zsil"], writes=["mixb"])
            if stage < 5:
                return
            gdn_tile2(ctx, T, it, h, htok)
            if stage < 6:
                return
            tp, tptok = ps("tp")
            tpb = tp[:].bitcast(BF16)
            s.op("pe", [lambda e, k=k: e.transpose(out=tpb[:, 128 * k:128 * k + T], in_=mixb[0:T, 128 * k:128 * (k + 1)], identity=identb[0:T, 0:T]) for k in range(8)],
                 reads=["mixb", "Kidentb"], writes=[tptok])
            copy_op(evac_eng(), mixT[:, :, 0:T], tpb.rearrange("p (k t) -> p k t", t=128)[:, :, 0:T], [tptok], ["mixT"])
            for gi, c0 in enumerate((0, 512)):
                pm, pmtok = ps("mm")
                wsg, wstok = wstage[gi], "wstage%d" % gi
                if sample:
                    s.dma("sp", lambda e, c0=c0, wsg=wsg: e.dma_start(out=wsg[:, :, :], in_=wscr2[:, :, c0:c0 + 512].rearrange("k p n -> p k n")), reads=["wscr2"], writes=[wstok] + ctx.get("ws_extra", []))
                s.op("pe", [lambda e, k=k, c0=c0, pm=pm, wsg=wsg: e.matmul(pm[0:T, :], lhsT=mixT[:, k, 0:T], rhs=wsg[:, k, :], start=(k == 0), stop=(k == 7)) for k in range(8)],
                     reads=["mixT", wstok], writes=[pmtok])
                s.op("dve", lambda e, c0=c0, pm=pm: e.tensor_tensor(out=xtile[0:T, c0:c0 + 512], in0=pm[0:T, :], in1=xtile[0:T, c0:c0 + 512], op=ALU.add), reads=[pmtok, xtok], writes=[xtok])
            s.op("act", lambda e: e.activation(out=xnb[0:T, :], in_=xtile[0:T, :], func=AF.Square, accum_out=small[0:T, 4:5]), reads=[xtok], writes=["xnb", "small"])
            s.op("dve", lambda e: e.tensor_scalar(out=small[0:T, 5:6], in0=small[0:T, 4:5], scalar1=1.0 / 1024, scalar2=1e-6, op0=ALU.mult, op1=ALU.add), reads=["small"], writes=["small"])
            s.op("act", lambda e: e.activation(out=small[0:T, 6:7], in_=small[0:T, 5:6], func=AF.Sqrt), reads=["small"], writes=["small"])
            s.op("dve", lambda e: e.reciprocal(out=small[0:T, 6:7], in_=small[0:T, 6:7]), reads=["small"], writes=["small"])
            s.op("dve", lambda e: e.scalar_tensor_tensor(out=xtile[0:T, :], in0=xtile[0:T, :], scalar=small[0:T, 6:7], in1=wfn_bc[0:T, :], op0=ALU.mult, op1=ALU.mult),
                 reads=[xtok, "small", "wfn_bc"], writes=[xtok])
            s.dma("sp", lambda e: e.dma_start(out=ctx["y_dst"], in_=xtile[0:T, :]), reads=[xtok])

        def compress(rowsT_k, rowsT_v, rtok, c0, nblk):
            pms = [ps("mm"), ps("mm")]
            for kind, rows in enumerate((rowsT_k, rowsT_v)):
                for hh in range(2):
                    pm, pmtok = pms[hh]
                    fl = []
                    for p in range(32):
                        rhs = rows[64 * hh:64 * hh + 64, 16 * c0 + p:16 * c0 + p + 16 * (nblk - 1) + 1:16]
                        fl.append(lambda e, kind=kind, hh=hh, p=p, rhs=rhs, pm=pm: e.matmul(pm[:, kind * 127:kind * 127 + nblk],
                                                                              lhsT=w1b[64 * hh:64 * hh + 64, kind, p, :], rhs=rhs, start=(p == 0), stop=(p == 31)))
                    s.op("pe", fl, reads=["w1b", rtok], writes=[pmtok])
            if stage < 4.11:
                return
            for kind in range(2):
                for hh in range(2):
                    pm, pmtok = pms[hh]
                    s.op("act", lambda e, kind=kind, hh=hh, pm=pm: e.activation(out=hTc[:, kind, hh, c0:c0 + nblk], in_=pm[:, kind * 127:kind * 127 + nblk], func=AF.Silu, bias=cbias[:, kind:kind + 1]), reads=[pmtok, "cbias"], writes=["hTc"])
            if stage < 4.12:
                return
            pg, pgtok = ps("g")
            s.op("pe", [lambda e, hh=hh: e.matmul(pg[64 * hh:64 * hh + 64, 0:128], lhsT=w2b[:, 0, :], rhs=hTc[:, 0, hh, :], start=True, stop=True) for hh in range(2)]
                 + [lambda e, hh=hh: e.matmul(pg[:, 128 + 64 * hh:128 + 64 * hh + 64], lhsT=hTc[:, 1, hh, :], rhs=w2b[:, 1, :], start=True, stop=True) for hh in range(2)],
                 reads=["hTc", "w2b"], writes=[pgtok])
            if stage < 4.13:
                return
            copy_op("act", cckT[:, :], pg[:, 0:128], [pgtok], ["cckT"])
            copy_op("dve", ccv1[:, :, 0:64], pg[:, 128:256].rearrange("p (h d) -> p h d", h=2), [pgtok], ["ccv1"])

        def attn_branch(T, q_rhs, key_tiles, acc_cols, first, last, acc, acctok, ncols):
            pass

        def nsa_core(T, hh, qsel, key_tiles, W, rd_extra, branch, first_branch, gate_col, colmap=None):
            acc, acctok = ps("acc")
            accv = acc[0:T, 0:4 * W].rearrange("p (g w) -> p g w", g=4)
            nkt = len(key_tiles)
            pend = {}

            def issue_qk(j):
                kt = key_tiles[j]
                nk = kt["nk"]
                pm, pmtok = ps("mm")
                ncol = kt.get("ncol", 4 * T)
                fl = [lambda e, kt=kt, pm=pm, nk=nk, ncol=ncol: e.matmul(pm[0:nk, 0:ncol], lhsT=kt["kT"], rhs=kt["q"], start=True, stop=(len(kt["masks"]) == 0))]
                for mi, (ml, mr) in enumerate(kt["masks"]):
                    fl.append(lambda e, ml=ml, mr=mr, pm=pm, nk=nk, ncol=ncol, mi=mi, nm=len(kt["masks"]): e.matmul(pm[0:nk, 0:ncol], lhsT=ml, rhs=mr, start=False, stop=(mi == nm - 1)))
                s.op("pe", fl, reads=kt["rd"], writes=[pmtok])
                pend[j] = (pm, pmtok)

            issue_qk(0)
            for j, kt in enumerate(key_tiles):
                nk = kt["nk"]
                ncol = kt.get("ncol", 4 * T)
                pm, pmtok = pend.pop(j)
                eb = ctx_et[0]
                ctx_et[0] ^= 1
                etok = "ET%d" % eb
                s.op("act", lambda e, kt=kt, pm=pm, nk=nk, ncol=ncol, eb=eb: e.activation(out=kt["et_out"](eb), in_=kt["sc_in"](pm), func=AF.Exp, scale=0.125), reads=[pmtok], writes=[etok])
                if j + 1 < nkt:
                    issue_qk(j + 1)
                s.op("pe", [lambda e, g=g, kt=kt, eb=eb, nk=nk: e.matmul(accv[:, g, :], lhsT=ET[0:nk, eb, g, 0:T], rhs=kt["v1"], start=(j == 0 and g == 0), stop=(j == nkt - 1), skip_group_check=True) for g in range(4)],
                     reads=[etok] + kt["rdv"], writes=[acctok])
                if TICK[0] is not None:
                    TICK[0]()
            return acc, accv, acctok

        ctx_et = [0]
        TICK = [None]

        def branch_epilogue(T, hh, accv, acctok, W, gate_col, first_branch, want_imp=False, nblk=32):
            s.op("dve", lambda e: e.tensor_scalar(out=rz[0:T, 0:4], in0=accv[:, :, 64], scalar1=1e-30, scalar2=None, op0=ALU.max), reads=[acctok], writes=["rz"])
            s.op("dve", lambda e: e.reciprocal(out=rz[0:T, 0:4], in_=rz[0:T, 0:4]), reads=["rz"], writes=["rz"])
            s.op("dve", lambda e: e.tensor_tensor(out=wgt[0:T, 0:4], in0=rz[0:T, 0:4], in1=gates[0:T, gate_col + 4 * hh:gate_col + 4 * hh + 4], op=ALU.mult), reads=["rz", "gates"], writes=["wgt"])
            wb_ = wgt[0:T, 0:4].unsqueeze(2).to_broadcast([T, 4, 64])
            od = onsa[0:T, 4 * hh:4 * hh + 4, :]
            if first_branch:
                s.op("dve", lambda e: e.tensor_tensor(out=od, in0=accv[:, :, 0:64], in1=wb_, op=ALU.mult), reads=[acctok, "wgt"], writes=["onsa"])
            else:
                s.op("dve", lambda e: e.tensor_tensor(out=otmp[0:T, 0:2, :].rearrange("p a (g d) -> p (a g) d", d=64), in0=accv[:, :, 0:64], in1=wb_, op=ALU.mult), reads=[acctok, "wgt"], writes=["otmp"])
                s.op("dve", lambda e: e.tensor_tensor(out=od, in0=od, in1=otmp[0:T, 0:2, :].rearrange("p a (g d) -> p (a g) d", d=64), op=ALU.add), reads=["otmp", "onsa"], writes=["onsa"])
            if want_imp:
                iv = imp[0:T, hh, 0:nblk]
                s.op("dve", lambda e: e.tensor_scalar(out=iv, in0=accv[:, 0, 65:65 + nblk], scalar1=rz[0:T, 0:1], scalar2=None, op0=ALU.mult), reads=[acctok, "rz"], writes=["imp"])
                for g in range(1, 4):
                    s.op("dve", lambda e, g=g: e.scalar_tensor_tensor(out=iv, in0=accv[:, g, 65:65 + nblk], scalar=rz[0:T, g:g + 1], in1=iv, op0=ALU.mult, op1=ALU.add), reads=[acctok, "rz", "imp"], writes=["imp"])

        def select_blocks(T, hh, it, sfx, nblk):
            iv = imp[0:T, hh, 0:nblk]
            s.op("dve", lambda e: e.tensor_tensor(out=iv, in0=iv, in1=K["valid_" + sfx][0:T, it, :], op=ALU.mult), reads=["imp", "Kvalid_" + sfx], writes=["imp"])
            s.op("dve", lambda e: e.tensor_tensor(out=iv, in0=iv, in1=K["bonus_" + sfx][0:T, it, :], op=ALU.add), reads=["imp", "Kbonus_" + sfx], writes=["imp"])
            s.op("dve", lambda e: e.max(out=m8[0:T, :], in_=iv), reads=["imp"], writes=["m8"])
            s.op("dve", lambda e: e.tensor_scalar(out=selneg[0:T, hh, 0:nblk], in0=iv, scalar1=m8[0:T, 7:8], scalar2=NEG, op0=ALU.is_lt, op1=ALU.mult), reads=["imp", "m8"], writes=["selneg"])
            tp, tptok = ps("tp")
            tpb = tp[:].bitcast(BF16)
            s.op("pe", lambda e: e.transpose(out=tpb[0:nblk, 0:T], in_=selneg[0:T, hh, 0:nblk], identity=identb[0:T, 0:T]), reads=["selneg", "Kidentb"], writes=[tptok])
            copy_op("dve", selT[0:nblk, hh, 0:T], tpb[0:nblk, 0:T], [tptok], ["selT"])

        def std_tile(T, hh, qw, kT_ap, nk, v1_ap, masks, rd, rdv):
            return dict(kT=kT_ap, nk=nk, q=qT[64 * hh:64 * hh + 64, qw, :, 0:T], masks=masks, v1=v1_ap, rd=["qT"] + rd, rdv=rdv,
                        et_out=lambda eb, nk=nk: ET[0:nk, eb, :, 0:T], sc_in=lambda pm, nk=nk: pm[0:nk, 0:4 * T].rearrange("p (g t) -> p g t", g=4))

        def nsa_prompt(ctx, T, it):
            c0 = 0 if it == 0 else 8 * it - 1
            c1 = 8 * it + 6
            if stage < 4.1:
                return
            compress(kT_all[:, 0, :], kT_all[:, 1, :], "kT_all", c0, c1 - c0 + 1)
            ident4 = lambda M: M.unsqueeze(1).to_broadcast([128, 4, 128])
            if stage < 4.2:
                return
            res_c = []
            for hh in range(2):
                kt = std_tile(T, hh, 0, cckT[64 * hh:64 * hh + 64, :], 128, ccv1[:, hh, :],
                              [(identb[:, :], K["cmask"][:, it, :].unsqueeze(1).to_broadcast([128, 4, 128]))], ["cckT", "Kcmask", "Kidentb"], ["ccv1"])
                res_c.append(nsa_core(T, hh, 0, [kt], 98, [], "cmp", True, 0))
            for hh in range(2):
                acc, accv, acctok = res_c[hh]
                branch_epilogue(T, hh, accv, acctok, 98, 0, True, want_imp=True, nblk=32)
                select_blocks(T, hh, it, "p", 32)
            for hh in range(2):
                tiles = []
                for j in range(max(0, it - 4), it + 1):
                    masks = []
                    if j == it:
                        masks.append((identb[:, :], ident4(K["causneg"][:, :])))
                    elif j == it - 4:
                        masks.append((identb[:, :], ident4(K["edgeneg"][:, :])))
                    tiles.append(std_tile(T, hh, 1, kT_all[64 * hh:64 * hh + 64, 3, 128 * j:128 * j + 128], 128, v1_all[:, 1, j, hh, :], masks,
                                          ["kT_all", "Kidentb", "Kcausneg", "Kedgeneg"], ["v1_all"]))
                acc, accv, acctok = nsa_core(T, hh, 1, tiles, 65, [], "win", False, 16)
                branch_epilogue(T, hh, accv, acctok, 65, 16, False)
            for hh in range(2):
                tiles = []
                for j in range(it + 1):
                    masks = [(K["expand_p"][:, j, :], selT[:, hh, 0:T].unsqueeze(1).to_broadcast([128, 4, T]))]
                    if j == it:
                        masks.append((identb[:, :], ident4(K["causneg"][:, :])))
                    tiles.append(std_tile(T, hh, 1, kT_all[64 * hh:64 * hh + 64, 2, 128 * j:128 * j + 128], 128, v1_all[:, 0, j, hh, :], masks,
                                          ["kT_all", "selT", "Kexpand_p", "Kidentb", "Kcausneg"], ["v1_all"]))
                acc, accv, acctok = nsa_core(T, hh, 1, tiles, 65, [], "slc", False, 8)
                branch_epilogue(T, hh, accv, acctok, 65, 8, False)

        def gdn_tile(ctx, T, it, h, htok):
            sample = ctx["sample"]
            sfx = "s" if sample else "p"
            nlev = 2 if sample else 6
            def prep_gen():
                for grp in range(3):
                    tp, tptok = ps("tp")
                    s.op("pe", [lambda e, c=c, grp=grp, tp=tp: e.transpose(out=tp[:, 128 * (c - 4 * grp):128 * (c - 4 * grp) + T], in_=h[0:T, C_QKV + 128 * c:C_QKV + 128 * (c + 1)], identity=identf[0:T, 0:T])
                                for c in range(4 * grp, 4 * grp + 4)], reads=[htok, "Kidentf"], writes=[tptok])
                    src = tp[:].rearrange("p (c t) -> p c t", t=128)[:, :, 0:T]
                    if not sample:
                        copy_op(evac_eng(), convb[:, 4 * grp:4 * grp + 4, 3:3 + T], src, [tptok], ["convb"])
                        yield
                    else:
                        dst = convb[:, 4 * grp:4 * grp + 4, 0:112].rearrange("p c (b r) -> p c b r", r=7)[:, :, :, 3:7]
                        copy_op(evac_eng(), dst, src.rearrange("p c (b t) -> p c b t", t=4), [tptok], ["convb"])
                        yield
                if sample:
                    s.dma("sp", lambda e: e.dma_start(out=cacc[0:48, :, :].rearrange("p c t -> p (c t)"), in_=D["sconv"]), writes=["cacc"])
                    yield
                    for grp in range(3):
                        tp, tptok = ps("tp")
                        s.op("pe", [lambda e, c=c, grp=grp, tp=tp: e.transpose(out=tp[:, 128 * (c - 4 * grp):128 * (c - 4 * grp) + 48], in_=cacc[0:48, c, :], identity=identf[0:48, 0:48])
                                    for c in range(4 * grp, 4 * grp + 4)], reads=["cacc", "Kidentf"], writes=[tptok])
                        src = tp[:].rearrange("p (c t) -> p c t", t=128)[:, :, 0:48].rearrange("p c (b r) -> p c b r", r=3)
                        dst = convb[:, 4 * grp:4 * grp + 4, 0:112].rearrange("p c (b r) -> p c b r", r=7)[:, :, :, 0:3]
                        copy_op(evac_eng(), dst, src, [tptok], ["convb"])
                        yield
                elif it == 0:
                    s.op("pool", lambda e: e.memset(convb[:, :, 0:3], 0.0), writes=["convb"])
                    yield

                def shifted(j):
                    if not sample:
                        return convb[:, :, j:j + T]
                    return convb[:, :, 0:112].rearrange("p c (b r) -> p c b r", r=7)[:, :, :, j:j + 4]

                def shp(ap):
                    return ap if not sample else ap.rearrange("p c (b t) -> p c b t", t=4)

                def wj(j):
                    w = wconv[:, :, j:j + 1]
                    return w.to_broadcast([128, 12, T]) if not sample else w.unsqueeze(3).to_broadcast([128, 12, 16, 4])
                s.op("dve", lambda e: e.tensor_tensor(out=shp(cacc[:, :, 0:T]), in0=shifted(0), in1=wj(0), op=ALU.mult), reads=["convb", "wconv"], writes=["cacc"])
                yield
                for j in range(1, 4):
                    eng = "pool" if j % 2 == 1 else "dve"
                    s.op(eng, lambda e, j=j: e.tensor_tensor(out=shp(ctmp[:, :, 0:T]), in0=shifted(j), in1=wj(j), op=ALU.mult), reads=["convb", "wconv"], writes=["ctmp"])
                    yield
                    s.op("dve", lambda e: e.tensor_tensor(out=cacc[:, :, 0:T], in0=cacc[:, :, 0:T], in1=ctmp[:, :, 0:T], op=ALU.add), reads=["cacc", "ctmp"], writes=["cacc"])
                    yield
                if not sample:
                    s.op("pool", lambda e: e.tensor_copy(out=ctmp[:, :, 0:3], in_=convb[:, :, T:T + 3]), reads=["convb"], writes=["ctmp"])
                    yield
                    s.op("pool", lambda e: e.tensor_copy(out=convb[:, :, 0:3], in_=ctmp[:, :, 0:3]), reads=["ctmp"], writes=["convb"])
                    yield
                s.op("act", lambda e: e.activation(out=gact[:, :, 0:T], in_=cacc[:, :, 0:T], func=AF.Silu), reads=["cacc"], writes=["gact"])
                yield
                s.op("pool", lambda e: e.tensor_tensor(out=gsq[:, :, 0:T], in0=gact[:, 0:8, 0:T], in1=gact[:, 0:8, 0:T], op=ALU.mult), reads=["gact"], writes=["gsq"])
                yield
                for half in range(2):
                    pm, pmtok = ps("g")
                    s.op("pe", [lambda e, c=c, pm=pm, half=half: e.matmul(pm[:, 128 * (c - 4 * half):128 * (c - 4 * half) + T], lhsT=K["onesb"][:, :], rhs=gsq[:, c, 0:T], start=True, stop=True) for c in range(4 * half, 4 * half + 4)],
                         reads=["gsq", "Konesb"], writes=[pmtok])
                    src = pm[:].rearrange("p (c t) -> p c t", t=128)[:, :, 0:T]
                    dst = gqk[:, 4 * half:4 * half + 4, 0:T]
                    s.op("dve", lambda e, src=src, dst=dst: e.tensor_scalar(out=dst, in0=src, scalar1=1e-6, scalar2=None, op0=ALU.add), reads=[pmtok], writes=["gqk"])
                    yield
                    s.op("act", lambda e, dst=dst: e.activation(out=dst, in_=dst, func=AF.Sqrt), reads=["gqk"], writes=["gqk"])
                    yield
                    s.op("dve", lambda e, dst=dst: e.reciprocal(out=dst, in_=dst), reads=["gqk"], writes=["gqk"])
                    yield
                    if half == 0:
                        s.op("dve", lambda e, dst=dst: e.scalar_tensor_tensor(out=dst, in0=dst, scalar=float(128 ** -0.5), in1=gact[:, 0:4, 0:T], op0=ALU.mult, op1=ALU.mult), reads=["gqk", "gact"], writes=["gqk"])
                        yield
                    else:
                        s.op("dve", lambda e, dst=dst: e.tensor_tensor(out=dst, in0=dst, in1=gact[:, 4:8, 0:T], op=ALU.mult), reads=["gqk", "gact"], writes=["gqk"])
                        yield
                s.op("act", lambda e: e.copy(out=gqkb[:, :, 0:T], in_=gqk[:, :, 0:T]), reads=["gqk"], writes=["gqkb"])
                yield
                for half in range(2):
                    tp, tptok = ps("tp")
                    srcs = [gqk[:, 4 + c, 0:T] for c in range(4)] if half == 0 else [gact[:, 8 + c, 0:T] for c in range(4)]
                    s.op("pe", [lambda e, c=c, tp=tp, src=src: e.transpose(out=tp[0:T, 128 * c:128 * (c + 1)], in_=src, identity=identf[:, :]) for c, src in enumerate(srcs)],
                         reads=["gqk", "gact", "Kidentf"], writes=[tptok])
                    copy_op(evac_eng(), gtok[0:T, 4 * half:4 * half + 4, :], tp[0:T, :].rearrange("p (c d) -> p c d", d=128), [tptok], ["gtok"])
                    yield
                s.op("act", lambda e: e.activation(out=gsc[0:T, 0:4], in_=h[0:T, C_B:C_B + 4], func=AF.Sigmoid), reads=[htok], writes=["gsc"])
                yield
                s.op("dve", lambda e: e.tensor_tensor(out=gsc[0:T, 4:8], in0=h[0:T, C_A:C_A + 4], in1=dtb_bc[0:T, :], op=ALU.add), reads=[htok, "dtb_bc"], writes=["gsc"])
                yield
                s.op("act", lambda e: e.activation(out=gsc[0:T, 8:12], in_=gsc[0:T, 4:8], func=AF.Abs), reads=["gsc"], writes=["gsc"])
                yield
                s.op("act", lambda e: e.activation(out=gsc[0:T, 8:12], in_=gsc[0:T, 8:12], func=AF.Exp, scale=-1.0), reads=["gsc"], writes=["gsc"])
                yield
                s.op("act", lambda e: e.activation(out=gsc[0:T, 8:12], in_=gsc[0:T, 8:12], func=AF.Ln, bias=1.0), reads=["gsc"], writes=["gsc"])
                yield
                s.op("dve", lambda e: e.scalar_tensor_tensor(out=gsc[0:T, 4:8], in0=gsc[0:T, 4:8], scalar=0.0, in1=gsc[0:T, 8:12], op0=ALU.max, op1=ALU.add), reads=["gsc"], writes=["gsc"])
                yield
                s.op("dve", lambda e: e.tensor_tensor(out=gsc[0:T, 4:8], in0=gsc[0:T, 4:8], in1=nea_bc[0:T, :], op=ALU.mult), reads=["gsc", "nea_bc"], writes=["gsc"])
                yield
                pg, pgtok = ps("g")
                s.op("pe", [lambda e: e.matmul(pg[0:T, 0:4], lhsT=K["tri_" + sfx][0:T, 0:T], rhs=gsc[0:T, 4:8], start=True, stop=True),
                            lambda e: e.matmul(pg[0:T, 4:8], lhsT=K["chk_" + sfx][0:T, 0:T], rhs=gsc[0:T, 4:8], start=True, stop=True)],
                     reads=["gsc", "Ktri_" + sfx, "Kchk_" + sfx], writes=[pgtok])
                s.op("dve", lambda e: e.tensor_copy(out=gsc[0:T, 12:20], in_=pg[0:T, 0:8]), reads=[pgtok], writes=["gsc"])
                s.op("act", lambda e: e.activation(out=gsc[0:T, 20:24], in_=gsc[0:T, 12:16], func=AF.Exp), reads=["gsc"], writes=["gsc"])
                yield
                s.op("dve", lambda e: e.tensor_tensor(out=gsc[0:T, 24:28], in0=gsc[0:T, 16:20], in1=gsc[0:T, 12:16], op=ALU.subtract), reads=["gsc"], writes=["gsc"])
                yield
                s.op("act", lambda e: e.activation(out=gsc[0:T, 24:28], in_=gsc[0:T, 24:28], func=AF.Exp), reads=["gsc"], writes=["gsc"])
                yield
                s.op("dve", lambda e: e.tensor_tensor(out=gsc[0:T, 20:24], in0=gsc[0:T, 20:24], in1=gsc[0:T, 0:4], op=ALU.mult), reads=["gsc"], writes=["gsc"])
                yield
                s.op("dve", lambda e: e.tensor_scalar(out=gsc[0:T, 28:32], in0=gsc[0:T, 0:4], scalar1=-1.0, scalar2=None, op0=ALU.mult), reads=["gsc"], writes=["gsc"])

            gp = prep_gen()
            ctx["gdn_prep"] = gp
            return

        def gdn_tile2(ctx, T, it, h, htok):
            sample = ctx["sample"]
            sfx = "s" if sample else "p"
            nlev = 2 if sample else 6
            for _ in ctx["gdn_prep"]:
                pass
            nchunk = 16 if sample else 2
            if sample:
                set0 = [k_ + "_0" for k_ in "Gb Eup Elo etmp EDrow uf wTf qdT qkT kd vnew GLb Nm0 Nm1 Mm0 Mm1 Xb0 Xb1".split()]
                s.op("pool", lambda e: e.memset(HBBIG[0][:, 0:16], 0.0), writes=["stage2"] + set0)
            banksA = PS["g"][0] + PS["acc"][0]
            banksB = PS["tp"][0] + PS["mm"][0]
            for i_ in range(NSET):
                HB[i_]["bankA"] = banksA[i_]
                HB[i_]["bankB"] = banksB[i_]

            def head_gen(hd, B):
                Gb, Eup, Elo, etmp, EDrow, uf, wTf, qdT, qkT, kd, vnew, GLb, Nm, Mm, Xb = (B[k_] for k_ in "Gb Eup Elo etmp EDrow uf wTf qdT qkT kd vnew GLb Nm Mm Xb".split())
                sx = "_%d" % B["i"]
                s.op("dve", lambda e, hd=hd: e.tensor_scalar(out=Gb[0:T, :], in0=K["onesf"][0:T, :], scalar1=gsc[0:T, 4 + hd:5 + hd], scalar2=None, op0=ALU.mult), reads=["gsc", "Konesf"], writes=["Gb" + sx])
                yield
                pg, pgtok = B["bankA"]
                s.op("pe", [lambda e, pg=pg: e.matmul(pg[:, 0:T], lhsT=Gb[0:T, :], rhs=K["tri_" + sfx][0:T, 0:T], start=True, stop=True),
                            lambda e, pg=pg: e.matmul(pg[:, 128:128 + nchunk], lhsT=Gb[0:T, :], rhs=K["chsel_" + sfx][0:T, :], start=True, stop=True),
                            lambda e, pg=pg, hd=hd: e.matmul(pg[0:T, 256:256 + T], lhsT=gqkb[:, 4 + hd, 0:T], rhs=gqkb[:, 4 + hd, 0:T], start=True, stop=True),
                            lambda e, pg=pg, hd=hd: e.matmul(pg[0:T, 384:384 + T], lhsT=gqkb[:, 4 + hd, 0:T], rhs=gqkb[:, hd, 0:T], start=True, stop=True)],
                     reads=["Gb" + sx, "Ktri_" + sfx, "Kchsel_" + sfx, "gqkb"], writes=[pgtok])
                yield
                s.op("act", lambda e, pg=pg: e.activation(out=GLb[:, 0:nchunk], in_=pg[:, 128:128 + nchunk], func=AF.Exp), reads=[pgtok], writes=["GLb" + sx])
                yield
                s.op("act", lambda e, pg=pg: e.activation(out=EDrow[:, 0:T], in_=pg[:, 0:T], func=AF.Exp), reads=[pgtok], writes=["EDrow" + sx])
                yield
                s.op("dve", lambda e, pg=pg, hd=hd: e.scalar_tensor_tensor(out=etmp[0:T, 0:T], in0=pg[0:T, 0:T], scalar=gsc[0:T, 12 + hd:13 + hd], in1=K["negU_" + sfx][0:T, 0:T], op0=ALU.subtract, op1=ALU.add),
                     reads=[pgtok, "gsc", "KnegU_" + sfx], writes=["etmp" + sx])
                yield
                s.op("act", lambda e: e.activation(out=Eup[0:T, 0:T], in_=etmp[0:T, 0:T], func=AF.Exp), reads=["etmp" + sx], writes=["Eup" + sx])
                yield
                s.op("dve", lambda e, pg=pg, hd=hd: e.scalar_tensor_tensor(out=etmp[0:T, 0:T], in0=pg[0:T, 0:T], scalar=-1.0, in1=K["negL_" + sfx][0:T, 0:T], op0=ALU.mult, op1=ALU.add),
                     reads=[pgtok, "KnegL_" + sfx, "Eup" + sx], writes=["etmp" + sx])
                yield
                s.op("act", lambda e, hd=hd: e.activation(out=Elo[0:T, 0:T], in_=etmp[0:T, 0:T], func=AF.Exp, bias=gsc[0:T, 12 + hd:13 + hd]), reads=["etmp" + sx, "gsc"], writes=["Elo" + sx])
                yield
                s.op("pool", lambda e: e.tensor_tensor(out=Elo[0:T, 0:T], in0=Elo[0:T, 0:T], in1=K["strictL_" + sfx][0:T, 0:T], op=ALU.mult), reads=["Elo" + sx, "KstrictL_" + sfx], writes=["Elo" + sx])
                yield
                s.op("dve", lambda e, pg=pg, hd=hd: e.scalar_tensor_tensor(out=Nm[0][0:T, 0:T], in0=pg[0:T, 256:256 + T], scalar=gsc[0:T, 28 + hd:29 + hd], in1=Elo[0:T, 0:T], op0=ALU.mult, op1=ALU.mult),
                     reads=[pgtok, "gsc", "Elo" + sx], writes=["Nm0" + sx])
                yield
                s.op("dve", lambda e, pg=pg: e.tensor_tensor(out=qkT[0:T, 0:T], in0=pg[0:T, 384:384 + T], in1=Eup[0:T, 0:T], op=ALU.mult), reads=[pgtok, "Eup" + sx], writes=["qkT" + sx])
                yield
                s.op("pool", lambda e, hd=hd: e.tensor_tensor(out=qdT[:, 0:T], in0=gqk[:, hd, 0:T], in1=EDrow[:, 0:T], op=ALU.mult), reads=["gqk", "EDrow" + sx], writes=["qdT" + sx])
                yield
                s.op("dve", lambda e, hd=hd: e.tensor_scalar(out=Xb[0][0:T, 0:128], in0=gtok[0:T, 4 + hd, :], scalar1=gsc[0:T, hd:hd + 1], scalar2=None, op0=ALU.mult), reads=["gtok", "gsc"], writes=["Xb0" + sx])
                yield
                s.op("dve", lambda e, hd=hd: e.tensor_scalar(out=Xb[0][0:T, 128:256], in0=gtok[0:T, hd, :], scalar1=gsc[0:T, 20 + hd:21 + hd], scalar2=None, op0=ALU.mult), reads=["gtok", "gsc"], writes=["Xb0" + sx])
                yield
                s.op("pool", lambda e, hd=hd: e.tensor_scalar(out=kd[0:T, :], in0=gtok[0:T, hd, :], scalar1=gsc[0:T, 24 + hd:25 + hd], scalar2=None, op0=ALU.mult), reads=["gtok", "gsc"], writes=["kd" + sx])
                yield
                tp, tptok = B["bankB"]
                tpb = tp[:].bitcast(BF16)
                s.op("pe", lambda e, tpb=tpb: e.transpose(out=tpb[0:T, 0:T], in_=Nm[0][0:T, 0:T], identity=identb[0:T, 0:T]), reads=["Nm0" + sx, "Kidentb"], writes=[tptok])
                yield
                copy_op("act", Mm[0][0:T, 0:T], tpb[0:T, 0:T], [tptok], ["Mm0" + sx])
                yield
                cur = 0
                for lev in range(nlev):
                    pg2, pg2tok = B["bankA"]
                    fl = [lambda e, pg2=pg2, cur=cur: e.matmul(pg2[0:T, 0:256], lhsT=Mm[cur][0:T, 0:T], rhs=Xb[cur][0:T, :], start=True, stop=True)]
                    lastlev = lev == nlev - 1
                    if not lastlev:
                        fl.append(lambda e, pg2=pg2, cur=cur: e.matmul(pg2[0:T, 256:256 + T], lhsT=Nm[cur][0:T, 0:T], rhs=Mm[cur][0:T, 0:T], start=True, stop=True))
                        if lev < nlev - 2:
                            fl.append(lambda e, pg2=pg2, cur=cur: e.matmul(pg2[0:T, 384:384 + T], lhsT=Mm[cur][0:T, 0:T], rhs=Nm[cur][0:T, 0:T], start=True, stop=True))
                    s.op("pe", fl, reads=["Mm%d" % cur + sx, "Nm%d" % cur + sx, "Xb%d" % cur + sx], writes=[pg2tok])
                    yield
                    nxt = cur ^ 1
                    if not lastlev:
                        s.op("dve", lambda e, pg2=pg2, cur=cur, nxt=nxt: e.tensor_tensor(out=Xb[nxt][0:T, :], in0=pg2[0:T, 0:256], in1=Xb[cur][0:T, :], op=ALU.add), reads=[pg2tok, "Xb%d" % cur + sx], writes=["Xb%d" % nxt + sx])
                        yield
                        copy_op("act", Mm[nxt][0:T, 0:T], pg2[0:T, 256:256 + T], [pg2tok], ["Mm%d" % nxt + sx])
                        yield
                        if lev < nlev - 2:
                            copy_op("pool" if False else "dve", Nm[nxt][0:T, 0:T], pg2[0:T, 384:384 + T], [pg2tok], ["Nm%d" % nxt + sx])
                            yield
                    else:
                        s.op("dve", lambda e, pg2=pg2, cur=cur: e.tensor_tensor(out=uf[0:T, :], in0=pg2[0:T, 0:128], in1=Xb[cur][0:T, 0:128], op=ALU.add), reads=[pg2tok, "Xb%d" % cur + sx], writes=["uf" + sx])
                        yield
                        s.op("dve", lambda e, pg2=pg2, cur=cur: e.tensor_tensor(out=etmp[0:T, :], in0=pg2[0:T, 128:256], in1=Xb[cur][0:T, 128:256], op=ALU.add), reads=[pg2tok, "Xb%d" % cur + sx, "Eup" + sx, "Elo" + sx], writes=["etmp" + sx])
                        yield
                    cur = nxt
                tp, tptok = B["bankB"]
                s.op("pe", lambda e, tp=tp: e.transpose(out=tp[:, 0:T], in_=etmp[0:T, :], identity=identf[0:T, 0:T]), reads=["etmp" + sx, "Kidentf"], writes=[tptok])
                yield
                copy_op("act", wTf[:, 0:T], tp[:, 0:T], [tptok], ["wTf" + sx])
                yield
                Sh, Stok = Sst[hd], "S%d" % hd
                if not sample:
                    if it == 0:
                        s.op("pool", lambda e, Sh=Sh: e.memset(Sh[:], 0.0), writes=[Stok])
                    for j in range(2):
                        R = slice(64 * j, 64 * j + 64)
                        pg3, pg3tok = B["bankA"]
                        s.op("pe", [lambda e, pg3=pg3, R=R, Sh=Sh: e.matmul(pg3[R, 0:128], lhsT=wTf[:, R], rhs=Sh[:, :], start=True, stop=True),
                                    lambda e, pg3=pg3, R=R, Sh=Sh: e.matmul(pg3[R, 128:256], lhsT=qdT[:, R], rhs=Sh[:, :], start=True, stop=True)],
                             reads=["wTf" + sx, "qdT" + sx, Stok], writes=[pg3tok])
                        yield
                        s.op("dve", lambda e, pg3=pg3, R=R: e.tensor_tensor(out=vnew[R, :], in0=uf[R, :], in1=pg3[R, 0:128], op=ALU.subtract), reads=[pg3tok, "uf" + sx], writes=["vnew" + sx])
                        yield
                        copy_op("act", ogdn[R, hd, :], pg3[R, 128:256], [pg3tok], ["ogdn%d" % hd])
                        yield
                        s.op("pe", [lambda e, pg3=pg3, R=R: e.matmul(pg3[R, 384:512], lhsT=qkT[R, R], rhs=vnew[R, :], start=True, stop=True),
                                    lambda e, pg3=pg3, R=R: e.matmul(pg3[:, 256:384], lhsT=kd[R, :], rhs=vnew[R, :], start=True, stop=True)],
                             reads=["qkT" + sx, "vnew" + sx, "kd" + sx], writes=[pg3tok])
                        yield
                        s.op("dve", lambda e, pg3=pg3, R=R, hd=hd: e.tensor_tensor(out=ogdn[R, hd, :], in0=ogdn[R, hd, :], in1=pg3[R, 384:512], op=ALU.add), reads=[pg3tok, "ogdn%d" % hd], writes=["ogdn%d" % hd])
                        yield
                        s.op("dve", lambda e, pg3=pg3, Sh=Sh, j=j: e.scalar_tensor_tensor(out=Sh[:, :], in0=Sh[:, :], scalar=GLb[:, j:j + 1], in1=pg3[:, 256:384], op0=ALU.mult, op1=ALU.add),
                             reads=[pg3tok, Stok, "GLb" + sx], writes=[Stok])
                        yield
                    if it == ctx["ntiles"] - 1:
                        s.dma("sp", lambda e, hd=hd, Sh=Sh: e.dma_start(out=D["gdn_p"][hd], in_=Sh[:, :]), reads=[Stok])
                else:
                    yield from gdn_sample_scan(T, hd, B)

            hb_free = list(range(NSET))
            for h0_ in range(0, 4, NSET):
                gens = [head_gen(hd, HB[hd - h0_]) for hd in range(h0_, min(4, h0_ + NSET))]
                STAG = int(os.environ.get("STAG", "0"))
                rnd = 0
                started = {id(g_): k_ * STAG for k_, g_ in enumerate(gens)}
                while gens:
                    for g_ in gens[:]:
                        if rnd < started[id(g_)]:
                            continue
                        try:
                            next(g_)
                        except StopIteration:
                            gens.remove(g_)
                    rnd += 1
            s.op("pool", lambda e: e.tensor_tensor(out=otmp[0:T, :, :], in0=ogdn[0:T, :, :], in1=ogdn[0:T, :, :], op=ALU.mult), reads=["ogdn0", "ogdn1", "ogdn2", "ogdn3"], writes=["otmp"])
            s.op("dve", lambda e: e.tensor_reduce(out=small[0:T, 8:12], in_=otmp[0:T, :, :], axis=AX.X, op=ALU.add), reads=["otmp"], writes=["small"])
            s.op("dve", lambda e: e.tensor_scalar(out=small[0:T, 8:12], in0=small[0:T, 8:12], scalar1=1.0 / 128, scalar2=1e-6, op0=ALU.mult, op1=ALU.add), reads=["small"], writes=["small"])
            s.op("act", lambda e: e.activation(out=small[0:T, 8:12], in_=small[0:T, 8:12], func=AF.Sqrt), reads=["small"], writes=["small"])
            s.op("dve", lambda e: e.reciprocal(out=small[0:T, 8:12], in_=small[0:T, 8:12]), reads=["small"], writes=["small"])
            s.op("dve", lambda e: e.tensor_tensor(out=otmp[0:T, :, :], in0=ogdn[0:T, :, :], in1=small[0:T, 8:12].unsqueeze(2).to_broadcast([T, 4, 128]), op=ALU.mult), reads=["ogdn0", "ogdn1", "ogdn2", "ogdn3", "small"], writes=["otmp"])
            s.op("pool", lambda e: e.tensor_tensor(out=otmp[0:T, :, :], in0=otmp[0:T, :, :], in1=wg_bc[0:T, :].unsqueeze(1).to_broadcast([T, 4, 128]), op=ALU.mult), reads=["otmp", "wg_bc"], writes=["otmp"])
            s.op("dve", lambda e: e.tensor_tensor(out=mixb[0:T, 512:1024], in0=otmp[0:T, :, :].rearrange("p a d -> p (a d)"), in1=zsil[0:T, 1, :], op=ALU.mult), reads=["otmp", "zsil"], writes=["mixb"])

        sample_env = {}

        def gdn_sample_scan(T, hd, B):
            uf, wTf, qdT, qkT, kd, vnew, GLb = (B[k_] for k_ in "uf wTf qdT qkT kd vnew GLb".split())
            sx = "_%d" % B["i"]
            mask = K["chk_s"]
            lm = [(B["Gb"], "Gb" + sx), (B["etmp"], "etmp" + sx)]
            stg = [(B["Elo"], "Elo" + sx), (B["EDrow"], "EDrow" + sx)]
            ohi, ohitok = B["Eup"], "Eup" + sx
            pgA, pgAtok = B["bankA"]
            for b in range(NB):
                St, Sttok = stg[b % 2]
                mk, mktok = lm[b % 2]
                s.dma("sp", lambda e, b=b, St=St: e.dma_start(out=St[:, :], in_=D["sgdn"][b, hd]), writes=[Sttok])
                s.op("pool", lambda e, mk=mk: e.memset(mk[:, :], 0.0), writes=[mktok])
                s.op("pool", lambda e, b=b, mk=mk: e.tensor_copy(out=mk[:, 4 * b:4 * b + 4], in_=wTf[:, 4 * b:4 * b + 4]), reads=["wTf" + sx], writes=[mktok])
                s.op("pool", lambda e, b=b, mk=mk: e.tensor_copy(out=mk[:, 64 + 4 * b:64 + 4 * b + 4], in_=qdT[:, 4 * b:4 * b + 4]), reads=["qdT" + sx], writes=[mktok])
                s.op("pe", lambda e, b=b, St=St, mk=mk: e.matmul(pgA[:, 0:128], lhsT=mk[:, :], rhs=St[:, :], start=(b == 0), stop=(b == NB - 1)), reads=[mktok, Sttok], writes=[pgAtok])
                yield
            s.op("dve", lambda e: e.tensor_tensor(out=vnew[0:64, :], in0=uf[0:64, :], in1=pgA[0:64, 0:128], op=ALU.subtract), reads=[pgAtok, "uf" + sx], writes=["vnew" + sx])
            copy_op("act", ohi[64:128, :], pgA[64:128, 0:128], [pgAtok], [ohitok])
            s.op("pe", lambda e: e.matmul(pgA[64:128, 128:256], lhsT=qkT[0:64, 0:64], rhs=vnew[0:64, :], start=True, stop=True), reads=["qkT" + sx, "vnew" + sx], writes=[pgAtok])
            s.op("dve", lambda e: e.tensor_tensor(out=ohi[64:128, :], in0=ohi[64:128, :], in1=pgA[64:128, 128:256], op=ALU.add), reads=[pgAtok, ohitok], writes=[ohitok])
            s.dma("sp", lambda e, hd=hd: e.dma_start(out=ogdn[0:64, hd, :], in_=ohi[64:128, :]), reads=[ohitok], writes=["ogdn%d" % hd])
            yield
            pgB, pgBtok = B["bankB"]
            for b in range(NB):
                St, Sttok = stg[b % 2]
                km, kmtok = lm[b % 2]
                s.dma("sp", lambda e, b=b, St=St: e.dma_start(out=St[:, :], in_=D["sgdn"][b, hd]), writes=[Sttok])
                s.op("pool", lambda e, b=b, km=km: e.tensor_scalar(out=km[0:64, :], in0=kd[0:64, :], scalar1=mask[0:64, 4 * b:4 * b + 1], scalar2=None, op0=ALU.mult), reads=["kd" + sx, "Kchk_s"], writes=[kmtok])
                s.op("pe", lambda e, km=km: e.matmul(pgB[:, 0:128], lhsT=km[0:64, :], rhs=vnew[0:64, :], start=True, stop=True), reads=[kmtok, "vnew" + sx], writes=[pgBtok])
                So, Sotok = sample_env["Sout"][b % 2], "Sout%d" % (b % 2)
                s.op("dve", lambda e, St=St, So=So, b=b: e.scalar_tensor_tensor(out=So[:, :], in0=St[:, :], scalar=GLb[:, b:b + 1], in1=pgB[:, 0:128], op0=ALU.mult, op1=ALU.add),
                     reads=[pgBtok, Sttok, "GLb" + sx], writes=[Sotok])
                s.dma("sp", lambda e, b=b, So=So, hd=hd: e.dma_start(out=D["gdn_s"][b, hd], in_=So[:, :]), reads=[Sotok])
                yield

        def nsa_sample(ctx, T):
            env = sample_env
            kTn, skTt, v1s, idx = env["kTn"], env["skTt"], env["v1s"], env["idx"]
            stages = [(env["stage"], "stage"), (HBBIG[0][:, 0:2048], "stage2")]
            s.op("pool", lambda e: e.memset(ET[:], 0.0), writes=["ET0", "ET1", "kT_all", "v1_all", "wstage0", "wstage1", "stage", "stage2", "kTn", "skTt", "wkTs", "v1s", "wv1s", "part"] + [k_ + "_0" for k_ in "Gb Eup Elo etmp EDrow uf wTf qdT qkT kd vnew GLb Nm0 Nm1 Mm0 Mm1 Xb0 Xb1".split()])
            s.op("pool", lambda e: e.memset(v1_all[:], 1.0), reads=["ET0"], writes=["v1s", "wv1s", "v1_all"])
            for b in range(NB):
                cnames = ("cck", "ccv", "csk", "csv")

                def issue_gather(bb, ci):
                    stg_, stok_ = stages[ci % 2]
                    s.dma("pool", lambda e: e.indirect_dma_start(out=stg_[:, :], out_offset=None, in_=D[cnames[ci]], in_offset=bass.IndirectOffsetOnAxis(ap=idx[:, bb:bb + 1], axis=0)),
                          reads=["idx"], writes=[stok_])
                if b == 0:
                    issue_gather(0, 0)
                    issue_gather(0, 1)
                for ci in range(4):
                    stg_, stok_ = stages[ci % 2]
                    if ci < 3:
                        for q4 in range(4):
                            tp, tptok = ps("tp")
                            s.op("pe", [lambda e, r=r: e.transpose(out=tp[:, 128 * (r - 4 * q4):128 * (r - 4 * q4 + 1)], in_=stg_[:, 128 * r:128 * (r + 1)], identity=identf[:, :]) for r in range(4 * q4, 4 * q4 + 4)],
                                 reads=[stok_, "Kidentf"], writes=[tptok])
                            src = tp[:].rearrange("p (r m) -> p r m", m=128)
                            if ci < 2:
                                dst = kTn[:, ci, :].rearrange("p (m r) -> p r m", r=16)[:, 4 * q4:4 * q4 + 4, :]
                                copy_op(evac_eng(), dst, src, [tptok], ["kTn"])
                            else:
                                copy_op(evac_eng(), skTt[:, 4 * q4:4 * q4 + 4, :], src, [tptok], ["skTt"])
                    else:
                        s.op("pool", lambda e: e.tensor_copy(out=v1s[:, :, :, 0:64], in_=stg_[:, :].rearrange("p (r h d) -> p r h d", r=16, h=2)), reads=[stok_], writes=["v1s"])
                    if ci < 2:
                        issue_gather(b, ci + 2)
                    elif b + 1 < NB:
                        issue_gather(b + 1, ci - 2)
                wst, wkTs, wv1s = env["ctmp"], env["wkTs"], env["wv1s"]
                s.dma("sp", [lambda e, b=b: e.dma_start(out=wst[:, 0, :, :], in_=D["cwk"][b].rearrange("(a p) f -> p a f", p=128)),
                             lambda e, b=b: e.dma_start(out=wst[:, 1, :, :], in_=D["cwv"][b].rearrange("(a p) f -> p a f", p=128))], writes=["ctmp"])
                s.dma("sp", [lambda e, b=b: e.dma_start(out=D["wk_s"][b, 0:508, :], in_=D["cwk"][b, 4:512, :]),
                             lambda e, b=b: e.dma_start(out=D["wv_s"][b, 0:508, :], in_=D["cwv"][b, 4:512, :])])
                tp, tptok = ps("tp")
                s.op("pe", [lambda e, a=a, tp=tp: e.transpose(out=tp[:, 128 * a:128 * (a + 1)], in_=wst[:, 0, a, :], identity=identf[:, :]) for a in range(4)], reads=["ctmp", "Kidentf"], writes=[tptok])
                copy_op(evac_eng(), wkTs[:, :, :], tp[:].rearrange("p (a m) -> p a m", m=128), [tptok], ["wkTs"])
                s.op("pool", lambda e: e.tensor_copy(out=wv1s[:, :, :, 0:64], in_=wst[:, 1, :, :].rearrange("p a (h d) -> p a h d", h=2)), reads=["ctmp"], writes=["wv1s"])
                compress(kTn[:, 0, :], kTn[:, 1, :], "kTn", 0, 127)
                cols = slice(4 * b, 4 * b + 4)
                for hh in range(2):
                    def smp_tile(qw, kT_ap, nk, v1_ap, masks, rd, rdv, first, last):
                        return dict(kT=kT_ap, nk=nk, q=qT[64 * hh:64 * hh + 64, qw, :, cols], masks=masks, v1=v1_ap, rd=["qT"] + rd, rdv=rdv, ncol=16,
                                    et_out=lambda eb, nk=nk: ET[0:nk, eb, :, cols], sc_in=lambda pm, nk=nk: pm[0:nk, 0:16].rearrange("p (g t) -> p g t", g=4),
                                    first=first, last=last)
                    for bi, br in ((0, "cmp"), (2, "win"), (1, "slc")):
                        if br == "cmp":
                            tiles = [smp_tile(0, cckT[64 * hh:64 * hh + 64, 0:127], 127, ccv1[0:127, hh, :], [], ["cckT"], ["ccv1"], True, True)]
                            W = 98
                        elif br == "slc":
                            tiles = []
                            selr = selT[:, hh, cols].unsqueeze(1).to_broadcast([128, 4, 4])
                            for r in range(16):
                                tiles.append(smp_tile(1, skTt[64 * hh:64 * hh + 64, r, :], 128, v1s[:, r, hh, :], [(K["expand_s"][:, :], selr)], ["skTt", "selT", "Kexpand_s"], ["v1s"], True, True))
                            seln = selT[:, hh, cols].unsqueeze(1).to_broadcast([128, 4, 4])
                            tiles.append(smp_tile(1, env["kT_new"][64 * hh:64 * hh + 64, 2, :], 64, env["v1_new"][:, 0, hh, :],
                                                  [(K["expand_n"][:, :], seln), (identb[:, 0:64], K["newmask_s"][:, b, :, :])], ["kT_new", "selT", "Kexpand_n", "Knewmask_s", "Kidentb"], ["v1_new"], True, True))
                            W = 65
                        else:
                            tiles = []
                            for a in range(4):
                                masks = [(identb[:, :], K["winmask_s"][:, :, :])] if a == 0 else []
                                tiles.append(smp_tile(1, wkTs[64 * hh:64 * hh + 64, a, :], 128, wv1s[:, a, hh, :], masks, ["wkTs", "Kidentb", "Kwinmask_s"], ["wv1s"], True, True))
                            tiles.append(smp_tile(1, env["kT_new"][64 * hh:64 * hh + 64, 3, :], 64, env["v1_new"][:, 1, hh, :],
                                                  [(identb[:, 0:64], K["newmask_s"][:, b, :, :])], ["kT_new", "Knewmask_s", "Kidentb"], ["v1_new"], True, True))
                            W = 65
                        acc, accv, acctok = nsa_core(T, hh, 0, tiles, W, [], br, True, 0)
                        pv = env["partv"](bi, hh)
                        if b == 0:
                            s.op("dve", lambda e, pv=pv, accv=accv: e.tensor_copy(out=pv, in_=accv), reads=[acctok], writes=["part"])
                        else:
                            s.op("dve", lambda e, pv=pv, accv=accv: e.tensor_tensor(out=pv, in0=pv, in1=accv, op=ALU.add), reads=[acctok, "part"], writes=["part"])
                        s.op("pool", lambda e, cols=cols: e.memset(ET[:, :, :, cols], 0.0), reads=[], writes=["ET0", "ET1"])
                        if br == "cmp":
                            if True:
                                pacc = env["partv"](0, hh)
                                s.op("dve", lambda e, pacc=pacc: e.tensor_scalar(out=rz[0:T, 0:4], in0=pacc[:, :, 64], scalar1=1e-30, scalar2=None, op0=ALU.max), reads=["part"], writes=["rz"])
                                s.op("dve", lambda e: e.reciprocal(out=rz[0:T, 0:4], in_=rz[0:T, 0:4]), reads=["rz"], writes=["rz"])
                                iv = imp[0:T, hh, 0:33]
                                s.op("dve", lambda e, pacc=pacc, iv=iv: e.tensor_scalar(out=iv, in0=pacc[:, 0, 65:98], scalar1=rz[0:T, 0:1], scalar2=None, op0=ALU.mult), reads=["part", "rz"], writes=["imp"])
                                for g in range(1, 4):
                                    s.op("dve", lambda e, g=g, pacc=pacc, iv=iv: e.scalar_tensor_tensor(out=iv, in0=pacc[:, g, 65:98], scalar=rz[0:T, g:g + 1], in1=iv, op0=ALU.mult, op1=ALU.add), reads=["part", "rz", "imp"], writes=["imp"])
                                select_blocks(T, hh, 0, "s", 33)
            for hh in range(2):
                for bi in range(3):
                    W = 98 if bi == 0 else 65
                    pacc = env["partv"](bi, hh)
                    branch_epilogue(T, hh, pacc, "part", W, 8 * bi, bi == 0)

        if do_prompt and stage >= 1:
            for it in range(nt_prompt):
                ctx = dict(T=128, it=it, slot=it % 2, sample=False, row0=128 * it, ntiles=nt_prompt,
                           x_src=D["xp"][128 * it:128 * (it + 1), :], y_dst=D["y_p"][128 * it:128 * (it + 1), :],
                           kT_dst=kT_all[:, :, 128 * it:128 * (it + 1)], kT_tok="kT_all",
                           v1_dst=v1_all[:, :, it, :, :], v1_tok="v1_all")
                run_tile(ctx)

        if do_sample:
            env = sample_env
            wsf0 = wstage[0][:].rearrange("p k n -> p (k n)").bitcast(F32)
            wsf1 = wstage[1][:].rearrange("p k n -> p (k n)").bitcast(F32)
            env["stage"] = wsf1
            env["kTn"] = kT_all[:, 0:2, :]
            env["skTt"] = kT_all[:, 2, :].rearrange("p (r m) -> p r m", m=128)
            env["wkTs"] = kT_all[:, 3, 0:512].rearrange("p (a m) -> p a m", m=128)
            env["v1s"] = v1_all[:, 0, :, :, :]
            env["wv1s"] = v1_all[:, 1, 0:4, :, :]
            env["ctmp"] = ctmp[:, 0:8, :].rearrange("p (w a) f -> p w a f", w=2)
            pc_ = wsf0[0:64, 0:784].rearrange("p (h g w) -> p h g w", h=2, g=4)
            ps_ = wsf0[0:64, 784:1824].rearrange("p (b h g w) -> p b h g w", b=2, h=2, g=4)
            env["partv"] = lambda bi, hh: (pc_[:, hh, :, :] if bi == 0 else ps_[:, bi - 1, hh, :, :])
            env["kT_new"] = sb("kT_new", [128, 4, 64], BF16)
            env["v1_new"] = sb("v1_new", [64, 2, 2, 65], BF16)
            env["idx"] = sb("idx", [128, 16], I32)
            env["Sstage"] = [sb("Sstage0", [128, 128]), sb("Sstage1", [128, 128])]
            env["Sout"] = [sb("Sout0", [128, 128]), sb("Sout1", [128, 128])]
            env["lmask"] = sb("lmask", [128, 128])
            env["kmask"] = sb("kmask", [64, 128])
            env["ohi"] = sb("ohi", [128, 128])
            idxf = sb("idxf", [128, 16])
            ptl_sb = sb("ptl_sb", [128, 16], I32)
            s.dma("sp", lambda e: e.dma_start(out=ptl_sb[:], in_=D["ptl"]), writes=["ptl_sb"])
            s.op("dve", lambda e: e.tensor_copy(out=idxf[:], in_=ptl_sb[:]), reads=["ptl_sb"], writes=["idxf"])
            s.op("dve", lambda e: e.tensor_scalar(out=idxf[:], in0=idxf[:], scalar1=8.0, scalar2=K["rgcol"][:, 0:1], op0=ALU.mult, op1=ALU.add), reads=["idxf", "Krgcol"], writes=["idxf"])
            s.op("dve", lambda e: e.tensor_copy(out=env["idx"][:], in_=idxf[:]), reads=["idxf"], writes=["idx"])
            s.op("pool", lambda e: e.memset(env["v1_new"][:], 1.0), writes=["v1_new"])
            ctx = dict(T=TS, it=0, slot=0, sample=True, row0=0, ntiles=1, ws_extra=["part", "stage"],
                       x_src=D["xs"][:, :], y_dst=D["y_s"][:, :],
                       kT_dst=env["kT_new"][:, :, :], kT_tok="kT_new",
                       v1_dst=env["v1_new"][:, :, :, :], v1_tok="v1_new")
            run_tile(ctx)

        s.finish()
        s.emit()
        if os.environ.get('DBG_SBUF'):
            print('SBUF remaining', nc.sbuf_bytes_remaining, 'ops', s.nops, {e: s.cnt[e] for e in s.ENGS})
    return nc, s


_CACHE = {}
CORES = list(range(NCORES))
BUILD_KW = {}
TRACE = False
LAST = {}


def kernel(x_prompt, x_sample, cache_cmp_k, cache_cmp_v, cache_slc_k, cache_slc_v, cache_win_k, cache_win_v,
           state_conv, state_gdn, page_table, w_norm, w_in, pe_cmp_k, w_cmp_k1, w_cmp_k2, pe_cmp_v, w_cmp_v1,
           w_cmp_v2, w_conv, a_log, dt_bias, w_gdn_norm, w_out, w_final_norm):
    f = lambda a: np.ascontiguousarray(np.asarray(a, dtype=np.float32))
    consts = make_consts()
    if "nc" not in _CACHE:
        _CACHE["nc"] = build_program(consts, **BUILD_KW)[0]
    nc = _CACHE["nc"]
    shared = {
        "cck": f(cache_cmp_k).reshape(2560 * 8, 2048), "ccv": f(cache_cmp_v).reshape(2560 * 8, 2048),
        "csk": f(cache_slc_k).reshape(2560 * 8, 2048), "csv": f(cache_slc_v).reshape(2560 * 8, 2048),
        "w_norm": f(w_norm).reshape(1, 1024), "w_in": f(w_in).reshape(1024, NIN),
        "pe_k": f(pe_cmp_k).reshape(32, 64), "w1k": f(w_cmp_k1).reshape(2048, 128), "w2k": f(w_cmp_k2).reshape(128, 64),
        "pe_v": f(pe_cmp_v).reshape(32, 64), "w1v": f(w_cmp_v1).reshape(2048, 128), "w2v": f(w_cmp_v2).reshape(128, 64),
        "w_conv": f(w_conv).reshape(4, 1536), "a_log": f(a_log).reshape(1, 4), "dt_bias": f(dt_bias).reshape(1, 4),
        "wg": f(w_gdn_norm).reshape(1, 128), "w_out": f(w_out).reshape(1024, 1024), "w_fn": f(w_final_norm).reshape(1, 1024),
    }
    for k, v in consts.items():
        shared["k_" + k] = v
    xp = f(x_prompt); xs = f(x_sample)
    cwk = f(cache_win_k).reshape(128, 512, 128); cwv = f(cache_win_v).reshape(128, 512, 128)
    sconv = f(state_conv).reshape(128, 3, 1536); sgdn = f(state_gdn).reshape(128, 4, 128, 128)
    pt = np.asarray(page_table, dtype=np.int32)
    in_maps = []
    for c in CORES:
        bs = slice(16 * c, 16 * (c + 1))
        m = dict(shared)
        m["xp"] = xp[c]
        m["xs"] = xs[bs].reshape(64, 1024)
        m["cwk"] = cwk[bs]; m["cwv"] = cwv[bs]
        m["sconv"] = sconv[bs].reshape(48, 1536)
        m["sgdn"] = sgdn[bs]
        m["ptl"] = np.ascontiguousarray(np.repeat(pt[bs].T, 8, axis=0)).astype(np.int32)
        in_maps.append(m)
    res = run_bass_kernel_spmd(nc, in_maps, core_ids=list(range(len(CORES))), **({'trace': True} if TRACE else {}))
    LAST['res'] = res
    R = res.results
    cat = lambda k: np.concatenate([r[k][None] for r in R], 0)
    y_p = cat("y_p")
    y_s = cat("y_s").reshape(-1, 4, 1024)
    outs = [y_p, y_s]
    for nm in ("ck", "cv", "sk", "sv"):
        outs.append(cat(nm + "_p").reshape(1, -1, 2048, 2, 64))
        outs.append(cat(nm + "_s").reshape(1, -1, 4, 2, 64))
    for nm in ("wk", "wv"):
        outs.append(cat(nm + "_p").reshape(1, -1, 512, 2, 64))
        outs.append(cat(nm + "_s").reshape(1, -1, 512, 2, 64))
    outs.append(cat("conv_p").reshape(1, -1, 3, 1536))
    outs.append(cat("conv_s").reshape(1, -1, 3, 1536))
    outs.append(cat("gdn_p").reshape(1, -1, 4, 128, 128))
    outs.append(cat("gdn_s").reshape(1, -1, 4, 128, 128))
    return tuple(np.ascontiguousarray(o, dtype=np.float32) for o in outs)
```
